# Optimizing a Trainium2 kernel written in Bass

```python
import jax, jax.numpy as jnp
from jax import lax
import numpy as np

D_MODEL = 2048
BATCH = 2
SEQ = 4096
DEPTH = 1

RET_HEADS = 8
RET_QK_DIM = 128
RET_V_DIM = 256
RET_CHUNK = 128
RET_ROPE_BASE = 10000.0
SWA_Q_HEADS = 32
SWA_KV_HEADS = 8
SWA_HEAD_DIM = 64
SWA_WINDOW = 128
SWA_BLOCK = 128
D_FF = 5632
NORM_EPS = 1e-6
NEG_INF = -1e30

RET_QK_W = RET_HEADS * RET_QK_DIM
RET_V_W = RET_HEADS * RET_V_DIM
SWA_Q_W = SWA_Q_HEADS * SWA_HEAD_DIM
SWA_KV_W = SWA_KV_HEADS * SWA_HEAD_DIM
IN_SIZES = [RET_QK_W, RET_QK_W, RET_V_W, RET_V_W, SWA_Q_W, SWA_KV_W, SWA_KV_W, D_MODEL, D_MODEL]
IN_WIDTH = sum(IN_SIZES)
IN_SPLITS = [int(v) for v in np.cumsum(IN_SIZES)[:-1]]
N_MOD = 9

kernel_name = 'hybrid_retention_swa_macaron_adaln'


def rms_norm(x, gain):
    xf = x.astype(jnp.float32)
    y = xf * lax.rsqrt(jnp.mean(xf * xf, axis=-1, keepdims=True) + NORM_EPS)
    return (y * gain.astype(jnp.float32)).astype(x.dtype)


def modulate(h, shift, scale):
    return h * (1.0 + scale[:, None, :]) + shift[:, None, :]


def swiglu(h, w_gu, w_down):
    gate, up = jnp.split(h @ w_gu, 2, axis=-1)
    return (jax.nn.silu(gate) * up) @ w_down


def rotary(t, pos):
    half = t.shape[-1] // 2
    inv_freq = 1.0 / (RET_ROPE_BASE ** (jnp.arange(half, dtype=jnp.float32) / half))
    ang = pos[:, None] * inv_freq[None, :]
    cos = jnp.cos(ang)[None, :, None, :]
    sin = jnp.sin(ang)[None, :, None, :]
    t1, t2 = t[..., :half], t[..., half:]
    return jnp.concatenate([t1 * cos - t2 * sin, t2 * cos + t1 * sin], axis=-1)


def retention_chunkwise(q, k, v):
    b, s, h, dk = q.shape
    dv = v.shape[-1]
    C = RET_CHUNK
    n = s // C
    log_gamma = jnp.log(1.0 - 2.0 ** (-5.0 - jnp.arange(h, dtype=jnp.float32)))
    idx = jnp.arange(C, dtype=jnp.float32)
    diff = idx[:, None] - idx[None, :]
    intra_decay = jnp.where(diff[None] >= 0,
                            jnp.exp(log_gamma[:, None, None] * jnp.maximum(diff, 0.0)[None]), 0.0)
    k = k * (dk ** -0.5)
    qc = q.reshape(b, n, C, h, dk)
    kc = k.reshape(b, n, C, h, dk)
    vc = v.reshape(b, n, C, h, dv)
    scores = jnp.einsum('bnqhd,bnkhd->bnhqk', qc, kc) * intra_decay[None, None]
    intra = jnp.einsum('bnhqk,bnkhe->bnqhe', scores, vc)
    k_decay = jnp.exp(log_gamma[None, :] * (C - 1.0 - idx[:, None]))
    chunk_kv = jnp.einsum('bnkhd,kh,bnkhe->nbhde', kc, k_decay, vc)
    chunk_decay = jnp.exp(log_gamma * C)[None, :, None, None]

    def step(state, kv):
        return chunk_decay * state + kv, state

    _, prev = lax.scan(step, jnp.zeros((b, h, dk, dv), jnp.float32), chunk_kv)
    q_decay = jnp.exp(log_gamma[None, :] * (idx[:, None] + 1.0))
    cross = jnp.einsum('bnqhd,nbhde->bnqhe', qc, prev) * q_decay[None, None, :, :, None]
    return (intra + cross).reshape(b, s, h, dv)


def swa_with_sinks(q, k, v, sinks):
    b, s, hq, d = q.shape
    hkv = k.shape[2]
    g = hq // hkv
    L = SWA_BLOCK
    n = s // L
    qb = q.reshape(b, n, L, hkv, g, d)

    def band(t):
        tp = jnp.pad(t, ((0, 0), (L, 0), (0, 0), (0, 0)))
        prev = tp[:, :s].reshape(b, n, L, hkv, d)
        cur = t.reshape(b, n, L, hkv, d)
        return jnp.concatenate([prev, cur], axis=2)

    kb, vb = band(k), band(v)
    logits = jnp.einsum('bnqhgd,bnkhd->bnhgqk', qb, kb).astype(jnp.float32) * (d ** -0.5)
    iq = jnp.arange(L)[:, None]
    ik = jnp.arange(2 * L)[None, :]
    rel = iq + L - ik
    blk = jnp.arange(n)[:, None, None]
    valid = (rel >= 0)[None] & (rel < SWA_WINDOW)[None] & (blk * L - L + ik[None] >= 0)
    logits = jnp.where(valid[None, :, None, None], logits, NEG_INF)
    sink = jnp.broadcast_to(sinks.reshape(hkv, g).astype(jnp.float32)[None, None, :, :, None, None],
                            logits.shape[:-1] + (1,))
    probs = jax.nn.softmax(jnp.concatenate([logits, sink], axis=-1), axis=-1)[..., :-1]
    out = jnp.einsum('bnhgqk,bnkhd->bnqhgd', probs.astype(v.dtype), vb)
    return out.reshape(b, s, hq * d)


def token_mixing(h, w_in, w_ret_branch, w_swa_branch, w_out, sinks):
    b, s, _ = h.shape
    rq, rk, rv, rg, sq, sk, sv, gl_ret, gl_swa = jnp.split(h @ w_in, IN_SPLITS, axis=-1)
    pos = jnp.arange(s, dtype=jnp.float32)
    rq = rotary(rq.reshape(b, s, RET_HEADS, RET_QK_DIM).astype(jnp.float32), pos)
    rk = rotary(rk.reshape(b, s, RET_HEADS, RET_QK_DIM).astype(jnp.float32), pos)
    rv = rv.reshape(b, s, RET_HEADS, RET_V_DIM).astype(jnp.float32)
    ro = retention_chunkwise(rq, rk, rv)
    ro = ro * lax.rsqrt(jnp.mean(ro * ro, axis=-1, keepdims=True) + NORM_EPS)
    ro = (jax.nn.silu(rg.astype(jnp.float32)) * ro.reshape(b, s, RET_V_W)).astype(h.dtype)
    y_ret = ro @ w_ret_branch
    so = swa_with_sinks(sq.reshape(b, s, SWA_Q_HEADS, SWA_HEAD_DIM),
                        sk.reshape(b, s, SWA_KV_HEADS, SWA_HEAD_DIM),
                        sv.reshape(b, s, SWA_KV_HEADS, SWA_HEAD_DIM), sinks)
    y_swa = so @ w_swa_branch
    merged = jax.nn.sigmoid(gl_ret) * y_ret + jax.nn.sigmoid(gl_swa) * y_swa
    return merged @ w_out


def setup_inputs(seed: int = 0) -> dict:
    key = jax.random.key(seed)
    ks = jax.random.split(key, 18)
    f32 = jnp.float32
    D = D_MODEL

    def nrm(k, shape, scale):
        return jax.random.normal(k, shape, f32) * scale

    return {
        'x': nrm(ks[0], (BATCH, SEQ, D), 1.0),
        'c': nrm(ks[1], (BATCH, D), 1.0),
        'w_ada': nrm(ks[2], (DEPTH, D, N_MOD * D), 0.5 * D ** -0.5),
        'b_ada': nrm(ks[3], (DEPTH, N_MOD * D), 0.01),
        'norm_ffn1': 1.0 + nrm(ks[4], (DEPTH, D), 0.02),
        'w_ffn1_gu': nrm(ks[5], (DEPTH, D, 2 * D_FF), D ** -0.5),
        'w_ffn1_down': nrm(ks[6], (DEPTH, D_FF, D), D_FF ** -0.5),
        'norm_mix': 1.0 + nrm(ks[7], (DEPTH, D), 0.02),
        'w_in': nrm(ks[8], (DEPTH, D, IN_WIDTH), D ** -0.5),
        'w_ret_branch': nrm(ks[9], (DEPTH, RET_V_W, D), RET_V_W ** -0.5),
        'w_swa_branch': nrm(ks[10], (DEPTH, SWA_Q_W, D), SWA_Q_W ** -0.5),
        'w_out': nrm(ks[11], (DEPTH, D, D), D ** -0.5),
        'sinks': nrm(ks[12], (DEPTH, SWA_Q_HEADS), 1.0),
        'norm_ffn2': 1.0 + nrm(ks[13], (DEPTH, D), 0.02),
        'w_ffn2_gu': nrm(ks[14], (DEPTH, D, 2 * D_FF), D ** -0.5),
        'w_ffn2_down': nrm(ks[15], (DEPTH, D_FF, D), D_FF ** -0.5),
        'norm_final': 1.0 + nrm(ks[16], (D,), 0.02),
    }


def reference(x, c, w_ada, b_ada, norm_ffn1, w_ffn1_gu, w_ffn1_down, norm_mix, w_in,
              w_ret_branch, w_swa_branch, w_out, sinks, norm_ffn2, w_ffn2_gu, w_ffn2_down,
              norm_final):
    b = x.shape[0]
    c_act = jax.nn.silu(c)
    for l in range(DEPTH):
        mod = (c_act @ w_ada[l] + b_ada[l]).reshape(b, N_MOD, D_MODEL)
        h = modulate(rms_norm(x, norm_ffn1[l]), mod[:, 0], mod[:, 1])
        x = x + 0.5 * mod[:, 2][:, None, :] * swiglu(h, w_ffn1_gu[l], w_ffn1_down[l])
        h = modulate(rms_norm(x, norm_mix[l]), mod[:, 3], mod[:, 4])
        x = x + mod[:, 5][:, None, :] * token_mixing(h, w_in[l], w_ret_branch[l], w_swa_branch[l],
                                                      w_out[l], sinks[l])
        h = modulate(rms_norm(x, norm_ffn2[l]), mod[:, 6], mod[:, 7])
        x = x + 0.5 * mod[:, 8][:, None, :] * swiglu(h, w_ffn2_gu[l], w_ffn2_down[l])
    return rms_norm(x, norm_final)
```

```python
import numpy as np
import concourse.bass as bass
import concourse.mybir as mybir
from concourse.bass_utils import run_bass_kernel_spmd

F32 = mybir.dt.float32
BF16 = mybir.dt.bfloat16
AF = mybir.ActivationFunctionType
ALU = mybir.AluOpType

NCORES = 8
P = 128
D = 2048
KC = 16
T = 1024
TH = 512
DFF = 5632
NJ = 44
CHUNKS = [6, 6, 6, 4, 6, 6, 6, 4]
NCH = len(CHUNKS)
MAXNJ = max(CHUNKS)
EPS = 1e-6
WCOLS = 8192
NSLOT = 2
PF = 1


class Eng:
    def __init__(self, name, h, sem):
        self.name = name
        self.h = h
        self.sem = sem
        self.cnt = 0
        self.seen = {}
        self.pending = False


class DSem:
    def __init__(self, sem, name):
        self.sem = sem
        self.cnt = 0
        self.name = name


class Sched:
    def __init__(self, nc, stack):
        self.nc = nc
        self.stack = stack
        self.E = {}
        for name, h in (("pe", nc.tensor), ("act", nc.scalar), ("dve", nc.vector),
                        ("pool", nc.gpsimd), ("sp", nc.sync)):
            sem = stack.enter_context(nc.semaphore("sem_" + name))
            self.E[name] = Eng(name, h, sem)
        self.res_w = {}
        self.res_r = {}
        self.dsems = []

    def new_dsem(self, name):
        d = DSem(self.stack.enter_context(self.nc.semaphore("dsem_" + name)), name)
        self.dsems.append(d)
        return d

    def _wait(self, E, tok):
        S, v = tok
        if S is E and E.name == "pe":
            return
        if isinstance(S, Eng) and S.name == "pe" and S.pending and v > S.cnt:
            raise RuntimeError("wait on unsignaled PE op")
        if E.seen.get(S, 0) < v:
            E.h.wait_ge(S.sem, v)
            E.seen[S] = v

    def _deps(self, E, r, w):
        toks = []
        for k in r:
            t = self.res_w.get(k)
            if t is not None:
                toks.append(t)
        for k in w:
            t = self.res_w.get(k)
            if t is not None:
                toks.append(t)
            for S, v in self.res_r.get(k, {}).items():
                toks.append((S, v))
        for t in toks:
            self._wait(E, t)

    def _record(self, tok, r, w):
        S, v = tok
        for k in r:
            d = self.res_r.setdefault(k, {})
            if d.get(S, 0) < v:
                d[S] = v
        for k in w:
            self.res_w[k] = tok
            self.res_r[k] = {}

    def op(self, en, fn, r=(), w=(), sig=True):
        E = self.E[en]
        self._deps(E, r, w)
        ins = fn(E.h)
        if sig:
            E.cnt += 1
            ins.then_inc(E.sem, 1)
            E.pending = False
            tok = (E, E.cnt)
        else:
            E.pending = True
            tok = (E, E.cnt + 1)
        self._record(tok, r, w)
        return ins

    def dma(self, en, out, in_, r=(), w=(), dsem=None):
        E = self.E[en]
        self._deps(E, r, w)
        self._wait(E, (dsem, dsem.cnt))
        ins = E.h.dma_start(out=out, in_=in_)
        dsem.cnt += 16
        ins.then_inc(dsem.sem, 16)
        self._record((dsem, dsem.cnt), r, w)
        return ins

    def barrier(self):
        assert not self.E["pe"].pending
        for E in self.E.values():
            for F in self.E.values():
                if F is not E and F.cnt > 0:
                    self._wait(E, (F, F.cnt))
            for d in self.dsems:
                if d.cnt > 0:
                    self._wait(E, (d, d.cnt))

    def final_wait(self, en, keys):
        E = self.E[en]
        self._deps(E, keys, ())


class WStream:
    def __init__(self, sc, slots, nbase, seq):
        self.sc = sc
        self.slots = slots
        self.nbase = nbase
        self.seq = seq
        self.issued = 0
        self.pf = 1
        self.dsems = [sc.new_dsem("w%d" % i) for i in range(len(slots))]
        self.slot_tile = [-1] * len(slots)
        self.slot_of = {}

    def _try_issue(self, k, cur):
        ap, allow_extra = self.seq[k]
        cand = list(range(len(self.slots) if allow_extra else self.nbase))
        free = [c for c in cand if self.slot_tile[c] < cur]
        if not free:
            return False
        s = min(free, key=lambda c: self.slot_tile[c])
        ncols = ap.shape[-1]
        self.sc.dma("pool", self.slots[s][:, 0:ncols], ap, w=[("wslot", s)], dsem=self.dsems[s])
        self.slot_tile[s] = k
        self.slot_of[k] = s
        return True

    def get(self, i, pf=None):
        pf = self.pf if pf is None else pf
        while self.issued < min(i + pf + 1, len(self.seq)):
            if not self._try_issue(self.issued, i):
                assert self.issued > i, "no slot for the tile being consumed"
                break
            self.issued += 1
        s = self.slot_of[i]
        return self.slots[s], ("wslot", s)


class Rot:
    def __init__(self, items):
        self.items = list(items)
        self.i = 0

    def next(self):
        v = self.items[self.i % len(self.items)]
        self.i += 1
        return v


def build_program(dbg_names=(), stages=("ffn1", "mixer", "ffn2")):
    from contextlib import ExitStack
    nc = bass.Bass("TRN2", target_bir_lowering=False)
    stack = ExitStack()
    sc = Sched(nc, stack)

    def din(name, shape, dt=F32):
        return nc.dram_tensor(name, list(shape), dt, kind="ExternalInput").ap()

    xT_d = din("xT", [P, KC * T])
    cT_d = din("cT", [P, KC])
    badaT_d = din("badaT", [P, 144])
    gains_d = din("gains", [P, 64])
    rope_d = din("rope", [P, 2 * T])
    qdec_d = din("qdec", [P, 8 * 128])
    retmask_d = din("retmask", [P, 8 * 128])
    kdec_d = din("kdec", [P, 8 + 64])
    swamask_d = din("swamask", [P, 256])
    sel_d = din("sel", [P, 16])
    coef_d = din("coef", [P, 32])
    sinks_d = din("sinksl", [P, 16])
    ident_d = din("ident", [P, P])
    wada_d = din("wada", [36, P, WCOLS])
    wgu_d = [din("wgu1", [22, P, WCOLS]), din("wgu2", [22, P, WCOLS])]
    wd_d = [din("wd1", [NCH * 4, P, MAXNJ * 512]), din("wd2", [NCH * 4, P, MAXNJ * 512])]
    NWIN = 22
    win_d = din("win", [NWIN, P, WCOLS])
    wgb_d = din("wgb", [16, P, WCOLS])
    wbr_d = din("wbr", [4, P, WCOLS])
    yT_d = nc.dram_tensor("yT", [P, KC * T], F32, kind="ExternalOutput").ap()
    xspill = nc.dram_tensor("xspill", [P, KC * T], F32).ap()
    bounce = [nc.dram_tensor("bounce%d" % i, [4 * P, 1536], F32) for i in range(2)]
    gath = [nc.dram_tensor("gath%d" % i, [4 * P, 1536], F32) for i in range(2)]
    dbg_out = {}

    def sb(name, shape, dt=F32):
        return stack.enter_context(nc.sbuf_tensor(name, list(shape), dt))

    xT = sb("xT_sb", [P, KC, T])
    hT = sb("hT", [P, KC, T], BF16)
    wslots = [sb("wslot%d" % i, [P, WCOLS], BF16) for i in range(NSLOT)]
    modT = sb("modT", [P, 144])
    cols = sb("cols", [P, 9 * 16])
    gains = sb("gains_sb", [P, 64])
    rstd = sb("rstd", [P, T])
    tmpf = [sb("tmpf%d" % i, [P, T]) for i in range(2)]
    sqb = [sb("sqb%d" % i, [P, T], BF16) for i in range(2)]
    onesD = sb("onesD", [P, P], BF16)
    ident = sb("ident_sb", [P, P], BF16)
    YBYTES = 57344
    yar = sb("yar", [P, YBYTES // 2], BF16)
    XB = xT[:].rearrange("p a b -> p (a b)").bitcast(BF16)
    YB = yar[:]

    def carve(arena, off, nbytes, dt):
        assert off % 4 == 0 and nbytes % 4 == 0
        v = arena[:, off // 2:(off + nbytes) // 2]
        return v if dt == BF16 else v.bitcast(F32)

    ps = [stack.enter_context(nc.psum_tensor("ps%d" % i, [P, 512], F32)) for i in range(8)]
    PSK = [("ps", i) for i in range(8)]
    dq = sc.new_dsem("io")
    dq2 = sc.new_dsem("io2")

    def Xk(kc):
        return ("x", kc)

    def Hk(kc):
        return ("h", kc)

    seq = []
    NADA0 = 8
    for g in range(NADA0):
        seq.append((wada_d[g], True))

    def ffn_seq(f):
        gu_t = 0
        plan = []
        order = [("gu", 0)]
        for c in range(1, NCH):
            order += [("gu", c), ("d", c - 1)]
        order.append(("d", NCH - 1))
        for kind, c in order:
            if kind == "gu":
                for _ in range(CHUNKS[c] // 2):
                    plan.append(wgu_d[f][gu_t])
                    gu_t += 1
            else:
                for mg in range(4):
                    plan.append(wd_d[f][c * 4 + mg][:, 0:CHUNKS[c] * 512])
        return plan

    if "ffn1" in stages:
        for idx, tl_ in enumerate(ffn_seq(0)):
            seq.append((tl_, True))
            if idx < 24 - NADA0:
                seq.append((wada_d[NADA0 + idx], True))
    else:
        for g in range(NADA0, 24):
            seq.append((wada_d[g], True))
    if "mixer" not in stages:
        for g in range(24, 36):
            seq.append((wada_d[g], True))
    wi = 0

    def win_next(n=1):
        nonlocal wi
        out = [(win_d[wi + k], False) for k in range(n)]
        wi += n
        return out

    if "mixer" in stages:
        seq += win_next(4 + 4 + 1 + 1)
        for j in range(4):
            seq += win_next(1)
            for q3 in range(3):
                seq.append((wada_d[24 + j * 3 + q3], False))
        for mp in range(8):
            seq.append((wgb_d[mp], False))
        seq += win_next(8)
        for mp in range(8):
            seq.append((wgb_d[8 + mp], False))
        for mg in range(4):
            seq.append((wbr_d[mg], False))
        assert wi == NWIN
    W_FFN2 = len(seq)
    if "ffn2" in stages:
        for idx, tl_ in enumerate(ffn_seq(1)):
            seq.append((tl_, idx >= 3))
    yslot = carve(YB, 32768, 16384, BF16)
    ws = WStream(sc, wslots + [yslot], NSLOT, seq)
    ws.pf = 2
    wpos = [0]

    def wnext(pf=None):
        t, k = ws.get(wpos[0], pf)
        wpos[0] += 1
        return t, k

    def mm(out, lhsT, rhs, start, stop, r, w, sig):
        sc.op("pe", lambda e: e.matmul(out, lhsT=lhsT, rhs=rhs, start=start, stop=stop), r=r, w=w, sig=sig)

    def debug_dump(name, ap, shape, keys):
        if name not in dbg_names:
            return
        d = nc.dram_tensor("dbg_" + name, list(shape), ap.dtype, kind="ExternalOutput").ap()
        ds = sc.new_dsem("dbg_" + name)
        sc.dma("sp", d, ap, r=keys, dsem=ds)
        dbg_out[name] = ds

    sc.dma("sp", xT[:].rearrange("p a b -> p (a b)"), xT_d, w=[Xk(k) for k in range(KC)], dsem=dq)
    small = {}
    yoff = [0]

    def ycarve(nbytes, dt=F32):
        v = carve(YB, yoff[0], nbytes, dt)
        yoff[0] += nbytes
        return v

    for name, d_ap, n in (("cT", cT_d, KC),):
        t = ycarve(n * 4)
        sc.dma("sp", t, d_ap, w=[name], dsem=dq2)
        small[name] = t
    badaT_sb = sb("badaT_sb", [P, 144])
    sc.dma("sp", badaT_sb[:], badaT_d, w=["badaT"], dsem=dq2)
    small["badaT"] = badaT_sb[:]
    sinks_sb = sb("sinks_sb", [P, 16])
    sc.dma("sp", sinks_sb[:], sinks_d, w=["sinks"], dsem=dq2)
    sc.dma("sp", gains[:], gains_d, w=["gains"], dsem=dq2)
    identf = ycarve(P * 4)
    sc.dma("sp", identf, ident_d, w=["identf"], dsem=dq2)
    sc.op("dve", lambda e: e.tensor_copy(out=ident[:], in_=identf), r=["identf"], w=["ident"])
    sc.op("dve", lambda e: e.memset(onesD[:], 1.0 / D), w=["onesD"])
    ones1 = sb("ones1", [1, 1])
    sc.op("dve", lambda e: e.memset(ones1[:], 1.0), w=["ones1"])
    ones64 = sb("ones64", [P, 64], BF16)
    sc.op("dve", lambda e: e.memset(ones64[:], 1.0), w=["ones64"])
    ones256 = sb("ones256", [P, P], BF16)
    sc.op("dve", lambda e: e.memset(ones256[:], 1.0 / 256.0), w=["ones256"])
    cact = sb("cact", [P, KC], BF16)
    sc.op("act", lambda e: e.activation(out=cact[:], in_=small["cT"], func=AF.Silu), r=["cT"], w=["cact"])
    sinkE = sb("sinkE", [P, 16])
    sc.op("act", lambda e: e.activation(out=sinkE[:], in_=sinks_sb[:], func=AF.Exp), r=["sinks"], w=["sinkE"])

    rowseg = [carve(YB, 28672 + i * 2048, 2048, F32)[0:1, :] for i in range(2)]
    MODB = 7
    ada_g = [0]
    ada_nb = [2]

    def derive(i, parts):
        gi, sh, scl, gt, gscale = ((0, 0, 1, 2, 0.5), (16, 3, 4, 5, 1.0), (32, 6, 7, 8, 0.5))[i]
        A = cols[:, (3 * i) * 16:(3 * i + 1) * 16]
        B = cols[:, (3 * i + 1) * 16:(3 * i + 2) * 16]
        G = cols[:, (3 * i + 2) * 16:(3 * i + 3) * 16]
        if "AB" in parts:
            sc.op("dve", lambda e: e.scalar_tensor_tensor(out=A, in0=modT[:, scl * 16:(scl + 1) * 16], scalar=1.0,
                                                           in1=gains[:, gi:gi + 16], op0=ALU.add, op1=ALU.mult),
                  r=["modT", "gains"], w=[("cols", i, "A")])
            sc.op("dve", lambda e: e.tensor_copy(out=B, in_=modT[:, sh * 16:(sh + 1) * 16]), r=["modT"], w=[("cols", i, "B")])
        if "G" in parts:
            sc.op("dve", lambda e: e.tensor_scalar(out=G, in0=modT[:, gt * 16:(gt + 1) * 16], scalar1=gscale, scalar2=None,
                                                    op0=ALU.mult), r=["modT"], w=[("cols", i, "G")])

    def ada_tile(rowbank, colbank):
        g = ada_g[0]
        if g >= 36:
            return
        ada_g[0] += 1
        wt, wk = wnext()
        b = g % ada_nb[0]
        for kc in range(KC):
            mm(ps[rowbank][0:1, :], cact[:, kc:kc + 1], wt[:, kc * 512:(kc + 1) * 512], kc == 0, kc == KC - 1,
               r=[wk, "cact"], w=[PSK[rowbank]], sig=(kc == KC - 1))
        sc.op("act", lambda e: e.copy(out=rowseg[b], in_=ps[rowbank][0:1, :]), r=[PSK[rowbank]], w=[("rowseg", b)])
        for q4 in range(4):
            mm(ps[colbank][:, q4:q4 + 1], rowseg[b][:, q4 * 128:(q4 + 1) * 128], ones1[0:1, 0:1], True, True,
               r=[("rowseg", b), "ones1"], w=[PSK[colbank]], sig=(q4 == 3))
        sc.op("dve", lambda e: e.tensor_tensor(out=modT[:, g * 4:g * 4 + 4], in0=ps[colbank][:, 0:4], in1=small["badaT"][:, g * 4:g * 4 + 4], op=ALU.add),
              r=[PSK[colbank], "badaT"], w=["modT"])
        if g + 1 == 8:
            derive(0, "AB")
        elif g + 1 == 12:
            derive(0, "G")
        elif g + 1 == 24:
            derive(1, "ABG")
        elif g + 1 == 36:
            derive(2, "ABG")

    NADA_FFN1 = 24
    for g in range(NADA0):
        ada_tile(g % 2, 2 + g % 2)
    if "ffn1" not in stages:
        while ada_g[0] < NADA_FFN1:
            ada_tile(ada_g[0] % 2, 2 + ada_g[0] % 2)
    if "mixer" not in stages:
        while ada_g[0] < 36:
            ada_tile(ada_g[0] % 2, 2 + ada_g[0] % 2)
    debug_dump("modT", modT[:], [P, 144], ["modT"])

    def colA(i, kc):
        return cols[:, (3 * i) * 16 + kc:(3 * i) * 16 + kc + 1]

    def colB(i, kc):
        return cols[:, (3 * i + 1) * 16 + kc:(3 * i + 1) * 16 + kc + 1]

    def colG(i, kc):
        return cols[:, (3 * i + 2) * 16 + kc:(3 * i + 2) * 16 + kc + 1]

    def rms_stats(src, nblk, ones_t, keyf, banks):
        for kc in range(nblk):
            q = sqb[kc % 2]
            sc.op("act", lambda e: e.activation(out=q[:], in_=src(kc), func=AF.Square), r=[keyf(kc)], w=[("sqb", kc % 2)])
            for t in range(2):
                mm(ps[banks[t]][:], ones_t[:], q[:, t * TH:(t + 1) * TH], kc == 0, kc == nblk - 1,
                   r=[("sqb", kc % 2), "onesD", "ones256"], w=[PSK[banks[t]]], sig=True)
        for t in range(2):
            rs = rstd[:, t * TH:(t + 1) * TH]
            sc.op("dve", lambda e: e.tensor_scalar(out=rs, in0=ps[banks[t]][:], scalar1=EPS, scalar2=None, op0=ALU.add),
                  r=[PSK[banks[t]]], w=["rstd"])
            sc.op("act", lambda e: e.activation(out=rs, in_=rs, func=AF.Sqrt), r=["rstd"], w=["rstd"])
            sc.op("dve", lambda e: e.reciprocal(out=rs, in_=rs), r=["rstd"], w=["rstd"])

    def norm_mod(i):
        rms_stats(lambda kc: xT[:, kc, :], KC, onesD, Xk, (0, 1))
        for kc in range(KC):
            tm = tmpf[kc % 2]
            sc.op("dve", lambda e: e.scalar_tensor_tensor(out=tm[:], in0=xT[:, kc, :], scalar=colA(i, kc), in1=rstd[:],
                                                           op0=ALU.mult, op1=ALU.mult),
                  r=[Xk(kc), "rstd", ("cols", i, "A")], w=[("tmpf", kc % 2)])
            sc.op("act", lambda e: e.activation(out=hT[:, kc, :], in_=tm[:], func=AF.Identity, bias=colB(i, kc), scale=1.0),
                  r=[("tmpf", kc % 2), ("cols", i, "B")], w=[Hk(kc)])

    def ffn(i, act, sgt, hook=lambda: None):
        gu_rot = Rot([(0, 1), (2, 3)])
        d_rot = Rot([4, 5])
        sg_rot = Rot([0, 1])
        jbase = [sum(CHUNKS[:c]) for c in range(NCH)]
        cur = {"tile": None, "key": None, "idx": -1}

        def GU(c):
            for jl in range(CHUNKS[c]):
                j = jbase[c] + jl
                if j // 2 != cur["idx"]:
                    if cur["idx"] >= 0:
                        hook()
                    cur["tile"], cur["key"] = wnext()
                    cur["idx"] = j // 2
                wt, wk = cur["tile"], cur["key"]
                wj = j % 2
                for t in range(2):
                    bg, bu = gu_rot.next()
                    for which, bnk in ((0, bg), (1, bu)):
                        for kc in range(KC):
                            o = kc * 512 + wj * 256 + which * 128
                            mm(ps[bnk][:], wt[:, o:o + 128], hT[:, kc, t * TH:(t + 1) * TH], kc == 0, kc == KC - 1,
                               r=[wk, Hk(kc)], w=[PSK[bnk]], sig=(kc == KC - 1))
                    s = sg_rot.next()
                    sc.op("act", lambda e: e.activation(out=sgt[s], in_=ps[bg][:], func=AF.Silu), r=[PSK[bg]], w=[("sgt", s)])
                    sc.op("dve", lambda e: e.tensor_tensor(out=act[c % 2][:, jl, t * TH:(t + 1) * TH], in0=sgt[s],
                                                            in1=ps[bu][:], op=ALU.mult),
                          r=[("sgt", s), PSK[bu]], w=[("act", c % 2, jl, t)])

        def DOWN(c):
            nj = CHUNKS[c]
            for mg in range(4):
                hook()
                wt, wk = wnext()
                for m in range(4):
                    mmi = mg * 4 + m
                    for t in range(2):
                        bnk = d_rot.next()
                        for jl in range(nj):
                            o = jl * 512 + m * 128
                            mm(ps[bnk][:], wt[:, o:o + 128], act[c % 2][:, jl, t * TH:(t + 1) * TH], jl == 0, jl == nj - 1,
                               r=[wk, ("act", c % 2, jl, t)], w=[PSK[bnk]], sig=(jl == nj - 1))
                        xs = xT[:, mmi, t * TH:(t + 1) * TH]
                        sc.op("dve", lambda e: e.scalar_tensor_tensor(out=xs, in0=ps[bnk][:], scalar=colG(i, mmi), in1=xs,
                                                                       op0=ALU.mult, op1=ALU.add),
                              r=[PSK[bnk], ("cols", i, "G"), Xk(mmi)], w=[Xk(mmi)])

        GU(0)
        for c in range(1, NCH):
            GU(c)
            DOWN(c - 1)
        DOWN(NCH - 1)
        hook()

    def run_ffn(i):
        norm_mod(i)
        sc.barrier()
        a0 = carve(YB, 0, 12288, BF16).rearrange("p (j t) -> p j t", j=MAXNJ)
        a1 = carve(YB, 12288, 12288, BF16).rearrange("p (j t) -> p j t", j=MAXNJ)
        s0 = carve(YB, 24576, 2048, F32)
        s1 = carve(YB, 26624, 2048, F32)
        ws.pf = 2
        if i == 0:
            ffn(i, [a0, a1], [s0, s1], hook=lambda: (ada_tile(6, 7) if ada_g[0] < NADA_FFN1 else None))
            assert ada_g[0] == NADA_FFN1, ada_g[0]
        else:
            ffn(i, [a0, a1], [s0, s1])
        sc.barrier()

    if "ffn1" in stages:
        run_ffn(0)
    debug_dump("x1", xT[:].rearrange("p a b -> p (a b)"), [P, KC * T], [Xk(k) for k in range(KC)])

    def mixer():
        ws.pf = 1
        norm_mod(1)
        sc.dma("sp", xspill, xT[:].rearrange("p a b -> p (a b)"), r=[Xk(k) for k in range(KC)], w=["xspill"], dsem=dq)
        sc.barrier()
        yo = [0]

        def ytab(d_ap, n, name):
            t = carve(YB, yo[0], n * 4, F32)
            yo[0] += n * 4
            sc.dma("sp", t, d_ap, w=[name], dsem=dq2)
            return t

        rope = ytab(rope_d, 2 * T, "rope")
        qdec = ytab(qdec_d, 1024, "qdec")
        retmask = ytab(retmask_d, 1024, "retmask")
        kdec = ytab(kdec_d, 72, "kdec")
        swamask = ytab(swamask_d, 256, "swamask")
        sel = ytab(sel_d, 16, "sel")
        coef = ytab(coef_d, 32, "coef")
        assert yo[0] <= 18432
        cosT = rope[:, 0:T]
        sinT = rope[:, T:2 * T]
        ZOFF = 18432
        zT = carve(YB, ZOFF, 32768, BF16).rearrange("p (k t) -> p k t", k=KC)
        Sbf = carve(YB, 51200, 4096, BF16).rearrange("p (h e) -> p h e", h=8)
        khL = carve(YB, 55296, 1024, BF16).rearrange("p (n d) -> p n d", n=4)
        maskp0 = carve(YB, 56320, 512, F32)
        scb = [carve(YB, 56832 + i * 256, 256, BF16) for i in range(2)]
        svt = carve(YB, ZOFF, 9216, BF16).rearrange("p (n c) -> p n c", n=9)
        payload = carve(YB, ZOFF + 9216, 12288, F32)
        slot = carve(YB, ZOFF + 21504, 6144, F32)
        halo = carve(YB, ZOFF + 27648, 4096, F32)
        NPB = 3
        probs = [carve(YB, ZOFF + 9216 + i * 2048, 2048, BF16).rearrange("p (k q) -> p k q", k=2) for i in range(NPB)]
        ebuf = [carve(YB, ZOFF + 9216 + 6144 + i * 1024, 1024, BF16) for i in range(2)]
        dtmp = [carve(YB, ZOFF + 9216 + 8192 + i * 2048, 2048, F32) for i in range(2)]
        krT = carve(XB, 0, 16384, BF16).rearrange("p (h t) -> p h t", h=8)
        vtm = carve(XB, 16384, 32768, BF16).rearrange("p (n c) -> p n c", n=8)
        skT = carve(XB, 55296, 9216, BF16).rearrange("p (j t) -> p j t", j=4)
        soT = carve(XB, 0, 32768, BF16).rearrange("p (k t) -> p k t", k=KC)
        sqT = carve(XB, 32768, 8192, BF16).rearrange("p (g t) -> p g t", g=4)
        sq0 = carve(XB, 40960, 4096, BF16).rearrange("p (k q) -> p k q", k=16)
        Sst = carve(XB, 45056, 8192, F32).rearrange("p (h e) -> p h e", h=8)
        roG = carve(XB, 0, 32768, BF16).rearrange("p (k t) -> p k t", k=KC)
        ro = carve(XB, 32768, 8192, F32).rearrange("p (e t) -> p e t", e=2)
        krh = carve(XB, 40960, 2048, BF16)
        qr = carve(XB, 43008, 2048, BF16)
        vh = carve(XB, 53248, 4096, BF16).rearrange("p (n e) -> p n e", n=8)
        kh = [carve(XB, 57344 + i * 2048, 2048, BF16).rearrange("p (n d) -> p n d", n=8) for i in range(2)]
        qd = carve(XB, 61440, 2048, BF16)
        rt = [tmpf[0][:, 0:TH], tmpf[0][:, TH:T]]
        rowseg[0] = carve(XB, 53248, 2048, F32)[0:1, :]
        ada_nb[0] = 1
        sgr = [tmpf[1][:, 0:TH], tmpf[1][:, TH:T]]
        krT_sp = nc.dram_tensor("krT_sp", [P, 8192], BF16).ap()
        v_sp = nc.dram_tensor("v_sp", [P, 8 * 2048], BF16).ap()
        K_krT = lambda h: ("krT", h)
        K_v = lambda n: ("v", n)

        def rotary_from(bn, bs_, t, dst, dkeys):
            sc.op("dve", lambda e: e.tensor_tensor(out=rt[0], in0=ps[bn][:], in1=cosT[:, t * TH:(t + 1) * TH], op=ALU.mult),
                  r=[PSK[bn], "rope"], w=[("rt", 0)])
            sc.op("dve", lambda e: e.tensor_tensor(out=rt[1], in0=ps[bs_][:], in1=sinT[:, t * TH:(t + 1) * TH], op=ALU.mult),
                  r=[PSK[bs_], "rope"], w=[("rt", 1)])
            sc.op("pool", lambda e: e.tensor_tensor(out=dst, in0=rt[0], in1=rt[1], op=ALU.add),
                  r=[("rt", 0), ("rt", 1)], w=dkeys)

        def proj_fm(wt, wk, blk, nblk, t, bnk):
            for kc in range(KC):
                o = kc * (nblk * 128) + blk * 128
                mm(ps[bnk][:], wt[:, o:o + 128], hT[:, kc, t * TH:(t + 1) * TH], kc == 0, kc == KC - 1,
                   r=[wk, Hk(kc)], w=[PSK[bnk]], sig=(kc == KC - 1))

        prot = Rot([(0, 1), (2, 3)])
        vrot = Rot([4, 5, 6, 7])
        for tl in range(4):
            wt, wk = wnext()
            for hh in range(2):
                h = tl * 2 + hh
                for t in range(2):
                    bn, bs_ = prot.next()
                    proj_fm(wt, wk, hh * 2, 4, t, bn)
                    proj_fm(wt, wk, hh * 2 + 1, 4, t, bs_)
                    rotary_from(bn, bs_, t, krT[:, h, t * TH:(t + 1) * TH], [K_krT(h)])
        for tl in range(4):
            wt, wk = wnext()
            for n in range(8):
                bnk = vrot.next()
                for kc in range(KC):
                    mm(ps[bnk][:], hT[:, kc, n * 128:(n + 1) * 128], wt[:, kc * 512:(kc + 1) * 512], kc == 0, kc == KC - 1,
                       r=[wk, Hk(kc)], w=[PSK[bnk]], sig=(kc == KC - 1))
                sc.op("act", lambda e: e.copy(out=vtm[:, n, tl * 512:(tl + 1) * 512], in_=ps[bnk][:]), r=[PSK[bnk]], w=[K_v(n)])
        wt, wk = wnext()
        for j in range(4):
            for t in range(2):
                bnk = prot.next()[0]
                proj_fm(wt, wk, j, 4, t, bnk)
                sc.op("act", lambda e: e.copy(out=skT[:, j, 128 + t * TH:128 + (t + 1) * TH], in_=ps[bnk][:]), r=[PSK[bnk]], w=[("skT", j)])
        wt, wk = wnext()
        for n in range(8):
            bnk = vrot.next()
            for kc in range(KC):
                mm(ps[bnk][:], hT[:, kc, n * 128:(n + 1) * 128], wt[:, kc * 512:(kc + 1) * 512], kc == 0, kc == KC - 1,
                   r=[wk, Hk(kc)], w=[PSK[bnk]], sig=(kc == KC - 1))
            sc.op("act", lambda e: e.copy(out=svt[:, n + 1, :], in_=ps[bnk][:]), r=[PSK[bnk]], w=[("svt", n + 1)])

        for h in range(8):
            lb = h // 2
            lcol = (h % 2) * 256
            for half in range(2):
                bnk = vrot.next()
                for n4 in range(4):
                    n = half * 4 + n4
                    mm(ps[bnk][:, n4 * 128:(n4 + 1) * 128], krT[:, h, n * 128:(n + 1) * 128], ident[:], True, True,
                       r=[K_krT(h), "ident"], w=[PSK[bnk]], sig=(n4 == 3))
                kl = kdec[:, 8 + h * 8 + half * 4: 8 + h * 8 + half * 4 + 4]
                sc.op("dve", lambda e: e.tensor_tensor(out=khL, in0=ps[bnk][:].rearrange("p (n d) -> p n d", n=4),
                                                        in1=kl.unsqueeze(2).broadcast_to([P, 4, 128]), op=ALU.mult),
                      r=[PSK[bnk], "kdec"], w=["khL"])
                for n4 in range(4):
                    n = half * 4 + n4
                    mm(ps[lb][:, lcol:lcol + 256], khL[:, n4, :], vtm[:, n, h * 256:(h + 1) * 256], n == 0, n == 7,
                       r=["khL", K_v(n)], w=[PSK[lb]], sig=True)
            if h % 2 == 1:
                sc.op("act", lambda e: e.copy(out=payload[:, lb * 512:(lb + 1) * 512], in_=ps[lb][:]), r=[PSK[lb]], w=["payload"])
        sc.op("act", lambda e: e.copy(out=payload[:, 2048:2560].rearrange("p (j t) -> p j t", j=4), in_=skT[:, :, 1024:1152]),
              r=[("skT", j) for j in range(4)], w=["payload"])
        sc.op("act", lambda e: e.copy(out=payload[:, 2560:3072], in_=svt[:, 8, :]), r=[("svt", 8)], w=["payload"])
        for i4 in range(4):
            for hf in range(2):
                sc.op("dve", lambda e: e.tensor_scalar(out=slot, in0=payload[:, hf * 1536:(hf + 1) * 1536], scalar1=sel[:, i4:i4 + 1],
                                                        scalar2=None, op0=ALU.mult), r=["payload", "sel"], w=["slot"])
                sc.dma("sp", bounce[hf].ap()[i4 * P:(i4 + 1) * P, :], slot, r=["slot"], w=["bounce"], dsem=dq)
        sc.dma("sp", krT_sp, carve(XB, 0, 16384, BF16), r=[K_krT(h) for h in range(8)], w=["krT_sp"], dsem=dq2)
        sc.dma("sp", v_sp, carve(XB, 16384, 32768, BF16), r=[K_v(n) for n in range(8)], w=["v_sp"], dsem=dq2)
        ccsem = stack.enter_context(nc.semaphore("ccsem"))
        ccd = DSem(ccsem, "cc")
        Epool = sc.E["pool"]
        sc._deps(Epool, ["bounce"], ["gath"])
        for hf in range(2):
            nc.gpsimd.collective_compute("AllReduce", ALU.add, replica_groups=[[0, 1, 2, 3], [4, 5, 6, 7]],
                                         ins=[bounce[hf].ap().opt()], outs=[gath[hf].ap().opt()]).then_inc(ccsem)
        ccd.cnt = 2
        sc._record((ccd, 2), ["bounce"], ["gath"])
        sc.barrier()

        lrot = Rot([(0, 1), (2, 3)])
        nrot = Rot([(4, 5), (6, 7)])
        g4 = lambda ap: ap.rearrange("p (g q) -> p g q", g=4)

        prrot = Rot(list(range(NPB)))
        dcnt = [0]

        def swa_steps(blocks, hook_every=0):
            SKEW = 2
            steps = []
            for blk in blocks:
                bnum, bden = nrot.next()
                for s in range(2):
                    steps.append((blk, s, bnum, bden))
            st = {}

            def front(i):
                (j, n, qsrc, qkeys, mprev, mkeys), s, bnum, bden = steps[i]
                r0, r1 = s * 64, (s + 1) * 64
                pb = prrot.next()
                blg = lrot.next()
                st[i] = pb
                for kb in range(2):
                    mm(g4(ps[blg[kb]][:]), skT[r0:r1, j, (n + kb) * 128:(n + kb + 1) * 128], qsrc[r0:r1],
                       True, True, r=[("skT", j)] + qkeys, w=[PSK[blg[kb]]], sig=True)
                    sc.op("act", lambda e: e.activation(out=ebuf[kb], in_=ps[blg[kb]][:], func=AF.Exp, scale=0.125),
                          r=[PSK[blg[kb]]], w=[("ebuf", kb)])
                    mk = mprev if kb == 0 else swamask[:, 128:256]
                    sc.op("dve" if kb == 0 else "pool",
                          lambda e: e.tensor_tensor(out=g4(probs[pb][:, kb, :]), in0=g4(ebuf[kb]),
                                                    in1=mk.unsqueeze(1).broadcast_to([P, 4, 128]), op=ALU.mult),
                          r=[("ebuf", kb), "swamask"] + mkeys, w=[("probs", pb, kb)])

            def back(i):
                (j, n, qsrc, qkeys, mprev, mkeys), s, bnum, bden = steps[i]
                r0, r1 = s * 64, (s + 1) * 64
                pb = st.pop(i)
                hk = 2 * j + s
                for kb in range(2):
                    mm(ps[bnum][r0:r1, :], svt[:, n + kb, hk * 64:(hk + 1) * 64], probs[pb][:, kb, :], kb == 0, kb == 1,
                       r=[("svt", n + kb), ("probs", pb, kb)], w=[PSK[bnum]], sig=(kb == 1))
                for kb in range(2):
                    mm(ps[bden][r0:r1, :], ones64[:], probs[pb][:, kb, :], kb == 0, kb == 1,
                       r=["ones64", ("probs", pb, kb)], w=[PSK[bden]], sig=(kb == 1))
                if s == 1:
                    di = dcnt[0] % 2
                    dcnt[0] += 1
                    dt_ = dtmp[di]
                    sc.op("dve", lambda e: e.tensor_tensor(out=g4(dt_), in0=g4(ps[bden][:]),
                                                            in1=sinkE[:, j * 4:(j + 1) * 4].unsqueeze(2).broadcast_to([P, 4, 128]), op=ALU.add),
                          r=[PSK[bden], "sinkE"], w=[("dtmp", di)])
                    sc.op("dve", lambda e: e.reciprocal(out=dt_, in_=dt_), r=[("dtmp", di)], w=[("dtmp", di)])
                    sc.op("dve", lambda e: e.tensor_tensor(out=soT[:, j * 4:(j + 1) * 4, n * 128:(n + 1) * 128],
                                                            in0=g4(ps[bnum][:]), in1=g4(dt_), op=ALU.mult),
                          r=[PSK[bnum], ("dtmp", di)], w=[("so", j * 4 + g) for g in range(4)])

            N = len(steps)
            nh = 0
            for i in range(N + SKEW):
                if i < N:
                    front(i)
                if i - SKEW >= 0:
                    back(i - SKEW)
                if hook_every and i % hook_every == hook_every - 1 and nh < 3:
                    nh += 1
                    bk = vrot.next()
                    ada_tile(bk, bk)
            while hook_every and nh < 3:
                nh += 1
                bk = vrot.next()
                ada_tile(bk, bk)

        for j in range(4):
            wt, wk = wnext()
            for g in range(4):
                for t in range(2):
                    bnk = vrot.next()
                    proj_fm(wt, wk, g, 4, t, bnk)
                    sc.op("act", lambda e: e.copy(out=sqT[:, g, t * TH:(t + 1) * TH], in_=ps[bnk][:]), r=[PSK[bnk]], w=[("sqT", g)])
            sc.op("pool", lambda e: e.tensor_copy(out=sq0[:, j * 4:(j + 1) * 4, :], in_=sqT[:, :, 0:128]),
                  r=[("sqT", g) for g in range(4)], w=[("sq0", j)])
            swa_steps([(j, n, sqT[:, :, n * 128:(n + 1) * 128], [("sqT", g) for g in range(4)], swamask[:, 0:128], []) for n in range(1, 8)],
                      hook_every=4)

        first = {}

        def acc(eng, dst, src, scal, key, rkeys):
            if key not in first:
                first[key] = 1
                sc.op(eng, lambda e: e.tensor_scalar(out=dst, in0=src, scalar1=scal, scalar2=None, op0=ALU.mult),
                      r=["slot"] + rkeys, w=[key])
            else:
                sc.op(eng, lambda e: e.scalar_tensor_tensor(out=dst, in0=src, scalar=scal, in1=dst, op0=ALU.mult, op1=ALU.add),
                      r=["slot", key] + rkeys, w=[key])

        for i4 in range(4):
            for hf in range(2):
                sc.dma("sp", slot, gath[hf].ap()[i4 * P:(i4 + 1) * P, :], r=["gath"], w=["slot"], dsem=dq)
                heads = range(6) if hf == 0 else range(6, 8)
                for h in heads:
                    o = h * 256 - hf * 1536
                    acc("dve", Sst[:, h, :], slot[:, o:o + 256], coef[:, i4 * 8 + h:i4 * 8 + h + 1], ("Sst", h), ["coef"])
                if hf == 1:
                    acc("dve", halo, slot[:, 512:1536], sel[:, 4 + i4:5 + i4], "halo", ["sel"])
        sc.op("act", lambda e: e.copy(out=skT[:, :, 0:128], in_=halo[:, 0:512].rearrange("p (j t) -> p j t", j=4)),
              r=["halo"], w=[("skT", j) for j in range(4)])
        sc.op("act", lambda e: e.copy(out=svt[:, 0, :], in_=halo[:, 512:1024]), r=["halo"], w=[("svt", 0)])
        for h in range(8):
            sc.op("act", lambda e: e.copy(out=Sbf[:, h, :], in_=Sst[:, h, :]), r=[("Sst", h)], w=[("Sbf", h)])
        sc.op("dve", lambda e: e.tensor_scalar(out=maskp0, in0=swamask[:, 0:128], scalar1=sel[:, 8:9], scalar2=None, op0=ALU.mult),
              r=["swamask", "sel"], w=["maskp0"])
        swa_steps([(j, 0, sq0[:, j * 4:(j + 1) * 4, :], [("sq0", j)], maskp0, ["maskp0"]) for j in range(4)])
        debug_dump("soT", carve(XB, 0, 32768, BF16), [P, KC * T], [("so", k) for k in range(KC)])
        sc.barrier()

        def gated_branch(src, skeys, accumulate):
            grot = Rot([(0, 1), (2, 3), (4, 5), (6, 7)])
            for mp in range(8):
                wt, wk = wnext()
                for m2 in range(2):
                    mmi = mp * 2 + m2
                    for t in range(2):
                        bg, by = grot.next()
                        proj_fm(wt, wk, m2, 4, t, bg)
                        for kc in range(KC):
                            o = kc * 512 + (2 + m2) * 128
                            mm(ps[by][:], wt[:, o:o + 128], src[:, kc, t * TH:(t + 1) * TH], kc == 0, kc == KC - 1,
                               r=[wk, skeys(kc)], w=[PSK[by]], sig=(kc == KC - 1))
                        s = t
                        sc.op("act", lambda e: e.activation(out=rt[s], in_=ps[bg][:], func=AF.Sigmoid), r=[PSK[bg]], w=[("rt", s)])
                        zs = zT[:, mmi, t * TH:(t + 1) * TH]
                        if not accumulate:
                            sc.op("dve", lambda e: e.tensor_tensor(out=zs, in0=rt[s], in1=ps[by][:], op=ALU.mult),
                                  r=[("rt", s), PSK[by]], w=[("z", mmi)])
                        else:
                            sc.op("dve", lambda e: e.tensor_tensor(out=rt[s], in0=rt[s], in1=ps[by][:], op=ALU.mult),
                                  r=[("rt", s), PSK[by]], w=[("rt", s)])
                            sc.op("pool", lambda e: e.tensor_tensor(out=zs, in0=rt[s], in1=zs, op=ALU.add),
                                  r=[("rt", s), ("z", mmi)], w=[("z", mmi)])

        gated_branch(soT, lambda kc: ("so", kc), False)
        sc.barrier()

        gam = [1.0 - 2.0 ** (-5.0 - h) for h in range(8)]
        for h in range(8):
            wq, wqk = wnext()
            wg, wgk = wq, wqk
            if True:
                sc.dma("sp", krh, krT_sp[:, h * T:(h + 1) * T], r=["krT_sp"], w=["krh"], dsem=dq)
                sc.dma("sp", vh, v_sp.rearrange("p (n c) -> p n c", n=8)[:, :, h * 256:(h + 1) * 256], r=["v_sp"], w=["vh"], dsem=dq2)
                for t in range(2):
                    bn, bs_ = prot.next()
                    proj_fm(wq, wqk, 0, 4, t, bn)
                    proj_fm(wq, wqk, 1, 4, t, bs_)
                    rotary_from(bn, bs_, t, qr[:, t * TH:(t + 1) * TH], ["qr"])
                    sc.op("pool", lambda e: e.tensor_tensor(out=qd[:, t * TH:(t + 1) * TH].rearrange("p (n i) -> p n i", n=4),
                                                             in0=qr[:, t * TH:(t + 1) * TH].rearrange("p (n i) -> p n i", n=4),
                                                             in1=qdec[:, h * 128:(h + 1) * 128].unsqueeze(1).broadcast_to([P, 4, 128]),
                                                             op=ALU.mult), r=["qr", "qdec"], w=["qd"])
                khh = kh[h % 2]
                for half in range(2):
                    bnk = vrot.next()
                    for n4 in range(4):
                        n = half * 4 + n4
                        mm(ps[bnk][:, n4 * 128:(n4 + 1) * 128], krh[:, n * 128:(n + 1) * 128], ident[:], True, True,
                           r=["krh", "ident"], w=[PSK[bnk]], sig=(n4 == 3))
                    sc.op("act", lambda e: e.activation(out=khh[:, half * 4:(half + 1) * 4, :],
                                                        in_=ps[bnk][:].rearrange("p (n d) -> p n d", n=4), func=AF.Identity,
                                                        scale=kdec[:, h:h + 1]), r=[PSK[bnk], "kdec"], w=[("kh", h % 2)])
                g128 = gam[h] ** 128

                def emit_sc(n):
                    b_sc = vrot.next()
                    mm(ps[b_sc][:, 0:128], krh[:, n * 128:(n + 1) * 128], qr[:, n * 128:(n + 1) * 128], True, True,
                       r=["krh", "qr"], w=[PSK[b_sc]], sig=True)
                    sb_ = scb[n % 2]
                    sc.op("dve", lambda e: e.tensor_tensor(out=sb_, in0=ps[b_sc][:, 0:128], in1=retmask[:, h * 128:(h + 1) * 128], op=ALU.mult),
                          r=[PSK[b_sc], "retmask"], w=[("scb", n % 2)])

                emit_sc(0)
                for n in range(8):
                    b_kv = None
                    if n < 7:
                        b_kv = vrot.next()
                        mm(ps[b_kv][:, 0:256], khh[:, n, :], vh[:, n, :], True, True,
                           r=[("kh", h % 2), "vh"], w=[PSK[b_kv]], sig=True)
                    sb_ = scb[n % 2]
                    b_o = vrot.next()
                    for eh in range(2):
                        mm(ps[b_o][:, eh * 128:(eh + 1) * 128], vh[:, n, eh * 128:(eh + 1) * 128], sb_, True, False,
                           r=["vh", ("scb", n % 2)], w=[PSK[b_o]], sig=False)
                        mm(ps[b_o][:, eh * 128:(eh + 1) * 128], Sbf[:, h, eh * 128:(eh + 1) * 128], qd[:, n * 128:(n + 1) * 128], False, True,
                           r=[("Sbf", h), "qd"], w=[PSK[b_o]], sig=True)
                    if n < 7:
                        emit_sc(n + 1)
                        sc.op("dve", lambda e: e.scalar_tensor_tensor(out=Sst[:, h, :], in0=Sst[:, h, :], scalar=float(g128), in1=ps[b_kv][:, 0:256],
                                                                       op0=ALU.mult, op1=ALU.add), r=[("Sst", h), PSK[b_kv]], w=[("Sst", h)])
                        sc.op("act", lambda e: e.copy(out=Sbf[:, h, :], in_=Sst[:, h, :]), r=[("Sst", h)], w=[("Sbf", h)])
                    sc.op("act", lambda e: e.copy(out=ro[:, :, n * 128:(n + 1) * 128], in_=ps[b_o][:, 0:256].rearrange("p (e q) -> p e q", e=2)),
                          r=[PSK[b_o]], w=["ro"])
                rms_stats(lambda eh: ro[:, eh, :], 2, ones256, lambda eh: "ro", (0, 1))
                for eh in range(2):
                    for t in range(2):
                        bnk = vrot.next()
                        proj_fm(wg, wgk, 2 + eh, 4, t, bnk)
                        s = t
                        sc.op("act", lambda e: e.activation(out=sgr[s], in_=ps[bnk][:], func=AF.Silu), r=[PSK[bnk]], w=[("sgr", s)])
                        sc.op("dve", lambda e: e.tensor_tensor(out=sgr[s], in0=sgr[s], in1=rstd[:, t * TH:(t + 1) * TH], op=ALU.mult),
                              r=[("sgr", s), "rstd"], w=[("sgr", s)])
                        sc.op("pool", lambda e: e.tensor_tensor(out=roG[:, h * 2 + eh, t * TH:(t + 1) * TH], in0=sgr[s],
                                                                 in1=ro[:, eh, t * TH:(t + 1) * TH], op=ALU.mult),
                              r=[("sgr", s), "ro"], w=[("roG", h * 2 + eh)])
        debug_dump("roG", carve(XB, 0, 32768, BF16), [P, KC * T], [("roG", k) for k in range(KC)])
        debug_dump("krh", krh, [P, T], ["krh"])
        debug_dump("qr", qr, [P, T], ["qr"])
        debug_dump("qd", qd, [P, T], ["qd"])
        debug_dump("vh", carve(XB, 53248, 4096, BF16), [P, 2048], ["vh"])
        debug_dump("ro", carve(XB, 32768, 8192, F32), [P, 2048], ["ro"])
        debug_dump("Sst", carve(XB, 45056, 8192, F32), [P, 2048], [("Sst", h) for h in range(8)])
        debug_dump("kh", carve(XB, 57344 + 2048, 2048, BF16), [P, 1024], [("kh", 1)])
        sc.barrier()
        gated_branch(roG, lambda kc: ("roG", kc), True)
        sc.barrier()
        sc.dma("sp", xT[:].rearrange("p a b -> p (a b)"), xspill, r=["xspill"], w=[Xk(k) for k in range(KC)], dsem=dq)
        orot = Rot([0, 1, 2, 3, 4, 5, 6, 7])
        for mg in range(4):
            wt, wk = wnext()
            for m in range(4):
                mmi = mg * 4 + m
                for t in range(2):
                    bnk = orot.next()
                    for kc in range(KC):
                        o = kc * 512 + m * 128
                        mm(ps[bnk][:], wt[:, o:o + 128], zT[:, kc, t * TH:(t + 1) * TH], kc == 0, kc == KC - 1,
                           r=[wk, ("z", kc)], w=[PSK[bnk]], sig=(kc == KC - 1))
                    xs = xT[:, mmi, t * TH:(t + 1) * TH]
                    sc.op("dve", lambda e: e.scalar_tensor_tensor(out=xs, in0=ps[bnk][:], scalar=colG(1, mmi), in1=xs,
                                                                   op0=ALU.mult, op1=ALU.add), r=[PSK[bnk], ("cols", 1, "G"), Xk(mmi)], w=[Xk(mmi)])
        sc.barrier()

    if "mixer" in stages:
        mixer()
    debug_dump("x2", xT[:].rearrange("p a b -> p (a b)"), [P, KC * T], [Xk(k) for k in range(KC)])
    assert wpos[0] == W_FFN2, (wpos[0], W_FFN2)
    if "ffn2" in stages:
        run_ffn(2)

    rms_stats(lambda kc: xT[:, kc, :], KC, onesD, Xk, (0, 1))
    dout = [sc.new_dsem("out%d" % i) for i in range(2)]
    for kc in range(KC):
        tm = tmpf[kc % 2]
        sc.op("dve", lambda e: e.scalar_tensor_tensor(out=tm[:], in0=xT[:, kc, :], scalar=gains[:, 48 + kc:49 + kc], in1=rstd[:],
                                                       op0=ALU.mult, op1=ALU.mult), r=[Xk(kc), "rstd", "gains"], w=[("tmpf", kc % 2)])
        sc.dma("sp", yT_d[:, kc * T:(kc + 1) * T], tm[:], r=[("tmpf", kc % 2)], w=[("yT", kc)], dsem=dout[kc % 2])
    sc.final_wait("sp", [("yT", kc) for kc in range(KC)])
    E = sc.E["sp"]
    for d in dout:
        sc._wait(E, (d, d.cnt))
    for name, d in dbg_out.items():
        sc._wait(E, (d, d.cnt))
    assert wpos[0] == len(seq), (wpos[0], len(seq))
    stack.close()
    return nc


def _tile_lhs(w, blocks):
    nb = len(blocks)
    cols = np.concatenate(blocks)
    sub = w[:, cols].reshape(KC, P, nb * 128)
    return np.ascontiguousarray(sub.transpose(1, 0, 2)).reshape(P, KC * nb * 128)


def _prep_shared(inp):
    f = lambda a: np.asarray(a, dtype=np.float32)
    sh = {}
    w_ada = f(inp["w_ada"])[0]
    sh["wada"] = np.ascontiguousarray(w_ada.reshape(KC, P, 36, 512).transpose(2, 1, 0, 3)).reshape(36, P, WCOLS)
    for i, (gu, dn) in enumerate((("w_ffn1_gu", "w_ffn1_down"), ("w_ffn2_gu", "w_ffn2_down"))):
        wgu = f(inp[gu])[0]
        tiles = []
        for tt in range(22):
            blocks = []
            for j in (2 * tt, 2 * tt + 1):
                blocks.append(np.arange(j * 128, (j + 1) * 128))
                blocks.append(DFF + np.arange(j * 128, (j + 1) * 128))
            tiles.append(_tile_lhs(wgu, blocks))
        sh["wgu%d" % (i + 1)] = np.stack(tiles)
        wd = f(inp[dn])[0]
        dt = np.zeros((NCH * 4, P, MAXNJ * 512), np.float32)
        jb = 0
        for c, nj in enumerate(CHUNKS):
            rows = wd[jb * 128:(jb + nj) * 128].reshape(nj, P, 4, 512)
            for mg in range(4):
                dt[c * 4 + mg, :, :nj * 512] = rows[:, :, mg, :].transpose(1, 0, 2).reshape(P, nj * 512)
            jb += nj
        sh["wd%d" % (i + 1)] = dt
    w_in = f(inp["w_in"])[0]
    o_rq, o_rk, o_rv, o_rg, o_sq, o_sk, o_sv, o_gr, o_gs = 0, 1024, 2048, 4096, 6144, 8192, 8704, 9216, 11264
    sw = np.concatenate([np.arange(64, 128), np.arange(0, 64)])
    win = []

    def qk_tiles(off):
        out = []
        for tl in range(4):
            blocks = []
            for hh in range(2):
                h = tl * 2 + hh
                base = off + h * 128
                blocks.append(base + np.arange(128))
                blocks.append(base + sw)
            out.append(_tile_lhs(w_in, blocks))
        return out

    def rhs_tiles(off, ncols):
        out = []
        for tl in range(ncols // 512):
            sub = w_in[:, off + tl * 512: off + (tl + 1) * 512].reshape(KC, P, 512)
            out.append(np.ascontiguousarray(sub.transpose(1, 0, 2)).reshape(P, WCOLS))
        return out

    win += qk_tiles(o_rk)
    win += rhs_tiles(o_rv, 2048)
    win.append(_tile_lhs(w_in, [o_sk + j * 128 + np.arange(128) for j in range(4)]))
    win += rhs_tiles(o_sv, 512)
    for j in range(4):
        blocks = []
        for g in range(4):
            ha, hb = 8 * j + g, 8 * j + 4 + g
            blocks.append(np.concatenate([o_sq + ha * 64 + np.arange(64), o_sq + hb * 64 + np.arange(64)]))
        win.append(_tile_lhs(w_in, blocks))
    for h in range(8):
        win.append(_tile_lhs(w_in, [o_rq + h * 128 + np.arange(128), o_rq + h * 128 + sw,
                                    o_rg + (2 * h) * 128 + np.arange(128), o_rg + (2 * h + 1) * 128 + np.arange(128)]))
    sh["win"] = np.stack(win)
    assert sh["win"].shape[0] == 22
    w_swa = f(inp["w_swa_branch"])[0]
    perm = []
    for j in range(4):
        for g in range(4):
            for s in range(2):
                perm.append((8 * j + 4 * s + g) * 64 + np.arange(64))
    perm = np.concatenate(perm)
    w_swa_p = w_swa[perm]
    w_ret = f(inp["w_ret_branch"])[0]
    w_out = f(inp["w_out"])[0]
    wgb = []
    for goff, wmat in ((o_gs, w_swa_p), (o_gr, w_ret)):
        for mp in range(8):
            A = _tile_lhs(w_in, [goff + (2 * mp + b) * 128 + np.arange(128) for b in range(2)]).reshape(P, KC, 256)
            B = _tile_lhs(wmat, [(2 * mp + b) * 128 + np.arange(128) for b in range(2)]).reshape(P, KC, 256)
            wgb.append(np.concatenate([A, B], axis=2).reshape(P, WCOLS))
    sh["wgb"] = np.stack(wgb)
    sh["wbr"] = np.stack([_tile_lhs(w_out, [(mg * 4 + b) * 128 + np.arange(128) for b in range(4)]) for mg in range(4)])
    gam = np.array([1.0 - 2.0 ** (-5.0 - h) for h in range(8)], np.float64)
    i = np.arange(128)
    qdec = np.stack([gam[h] ** (i + 1.0) for h in range(8)]).reshape(1, 1024)
    sh["qdec"] = np.broadcast_to(qdec, (P, 1024)).astype(np.float32).copy()
    k_ = i[:, None]
    q_ = i[None, :]
    rm = np.concatenate([np.where(q_ >= k_, gam[h] ** np.maximum(q_ - k_, 0) * (128.0 ** -0.5), 0.0) for h in range(8)], axis=1)
    sh["retmask"] = rm.astype(np.float32)
    kd = np.zeros((P, 72), np.float64)
    for h in range(8):
        kd[:, h] = (128.0 ** -0.5) * gam[h] ** (127.0 - i)
        for n in range(8):
            kd[:, 8 + h * 8 + n] = (128.0 ** -0.5) * gam[h] ** (1023.0 - 128.0 * n - i)
    sh["kdec"] = kd.astype(np.float32)
    sm = np.zeros((P, 256), np.float32)
    sm[:, 0:128] = (k_ > q_)
    sm[:, 128:256] = (k_ <= q_)
    sh["swamask"] = sm
    sh["ident"] = np.eye(P, dtype=np.float32)
    gl = np.zeros((P, 64), np.float32)
    for c, nm in enumerate(("norm_ffn1", "norm_mix", "norm_ffn2")):
        gl[:, c * 16:(c + 1) * 16] = f(inp[nm])[0].reshape(KC, P).T
    gl[:, 48:64] = f(inp["norm_final"]).reshape(KC, P).T
    sh["gains"] = gl
    sk = f(inp["sinks"])[0]
    sl = np.zeros((P, 16), np.float32)
    for j in range(4):
        for g in range(4):
            sl[0:64, j * 4 + g] = sk[8 * j + g]
            sl[64:128, j * 4 + g] = sk[8 * j + 4 + g]
    sh["sinksl"] = sl
    return sh, gam


def _prep_core(inp, r, gam):
    f = lambda a: np.asarray(a, dtype=np.float32)
    b, q = r // 4, r % 4
    m = {}
    xs = f(inp["x"])[b, q * T:(q + 1) * T]
    m["xT"] = np.ascontiguousarray(xs.reshape(T, KC, P).transpose(2, 1, 0)).reshape(P, KC * T)
    m["cT"] = np.ascontiguousarray(f(inp["c"])[b].reshape(KC, P).T)
    m["badaT"] = np.ascontiguousarray(f(inp["b_ada"])[0].reshape(144, P).T)
    half = 64
    inv_freq = (1.0 / (np.float32(10000.0) ** (np.arange(half, dtype=np.float32) / np.float32(half)))).astype(np.float32)
    pos = np.arange(q * T, (q + 1) * T, dtype=np.float32)
    ang = (pos[None, :] * inv_freq[:, None]).astype(np.float32)
    cos = np.cos(ang).astype(np.float32)
    sin = np.sin(ang).astype(np.float32)
    rope = np.zeros((P, 2 * T), np.float32)
    rope[0:64, 0:T] = cos
    rope[64:128, 0:T] = cos
    rope[0:64, T:] = -sin
    rope[64:128, T:] = sin
    m["rope"] = rope
    sel = np.zeros((P, 16), np.float32)
    sel[:, q] = 1.0
    if q > 0:
        sel[:, 4 + q - 1] = 1.0
        sel[:, 8] = 1.0
    m["sel"] = sel
    cf = np.zeros((P, 32), np.float64)
    for i4 in range(q):
        for h in range(8):
            cf[:, i4 * 8 + h] = gam[h] ** (128.0 * 8 * (q - 1 - i4))
    m["coef"] = cf.astype(np.float32)
    return m


_CACHE = {}


def kernel(**inputs):
    sh, gam = _prep_shared(inputs)
    if "nc" not in _CACHE:
        _CACHE["nc"] = build_program()
    nc = _CACHE["nc"]
    in_maps = []
    for r in range(NCORES):
        m = dict(sh)
        m.update(_prep_core(inputs, r, gam))
        in_maps.append(m)
    res = run_bass_kernel_spmd(nc, in_maps, core_ids=list(range(NCORES)))
    B, S = 2, 4096
    out = np.zeros((B, S, D), np.float32)
    for r in range(NCORES):
        b, q = r // 4, r % 4
        yT = np.asarray(res.results[r]["yT"]).reshape(P, KC, T)
        out[b, q * T:(q + 1) * T, :] = yT.transpose(2, 1, 0).reshape(T, D)
    return out
```

```python
import numpy as np
import concourse.bass as bass
import concourse.mybir as mybir
from concourse.bass_utils import run_bass_kernel_spmd

F32 = mybir.dt.float32
BF16 = mybir.dt.bfloat16
AF = mybir.ActivationFunctionType
ALU = mybir.AluOpType

NCORES = 8
P = 128
D = 2048
KC = 16
T = 1024
TH = 512
DFF = 5632
NJ = 44
CHUNKS = [6, 6, 6, 4, 6, 6, 6, 4]
NCH = len(CHUNKS)
MAXNJ = max(CHUNKS)
EPS = 1e-6
WCOLS = 8192
NSLOT = 2
PF = 1


class Eng:
    def __init__(self, name, h, sem):
        self.name = name
        self.h = h
        self.sem = sem
        self.cnt = 0
        self.seen = {}
        self.pending = False


class DSem:
    def __init__(self, sem, name):
        self.sem = sem
        self.cnt = 0
        self.name = name


class Sched:
    def __init__(self, nc, stack):
        self.nc = nc
        self.stack = stack
        self.E = {}
        for name, h in (("pe", nc.tensor), ("act", nc.scalar), ("dve", nc.vector),
                        ("pool", nc.gpsimd), ("sp", nc.sync)):
            sem = stack.enter_context(nc.semaphore("sem_" + name))
            self.E[name] = Eng(name, h, sem)
        self.res_w = {}
        self.res_r = {}
        self.dsems = []

    def new_dsem(self, name):
        d = DSem(self.stack.enter_context(self.nc.semaphore("dsem_" + name)), name)
        self.dsems.append(d)
        return d

    def _wait(self, E, tok):
        S, v = tok
        if S is E and E.name == "pe":
            return
        if isinstance(S, Eng) and S.name == "pe" and S.pending and v > S.cnt:
            raise RuntimeError("wait on unsignaled PE op")
        if E.seen.get(S, 0) < v:
            E.h.wait_ge(S.sem, v)
            E.seen[S] = v

    def _deps(self, E, r, w):
        toks = []
        for k in r:
            t = self.res_w.get(k)
            if t is not None:
                toks.append(t)
        for k in w:
            t = self.res_w.get(k)
            if t is not None:
                toks.append(t)
            for S, v in self.res_r.get(k, {}).items():
                toks.append((S, v))
        for t in toks:
            self._wait(E, t)

    def _record(self, tok, r, w):
        S, v = tok
        for k in r:
            d = self.res_r.setdefault(k, {})
            if d.get(S, 0) < v:
                d[S] = v
        for k in w:
            self.res_w[k] = tok
            self.res_r[k] = {}

    def op(self, en, fn, r=(), w=(), sig=True):
        E = self.E[en]
        self._deps(E, r, w)
        ins = fn(E.h)
        if sig:
            E.cnt += 1
            ins.then_inc(E.sem, 1)
            E.pending = False
            tok = (E, E.cnt)
        else:
            E.pending = True
            tok = (E, E.cnt + 1)
        self._record(tok, r, w)
        return ins

    def dma(self, en, out, in_, r=(), w=(), dsem=None):
        E = self.E[en]
        self._deps(E, r, w)
        self._wait(E, (dsem, dsem.cnt))
        ins = E.h.dma_start(out=out, in_=in_)
        dsem.cnt += 16
        ins.then_inc(dsem.sem, 16)
        self._record((dsem, dsem.cnt), r, w)
        return ins

    def barrier(self):
        assert not self.E["pe"].pending
        for E in self.E.values():
            for F in self.E.values():
                if F is not E and F.cnt > 0:
                    self._wait(E, (F, F.cnt))
            for d in self.dsems:
                if d.cnt > 0:
                    self._wait(E, (d, d.cnt))

    def final_wait(self, en, keys):
        E = self.E[en]
        self._deps(E, keys, ())


class WStream:
    def __init__(self, sc, slots, nbase, seq):
        self.sc = sc
        self.slots = slots
        self.nbase = nbase
        self.seq = seq
        self.issued = 0
        self.pf = 1
        self.dsems = [sc.new_dsem("w%d" % i) for i in range(len(slots))]
        self.slot_tile = [-1] * len(slots)
        self.slot_of = {}

    def _try_issue(self, k, cur):
        ap, allow_extra = self.seq[k]
        cand = list(range(len(self.slots) if allow_extra else self.nbase))
        free = [c for c in cand if self.slot_tile[c] < cur]
        if not free:
            return False
        s = min(free, key=lambda c: self.slot_tile[c])
        ncols = ap.shape[-1]
        self.sc.dma("pool", self.slots[s][:, 0:ncols], ap, w=[("wslot", s)], dsem=self.dsems[s])
        self.slot_tile[s] = k
        self.slot_of[k] = s
        return True

    def get(self, i, pf=None):
        pf = self.pf if pf is None else pf
        while self.issued < min(i + pf + 1, len(self.seq)):
            if not self._try_issue(self.issued, i):
                assert self.issued > i, "no slot for the tile being consumed"
                break
            self.issued += 1
        s = self.slot_of[i]
        return self.slots[s], ("wslot", s)


class Rot:
    def __init__(self, items):
        self.items = list(items)
        self.i = 0

    def next(self):
        v = self.items[self.i % len(self.items)]
        self.i += 1
        return v


def build_program(dbg_names=(), stages=("ffn1", "mixer", "ffn2")):
    from contextlib import ExitStack
    nc = bass.Bass("TRN2", target_bir_lowering=False)
    stack = ExitStack()
    sc = Sched(nc, stack)

    def din(name, shape, dt=F32):
        return nc.dram_tensor(name, list(shape), dt, kind="ExternalInput").ap()

    xT_d = din("xT", [P, KC * T])
    cT_d = din("cT", [P, KC])
    badaT_d = din("badaT", [P, 144])
    gains_d = din("gains", [P, 64])
    rope_d = din("rope", [P, 2 * T])
    qdec_d = din("qdec", [P, 8 * 128])
    retmask_d = din("retmask", [P, 8 * 128])
    kdec_d = din("kdec", [P, 8 + 64])
    swamask_d = din("swamask", [P, 256])
    sel_d = din("sel", [P, 16])
    coef_d = din("coef", [P, 32])
    sinks_d = din("sinksl", [P, 16])
    ident_d = din("ident", [P, P])
    wada_d = din("wada", [36, P, WCOLS])
    wgu_d = [din("wgu1", [22, P, WCOLS]), din("wgu2", [22, P, WCOLS])]
    wd_d = [din("wd1", [NCH * 4, P, MAXNJ * 512]), din("wd2", [NCH * 4, P, MAXNJ * 512])]
    NWIN = 22
    win_d = din("win", [NWIN, P, WCOLS])
    wgb_d = din("wgb", [16, P, WCOLS])
    wbr_d = din("wbr", [4, P, WCOLS])
    yT_d = nc.dram_tensor("yT", [P, KC * T], F32, kind="ExternalOutput").ap()
    xspill = nc.dram_tensor("xspill", [P, KC * T], F32).ap()
    bounce = [nc.dram_tensor("bounce%d" % i, [4 * P, 1536], F32) for i in range(2)]
    gath = [nc.dram_tensor("gath%d" % i, [4 * P, 1536], F32) for i in range(2)]
    dbg_out = {}

    def sb(name, shape, dt=F32):
        return stack.enter_context(nc.sbuf_tensor(name, list(shape), dt))

    xT = sb("xT_sb", [P, KC, T])
    hT = sb("hT", [P, KC, T], BF16)
    wslots = [sb("wslot%d" % i, [P, WCOLS], BF16) for i in range(NSLOT)]
    modT = sb("modT", [P, 144])
    cols = sb("cols", [P, 9 * 16])
    gains = sb("gains_sb", [P, 64])
    rstd = sb("rstd", [P, T])
    tmpf = [sb("tmpf%d" % i, [P, T]) for i in range(2)]
    sqb = [sb("sqb%d" % i, [P, T], BF16) for i in range(2)]
    onesD = sb("onesD", [P, P], BF16)
    ident = sb("ident_sb", [P, P], BF16)
    YBYTES = 57344
    yar = sb("yar", [P, YBYTES // 2], BF16)
    XB = xT[:].rearrange("p a b -> p (a b)").bitcast(BF16)
    YB = yar[:]

    def carve(arena, off, nbytes, dt):
        assert off % 4 == 0 and nbytes % 4 == 0
        v = arena[:, off // 2:(off + nbytes) // 2]
        return v if dt == BF16 else v.bitcast(F32)

    ps = [stack.enter_context(nc.psum_tensor("ps%d" % i, [P, 512], F32)) for i in range(8)]
    PSK = [("ps", i) for i in range(8)]
    dq = sc.new_dsem("io")
    dq2 = sc.new_dsem("io2")

    def Xk(kc):
        return ("x", kc)

    def Hk(kc, t):
        return ("h", kc, t)

    seq = []
    NADA0 = 8
    for g in range(NADA0):
        seq.append((wada_d[g], True))

    def ffn_seq(f):
        gu_t = 0
        plan = []
        order = [("gu", 0)]
        for c in range(1, NCH):
            order += [("gu", c), ("d", c - 1)]
        order.append(("d", NCH - 1))
        for kind, c in order:
            if kind == "gu":
                for _ in range(CHUNKS[c] // 2):
                    plan.append(wgu_d[f][gu_t])
                    gu_t += 1
            else:
                for mg in range(4):
                    plan.append(wd_d[f][c * 4 + mg][:, 0:CHUNKS[c] * 512])
        return plan

    if "ffn1" in stages:
        for idx, tl_ in enumerate(ffn_seq(0)):
            seq.append((tl_, True))
            if idx < 24 - NADA0:
                seq.append((wada_d[NADA0 + idx], True))
    else:
        for g in range(NADA0, 24):
            seq.append((wada_d[g], True))
    if "mixer" not in stages:
        for g in range(24, 36):
            seq.append((wada_d[g], True))
    wi = 0

    def win_next(n=1):
        nonlocal wi
        out = [(win_d[wi + k], False) for k in range(n)]
        wi += n
        return out

    if "mixer" in stages:
        seq += win_next(4 + 4 + 1 + 1)
        for j in range(4):
            seq += win_next(1)
            for q3 in range(3):
                seq.append((wada_d[24 + j * 3 + q3], False))
        for mp in range(8):
            seq.append((wgb_d[mp], False))
        seq += win_next(8)
        for mp in range(8):
            seq.append((wgb_d[8 + mp], False))
        for mg in range(4):
            seq.append((wbr_d[mg], False))
        assert wi == NWIN
    W_FFN2 = len(seq)
    if "ffn2" in stages:
        for idx, tl_ in enumerate(ffn_seq(1)):
            seq.append((tl_, idx >= 3))
    yslot = carve(YB, 32768, 16384, BF16)
    ws = WStream(sc, wslots + [yslot], NSLOT, seq)
    ws.pf = 2
    wpos = [0]

    def wnext(pf=None):
        t, k = ws.get(wpos[0], pf)
        wpos[0] += 1
        return t, k

    def mm(out, lhsT, rhs, start, stop, r, w, sig):
        sc.op("pe", lambda e: e.matmul(out, lhsT=lhsT, rhs=rhs, start=start, stop=stop), r=r, w=w, sig=sig)

    def debug_dump(name, ap, shape, keys):
        if name not in dbg_names:
            return
        d = nc.dram_tensor("dbg_" + name, list(shape), ap.dtype, kind="ExternalOutput").ap()
        ds = sc.new_dsem("dbg_" + name)
        sc.dma("sp", d, ap, r=keys, dsem=ds)
        dbg_out[name] = ds

    sc.dma("sp", xT[:].rearrange("p a b -> p (a b)"), xT_d, w=[Xk(k) for k in range(KC)], dsem=dq)
    small = {}
    yoff = [0]

    def ycarve(nbytes, dt=F32):
        v = carve(YB, yoff[0], nbytes, dt)
        yoff[0] += nbytes
        return v

    for name, d_ap, n in (("cT", cT_d, KC),):
        t = ycarve(n * 4)
        sc.dma("sp", t, d_ap, w=[name], dsem=dq2)
        small[name] = t
    badaT_sb = sb("badaT_sb", [P, 144])
    sc.dma("sp", badaT_sb[:], badaT_d, w=["badaT"], dsem=dq2)
    small["badaT"] = badaT_sb[:]
    sinks_sb = sb("sinks_sb", [P, 16])
    sc.dma("sp", sinks_sb[:], sinks_d, w=["sinks"], dsem=dq2)
    sc.dma("sp", gains[:], gains_d, w=["gains"], dsem=dq2)
    identf = ycarve(P * 4)
    sc.dma("sp", identf, ident_d, w=["identf"], dsem=dq2)
    sc.op("dve", lambda e: e.tensor_copy(out=ident[:], in_=identf), r=["identf"], w=["ident"])
    sc.op("dve", lambda e: e.memset(onesD[:], 1.0 / D), w=["onesD"])
    ones1 = sb("ones1", [1, 1])
    sc.op("dve", lambda e: e.memset(ones1[:], 1.0), w=["ones1"])
    ones64 = sb("ones64", [P, 64], BF16)
    sc.op("dve", lambda e: e.memset(ones64[:], 1.0), w=["ones64"])
    ones256 = sb("ones256", [P, P], BF16)
    sc.op("dve", lambda e: e.memset(ones256[:], 1.0 / 256.0), w=["ones256"])
    cact = sb("cact", [P, KC], BF16)
    sc.op("act", lambda e: e.activation(out=cact[:], in_=small["cT"], func=AF.Silu), r=["cT"], w=["cact"])
    sinkE = sb("sinkE", [P, 16])
    sc.op("act", lambda e: e.activation(out=sinkE[:], in_=sinks_sb[:], func=AF.Exp), r=["sinks"], w=["sinkE"])

    rowseg = [carve(YB, 28672 + i * 2048, 2048, F32)[0:1, :] for i in range(2)]
    MODB = 7
    ada_g = [0]
    ada_nb = [2]

    def derive(i, parts):
        gi, sh, scl, gt, gscale = ((0, 0, 1, 2, 0.5), (16, 3, 4, 5, 1.0), (32, 6, 7, 8, 0.5))[i]
        A = cols[:, (3 * i) * 16:(3 * i + 1) * 16]
        B = cols[:, (3 * i + 1) * 16:(3 * i + 2) * 16]
        G = cols[:, (3 * i + 2) * 16:(3 * i + 3) * 16]
        if "AB" in parts:
            sc.op("dve", lambda e: e.scalar_tensor_tensor(out=A, in0=modT[:, scl * 16:(scl + 1) * 16], scalar=1.0,
                                                           in1=gains[:, gi:gi + 16], op0=ALU.add, op1=ALU.mult),
                  r=["modT", "gains"], w=[("cols", i, "A")])
            sc.op("dve", lambda e: e.tensor_copy(out=B, in_=modT[:, sh * 16:(sh + 1) * 16]), r=["modT"], w=[("cols", i, "B")])
        if "G" in parts:
            sc.op("dve", lambda e: e.tensor_scalar(out=G, in0=modT[:, gt * 16:(gt + 1) * 16], scalar1=gscale, scalar2=None,
                                                    op0=ALU.mult), r=["modT"], w=[("cols", i, "G")])

    def ada_tile(rowbank, colbank):
        g = ada_g[0]
        if g >= 36:
            return
        ada_g[0] += 1
        wt, wk = wnext()
        b = g % ada_nb[0]
        for kc in range(KC):
            mm(ps[rowbank][0:1, :], cact[:, kc:kc + 1], wt[:, kc * 512:(kc + 1) * 512], kc == 0, kc == KC - 1,
               r=[wk, "cact"], w=[PSK[rowbank]], sig=(kc == KC - 1))
        sc.op("act", lambda e: e.copy(out=rowseg[b], in_=ps[rowbank][0:1, :]), r=[PSK[rowbank]], w=[("rowseg", b)])
        for q4 in range(4):
            mm(ps[colbank][:, q4:q4 + 1], rowseg[b][:, q4 * 128:(q4 + 1) * 128], ones1[0:1, 0:1], True, True,
               r=[("rowseg", b), "ones1"], w=[PSK[colbank]], sig=(q4 == 3))
        sc.op("dve", lambda e: e.tensor_tensor(out=modT[:, g * 4:g * 4 + 4], in0=ps[colbank][:, 0:4], in1=small["badaT"][:, g * 4:g * 4 + 4], op=ALU.add),
              r=[PSK[colbank], "badaT"], w=["modT"])
        if g + 1 == 8:
            derive(0, "AB")
        elif g + 1 == 12:
            derive(0, "G")
        elif g + 1 == 24:
            derive(1, "ABG")
        elif g + 1 == 36:
            derive(2, "ABG")

    NADA_FFN1 = 24
    sqh = [sqb[0][:, 0:TH], sqb[0][:, TH:T], sqb[1][:, 0:TH], sqb[1][:, TH:T]]
    tmh = [tmpf[0][:, 0:TH], tmpf[0][:, TH:T], tmpf[1][:, 0:TH], tmpf[1][:, TH:T]]
    nm_cnt = [0]

    def norm_stats(t):
        for kc in range(KC):
            qi = kc % 4
            sc.op("act", lambda e: e.activation(out=sqh[qi], in_=xT[:, kc, t * TH:(t + 1) * TH], func=AF.Square),
                  r=[Xk(kc)], w=[("sqh", qi)])
            mm(ps[t][:], onesD[:], sqh[qi], kc == 0, kc == KC - 1, r=[("sqh", qi), "onesD"], w=[PSK[t]], sig=True)
        rs = rstd[:, t * TH:(t + 1) * TH]
        sc.op("dve", lambda e: e.tensor_scalar(out=rs, in0=ps[t][:], scalar1=EPS, scalar2=None, op0=ALU.add),
              r=[PSK[t]], w=[("rstd", t)])
        sc.op("act", lambda e: e.activation(out=rs, in_=rs, func=AF.Sqrt), r=[("rstd", t)], w=[("rstd", t)])
        sc.op("dve", lambda e: e.reciprocal(out=rs, in_=rs), r=[("rstd", t)], w=[("rstd", t)])

    if "ffn1" in stages:
        norm_stats(0)
        norm_stats(1)
    for g in range(NADA0):
        ada_tile(4 + g % 2, 6 + g % 2)
    if "ffn1" not in stages:
        while ada_g[0] < NADA_FFN1:
            ada_tile(ada_g[0] % 2, 2 + ada_g[0] % 2)
    if "mixer" not in stages:
        while ada_g[0] < 36:
            ada_tile(ada_g[0] % 2, 2 + ada_g[0] % 2)
    debug_dump("modT", modT[:], [P, 144], ["modT"])

    def colA(i, kc):
        return cols[:, (3 * i) * 16 + kc:(3 * i) * 16 + kc + 1]

    def colB(i, kc):
        return cols[:, (3 * i + 1) * 16 + kc:(3 * i + 1) * 16 + kc + 1]

    def colG(i, kc):
        return cols[:, (3 * i + 2) * 16 + kc:(3 * i + 2) * 16 + kc + 1]

    def rms_stats(src, nblk, ones_t, keyf, banks):
        for kc in range(nblk):
            q = sqb[kc % 2]
            sc.op("act", lambda e: e.activation(out=q[:], in_=src(kc), func=AF.Square), r=[keyf(kc)], w=[("sqb", kc % 2)])
            for t in range(2):
                mm(ps[banks[t]][:], ones_t[:], q[:, t * TH:(t + 1) * TH], kc == 0, kc == nblk - 1,
                   r=[("sqb", kc % 2), "onesD", "ones256"], w=[PSK[banks[t]]], sig=True)
        for t in range(2):
            rs = rstd[:, t * TH:(t + 1) * TH]
            sc.op("dve", lambda e: e.tensor_scalar(out=rs, in0=ps[banks[t]][:], scalar1=EPS, scalar2=None, op0=ALU.add),
                  r=[PSK[banks[t]]], w=["rstd"])
            sc.op("act", lambda e: e.activation(out=rs, in_=rs, func=AF.Sqrt), r=["rstd"], w=["rstd"])
            sc.op("dve", lambda e: e.reciprocal(out=rs, in_=rs), r=["rstd"], w=["rstd"])

    def norm_apply(i, t):
        rs = rstd[:, t * TH:(t + 1) * TH]
        for kc in range(KC):
            ti = nm_cnt[0] % 4
            nm_cnt[0] += 1
            sc.op("dve", lambda e: e.scalar_tensor_tensor(out=tmh[ti], in0=xT[:, kc, t * TH:(t + 1) * TH], scalar=colA(i, kc), in1=rs,
                                                           op0=ALU.mult, op1=ALU.mult),
                  r=[Xk(kc), ("rstd", t), ("cols", i, "A")], w=[("tmh", ti)])
            sc.op("act", lambda e: e.activation(out=hT[:, kc, t * TH:(t + 1) * TH], in_=tmh[ti], func=AF.Identity, bias=colB(i, kc), scale=1.0),
                  r=[("tmh", ti), ("cols", i, "B")], w=[Hk(kc, t)])

    def norm_mod(i, stats_done=False):
        for t in range(2):
            if not stats_done:
                norm_stats(t)
            norm_apply(i, t)

    def ffn(i, act, sgt, hook=lambda: None):
        gu_rot = Rot([(0, 1), (2, 3)])
        d_rot = Rot([4, 5])
        sg_rot = Rot([0, 1])
        jbase = [sum(CHUNKS[:c]) for c in range(NCH)]
        cur = {"tile": None, "key": None, "idx": -1}

        def GU(c):
            for jl in range(CHUNKS[c]):
                j = jbase[c] + jl
                if j // 2 != cur["idx"]:
                    if cur["idx"] >= 0:
                        hook()
                    cur["tile"], cur["key"] = wnext()
                    cur["idx"] = j // 2
                wt, wk = cur["tile"], cur["key"]
                wj = j % 2
                for t in range(2):
                    bg, bu = gu_rot.next()
                    for which, bnk in ((0, bg), (1, bu)):
                        for kc in range(KC):
                            o = kc * 512 + wj * 256 + which * 128
                            mm(ps[bnk][:], wt[:, o:o + 128], hT[:, kc, t * TH:(t + 1) * TH], kc == 0, kc == KC - 1,
                               r=[wk, Hk(kc, t)], w=[PSK[bnk]], sig=(kc == KC - 1))
                    s = sg_rot.next()
                    sc.op("act", lambda e: e.activation(out=sgt[s], in_=ps[bg][:], func=AF.Silu), r=[PSK[bg]], w=[("sgt", s)])
                    sc.op("dve", lambda e: e.tensor_tensor(out=act[c % 2][:, jl, t * TH:(t + 1) * TH], in0=sgt[s],
                                                            in1=ps[bu][:], op=ALU.mult),
                          r=[("sgt", s), PSK[bu]], w=[("act", c % 2, jl, t)])

        def DOWN(c):
            nj = CHUNKS[c]
            for mg in range(4):
                hook()
                wt, wk = wnext()
                for m in range(4):
                    mmi = mg * 4 + m
                    for t in range(2):
                        bnk = d_rot.next()
                        for jl in range(nj):
                            o = jl * 512 + m * 128
                            mm(ps[bnk][:], wt[:, o:o + 128], act[c % 2][:, jl, t * TH:(t + 1) * TH], jl == 0, jl == nj - 1,
                               r=[wk, ("act", c % 2, jl, t)], w=[PSK[bnk]], sig=(jl == nj - 1))
                        xs = xT[:, mmi, t * TH:(t + 1) * TH]
                        sc.op("dve", lambda e: e.scalar_tensor_tensor(out=xs, in0=ps[bnk][:], scalar=colG(i, mmi), in1=xs,
                                                                       op0=ALU.mult, op1=ALU.add),
                              r=[PSK[bnk], ("cols", i, "G"), Xk(mmi)], w=[Xk(mmi)])

        GU(0)
        for c in range(1, NCH):
            GU(c)
            DOWN(c - 1)
        DOWN(NCH - 1)
        hook()

    def run_ffn(i):
        norm_mod(i, stats_done=(i == 0))
        sc.barrier()
        a0 = carve(YB, 0, 12288, BF16).rearrange("p (j t) -> p j t", j=MAXNJ)
        a1 = carve(YB, 12288, 12288, BF16).rearrange("p (j t) -> p j t", j=MAXNJ)
        s0 = carve(YB, 24576, 2048, F32)
        s1 = carve(YB, 26624, 2048, F32)
        ws.pf = 2
        if i == 0:
            ffn(i, [a0, a1], [s0, s1], hook=lambda: (ada_tile(6, 7) if ada_g[0] < NADA_FFN1 else None))
            assert ada_g[0] == NADA_FFN1, ada_g[0]
        else:
            ffn(i, [a0, a1], [s0, s1])
        sc.barrier()

    if "ffn1" in stages:
        run_ffn(0)
    debug_dump("x1", xT[:].rearrange("p a b -> p (a b)"), [P, KC * T], [Xk(k) for k in range(KC)])

    def mixer():
        ws.pf = 1
        norm_mod(1)
        sc.dma("sp", xspill, xT[:].rearrange("p a b -> p (a b)"), r=[Xk(k) for k in range(KC)], w=["xspill"], dsem=dq)
        sc.barrier()
        yo = [0]

        def ytab(d_ap, n, name):
            t = carve(YB, yo[0], n * 4, F32)
            yo[0] += n * 4
            sc.dma("sp", t, d_ap, w=[name], dsem=dq2)
            return t

        rope = ytab(rope_d, 2 * T, "rope")
        qdec = ytab(qdec_d, 1024, "qdec")
        retmask = ytab(retmask_d, 1024, "retmask")
        kdec = ytab(kdec_d, 72, "kdec")
        swamask = ytab(swamask_d, 256, "swamask")
        sel = ytab(sel_d, 16, "sel")
        coef = ytab(coef_d, 32, "coef")
        assert yo[0] <= 18432
        cosT = rope[:, 0:T]
        sinT = rope[:, T:2 * T]
        ZOFF = 18432
        zT = carve(YB, ZOFF, 32768, BF16).rearrange("p (k t) -> p k t", k=KC)
        Sbf = carve(YB, 51200, 4096, BF16).rearrange("p (h e) -> p h e", h=8)
        khL = carve(YB, 55296, 1024, BF16).rearrange("p (n d) -> p n d", n=4)
        maskp0 = carve(YB, 56320, 512, F32)
        scb = [carve(YB, 56832 + i * 256, 256, BF16) for i in range(2)]
        svt = carve(YB, ZOFF, 9216, BF16).rearrange("p (n c) -> p n c", n=9)
        payload = carve(YB, ZOFF + 9216, 12288, F32)
        slot = carve(YB, ZOFF + 21504, 6144, F32)
        halo = carve(YB, ZOFF + 27648, 4096, F32)
        NPB = 3
        probs = [carve(YB, ZOFF + 9216 + i * 2048, 2048, BF16).rearrange("p (k q) -> p k q", k=2) for i in range(NPB)]
        ebuf = [carve(YB, ZOFF + 9216 + 6144 + i * 1024, 1024, BF16) for i in range(2)]
        dtmp = [carve(YB, ZOFF + 9216 + 8192 + i * 2048, 2048, F32) for i in range(2)]
        krT = carve(XB, 0, 16384, BF16).rearrange("p (h t) -> p h t", h=8)
        vtm = carve(XB, 16384, 32768, BF16).rearrange("p (n c) -> p n c", n=8)
        skT = carve(XB, 55296, 9216, BF16).rearrange("p (j t) -> p j t", j=4)
        soT = carve(XB, 0, 32768, BF16).rearrange("p (k t) -> p k t", k=KC)
        sqT = carve(XB, 32768, 8192, BF16).rearrange("p (g t) -> p g t", g=4)
        sq0 = carve(XB, 40960, 4096, BF16).rearrange("p (k q) -> p k q", k=16)
        Sst = carve(XB, 45056, 8192, F32).rearrange("p (h e) -> p h e", h=8)
        roG = carve(XB, 0, 32768, BF16).rearrange("p (k t) -> p k t", k=KC)
        ro = carve(XB, 32768, 8192, F32).rearrange("p (e t) -> p e t", e=2)
        krh = carve(XB, 40960, 2048, BF16)
        qr = carve(XB, 43008, 2048, BF16)
        vh = carve(XB, 53248, 4096, BF16).rearrange("p (n e) -> p n e", n=8)
        kh = [carve(XB, 57344 + i * 2048, 2048, BF16).rearrange("p (n d) -> p n d", n=8) for i in range(2)]
        qd = carve(XB, 61440, 2048, BF16)
        rt = [tmpf[0][:, 0:TH], tmpf[0][:, TH:T]]
        rowseg[0] = carve(XB, 53248, 2048, F32)[0:1, :]
        ada_nb[0] = 1
        sgr = [tmpf[1][:, 0:TH], tmpf[1][:, TH:T]]
        krT_sp = nc.dram_tensor("krT_sp", [P, 8192], BF16).ap()
        v_sp = nc.dram_tensor("v_sp", [P, 8 * 2048], BF16).ap()
        K_krT = lambda h: ("krT", h)
        K_v = lambda n: ("v", n)

        def rotary_from(bn, bs_, t, dst, dkeys):
            sc.op("dve", lambda e: e.tensor_tensor(out=rt[0], in0=ps[bn][:], in1=cosT[:, t * TH:(t + 1) * TH], op=ALU.mult),
                  r=[PSK[bn], "rope"], w=[("rt", 0)])
            sc.op("dve", lambda e: e.tensor_tensor(out=rt[1], in0=ps[bs_][:], in1=sinT[:, t * TH:(t + 1) * TH], op=ALU.mult),
                  r=[PSK[bs_], "rope"], w=[("rt", 1)])
            sc.op("pool", lambda e: e.tensor_tensor(out=dst, in0=rt[0], in1=rt[1], op=ALU.add),
                  r=[("rt", 0), ("rt", 1)], w=dkeys)

        def proj_fm(wt, wk, blk, nblk, t, bnk):
            for kc in range(KC):
                o = kc * (nblk * 128) + blk * 128
                mm(ps[bnk][:], wt[:, o:o + 128], hT[:, kc, t * TH:(t + 1) * TH], kc == 0, kc == KC - 1,
                   r=[wk, Hk(kc, t)], w=[PSK[bnk]], sig=(kc == KC - 1))

        prot = Rot([(0, 1), (2, 3)])
        vrot = Rot([4, 5, 6, 7])
        for tl in range(4):
            wt, wk = wnext()
            for hh in range(2):
                h = tl * 2 + hh
                for t in range(2):
                    bn, bs_ = prot.next()
                    proj_fm(wt, wk, hh * 2, 4, t, bn)
                    proj_fm(wt, wk, hh * 2 + 1, 4, t, bs_)
                    rotary_from(bn, bs_, t, krT[:, h, t * TH:(t + 1) * TH], [K_krT(h)])
        for tl in range(4):
            wt, wk = wnext()
            for n in range(8):
                bnk = vrot.next()
                for kc in range(KC):
                    mm(ps[bnk][:], hT[:, kc, n * 128:(n + 1) * 128], wt[:, kc * 512:(kc + 1) * 512], kc == 0, kc == KC - 1,
                       r=[wk, Hk(kc, n // 4)], w=[PSK[bnk]], sig=(kc == KC - 1))
                sc.op("act", lambda e: e.copy(out=vtm[:, n, tl * 512:(tl + 1) * 512], in_=ps[bnk][:]), r=[PSK[bnk]], w=[K_v(n)])
        wt, wk = wnext()
        for j in range(4):
            for t in range(2):
                bnk = prot.next()[0]
                proj_fm(wt, wk, j, 4, t, bnk)
                sc.op("act", lambda e: e.copy(out=skT[:, j, 128 + t * TH:128 + (t + 1) * TH], in_=ps[bnk][:]), r=[PSK[bnk]], w=[("skT", j)])
        wt, wk = wnext()
        for n in range(8):
            bnk = vrot.next()
            for kc in range(KC):
                mm(ps[bnk][:], hT[:, kc, n * 128:(n + 1) * 128], wt[:, kc * 512:(kc + 1) * 512], kc == 0, kc == KC - 1,
                   r=[wk, Hk(kc, n // 4)], w=[PSK[bnk]], sig=(kc == KC - 1))
            sc.op("act", lambda e: e.copy(out=svt[:, n + 1, :], in_=ps[bnk][:]), r=[PSK[bnk]], w=[("svt", n + 1)])

        for h in range(8):
            lb = h // 2
            lcol = (h % 2) * 256
            for half in range(2):
                bnk = vrot.next()
                for n4 in range(4):
                    n = half * 4 + n4
                    mm(ps[bnk][:, n4 * 128:(n4 + 1) * 128], krT[:, h, n * 128:(n + 1) * 128], ident[:], True, True,
                       r=[K_krT(h), "ident"], w=[PSK[bnk]], sig=(n4 == 3))
                kl = kdec[:, 8 + h * 8 + half * 4: 8 + h * 8 + half * 4 + 4]
                sc.op("dve", lambda e: e.tensor_tensor(out=khL, in0=ps[bnk][:].rearrange("p (n d) -> p n d", n=4),
                                                        in1=kl.unsqueeze(2).broadcast_to([P, 4, 128]), op=ALU.mult),
                      r=[PSK[bnk], "kdec"], w=["khL"])
                for n4 in range(4):
                    n = half * 4 + n4
                    mm(ps[lb][:, lcol:lcol + 256], khL[:, n4, :], vtm[:, n, h * 256:(h + 1) * 256], n == 0, n == 7,
                       r=["khL", K_v(n)], w=[PSK[lb]], sig=True)
            if h % 2 == 1:
                sc.op("act", lambda e: e.copy(out=payload[:, lb * 512:(lb + 1) * 512], in_=ps[lb][:]), r=[PSK[lb]], w=["payload"])
        sc.op("act", lambda e: e.copy(out=payload[:, 2048:2560].rearrange("p (j t) -> p j t", j=4), in_=skT[:, :, 1024:1152]),
              r=[("skT", j) for j in range(4)], w=["payload"])
        sc.op("act", lambda e: e.copy(out=payload[:, 2560:3072], in_=svt[:, 8, :]), r=[("svt", 8)], w=["payload"])
        for i4 in range(4):
            for hf in range(2):
                sc.op("dve", lambda e: e.tensor_scalar(out=slot, in0=payload[:, hf * 1536:(hf + 1) * 1536], scalar1=sel[:, i4:i4 + 1],
                                                        scalar2=None, op0=ALU.mult), r=["payload", "sel"], w=["slot"])
                sc.dma("sp", bounce[hf].ap()[i4 * P:(i4 + 1) * P, :], slot, r=["slot"], w=["bounce"], dsem=dq)
        sc.dma("sp", krT_sp, carve(XB, 0, 16384, BF16), r=[K_krT(h) for h in range(8)], w=["krT_sp"], dsem=dq2)
        sc.dma("sp", v_sp, carve(XB, 16384, 32768, BF16), r=[K_v(n) for n in range(8)], w=["v_sp"], dsem=dq2)
        ccsem = stack.enter_context(nc.semaphore("ccsem"))
        ccd = DSem(ccsem, "cc")
        Epool = sc.E["pool"]
        sc._deps(Epool, ["bounce"], ["gath"])
        for hf in range(2):
            nc.gpsimd.collective_compute("AllReduce", ALU.add, replica_groups=[[0, 1, 2, 3], [4, 5, 6, 7]],
                                         ins=[bounce[hf].ap().opt()], outs=[gath[hf].ap().opt()]).then_inc(ccsem)
        ccd.cnt = 2
        sc._record((ccd, 2), ["bounce"], ["gath"])
        sc.barrier()

        lrot = Rot([(0, 1), (2, 3)])
        nrot = Rot([(4, 5), (6, 7)])
        g4 = lambda ap: ap.rearrange("p (g q) -> p g q", g=4)

        prrot = Rot(list(range(NPB)))
        dcnt = [0]

        def swa_steps(blocks, hook_every=0):
            SKEW = 2
            steps = []
            for blk in blocks:
                bnum, bden = nrot.next()
                for s in range(2):
                    steps.append((blk, s, bnum, bden))
            st = {}

            def front(i):
                (j, n, qsrc, qkeys, mprev, mkeys), s, bnum, bden = steps[i]
                r0, r1 = s * 64, (s + 1) * 64
                pb = prrot.next()
                blg = lrot.next()
                st[i] = pb
                for kb in range(2):
                    mm(g4(ps[blg[kb]][:]), skT[r0:r1, j, (n + kb) * 128:(n + kb + 1) * 128], qsrc[r0:r1],
                       True, True, r=[("skT", j)] + qkeys, w=[PSK[blg[kb]]], sig=True)
                    sc.op("act", lambda e: e.activation(out=ebuf[kb], in_=ps[blg[kb]][:], func=AF.Exp, scale=0.125),
                          r=[PSK[blg[kb]]], w=[("ebuf", kb)])
                    mk = mprev if kb == 0 else swamask[:, 128:256]
                    sc.op("dve" if kb == 0 else "pool",
                          lambda e: e.tensor_tensor(out=g4(probs[pb][:, kb, :]), in0=g4(ebuf[kb]),
                                                    in1=mk.unsqueeze(1).broadcast_to([P, 4, 128]), op=ALU.mult),
                          r=[("ebuf", kb), "swamask"] + mkeys, w=[("probs", pb, kb)])

            def back(i):
                (j, n, qsrc, qkeys, mprev, mkeys), s, bnum, bden = steps[i]
                r0, r1 = s * 64, (s + 1) * 64
                pb = st.pop(i)
                hk = 2 * j + s
                for kb in range(2):
                    mm(ps[bnum][r0:r1, :], svt[:, n + kb, hk * 64:(hk + 1) * 64], probs[pb][:, kb, :], kb == 0, kb == 1,
                       r=[("svt", n + kb), ("probs", pb, kb)], w=[PSK[bnum]], sig=(kb == 1))
                for kb in range(2):
                    mm(ps[bden][r0:r1, :], ones64[:], probs[pb][:, kb, :], kb == 0, kb == 1,
                       r=["ones64", ("probs", pb, kb)], w=[PSK[bden]], sig=(kb == 1))
                if s == 1:
                    di = dcnt[0] % 2
                    dcnt[0] += 1
                    dt_ = dtmp[di]
                    sc.op("dve", lambda e: e.tensor_tensor(out=g4(dt_), in0=g4(ps[bden][:]),
                                                            in1=sinkE[:, j * 4:(j + 1) * 4].unsqueeze(2).broadcast_to([P, 4, 128]), op=ALU.add),
                          r=[PSK[bden], "sinkE"], w=[("dtmp", di)])
                    sc.op("dve", lambda e: e.reciprocal(out=dt_, in_=dt_), r=[("dtmp", di)], w=[("dtmp", di)])
                    sc.op("dve", lambda e: e.tensor_tensor(out=soT[:, j * 4:(j + 1) * 4, n * 128:(n + 1) * 128],
                                                            in0=g4(ps[bnum][:]), in1=g4(dt_), op=ALU.mult),
                          r=[PSK[bnum], ("dtmp", di)], w=[("so", j * 4 + g) for g in range(4)])

            N = len(steps)
            nh = 0
            for i in range(N + SKEW):
                if i < N:
                    front(i)
                if i - SKEW >= 0:
                    back(i - SKEW)
                if hook_every and i % hook_every == hook_every - 1 and nh < 3:
                    nh += 1
                    bk = vrot.next()
                    ada_tile(bk, bk)
            while hook_every and nh < 3:
                nh += 1
                bk = vrot.next()
                ada_tile(bk, bk)

        for j in range(4):
            wt, wk = wnext()
            for g in range(4):
                for t in range(2):
                    bnk = vrot.next()
                    proj_fm(wt, wk, g, 4, t, bnk)
                    sc.op("act", lambda e: e.copy(out=sqT[:, g, t * TH:(t + 1) * TH], in_=ps[bnk][:]), r=[PSK[bnk]], w=[("sqT", g)])
            sc.op("pool", lambda e: e.tensor_copy(out=sq0[:, j * 4:(j + 1) * 4, :], in_=sqT[:, :, 0:128]),
                  r=[("sqT", g) for g in range(4)], w=[("sq0", j)])
            swa_steps([(j, n, sqT[:, :, n * 128:(n + 1) * 128], [("sqT", g) for g in range(4)], swamask[:, 0:128], []) for n in range(1, 8)],
                      hook_every=4)

        first = {}

        def acc(eng, dst, src, scal, key, rkeys):
            if key not in first:
                first[key] = 1
                sc.op(eng, lambda e: e.tensor_scalar(out=dst, in0=src, scalar1=scal, scalar2=None, op0=ALU.mult),
                      r=["slot"] + rkeys, w=[key])
            else:
                sc.op(eng, lambda e: e.scalar_tensor_tensor(out=dst, in0=src, scalar=scal, in1=dst, op0=ALU.mult, op1=ALU.add),
                      r=["slot", key] + rkeys, w=[key])

        for i4 in range(4):
            for hf in range(2):
                sc.dma("sp", slot, gath[hf].ap()[i4 * P:(i4 + 1) * P, :], r=["gath"], w=["slot"], dsem=dq)
                heads = range(6) if hf == 0 else range(6, 8)
                for h in heads:
                    o = h * 256 - hf * 1536
                    acc("dve", Sst[:, h, :], slot[:, o:o + 256], coef[:, i4 * 8 + h:i4 * 8 + h + 1], ("Sst", h), ["coef"])
                if hf == 1:
                    acc("dve", halo, slot[:, 512:1536], sel[:, 4 + i4:5 + i4], "halo", ["sel"])
        sc.op("act", lambda e: e.copy(out=skT[:, :, 0:128], in_=halo[:, 0:512].rearrange("p (j t) -> p j t", j=4)),
              r=["halo"], w=[("skT", j) for j in range(4)])
        sc.op("act", lambda e: e.copy(out=svt[:, 0, :], in_=halo[:, 512:1024]), r=["halo"], w=[("svt", 0)])
        for h in range(8):
            sc.op("act", lambda e: e.copy(out=Sbf[:, h, :], in_=Sst[:, h, :]), r=[("Sst", h)], w=[("Sbf", h)])
        sc.op("dve", lambda e: e.tensor_scalar(out=maskp0, in0=swamask[:, 0:128], scalar1=sel[:, 8:9], scalar2=None, op0=ALU.mult),
              r=["swamask", "sel"], w=["maskp0"])
        swa_steps([(j, 0, sq0[:, j * 4:(j + 1) * 4, :], [("sq0", j)], maskp0, ["maskp0"]) for j in range(4)])
        debug_dump("soT", carve(XB, 0, 32768, BF16), [P, KC * T], [("so", k) for k in range(KC)])
        sc.barrier()

        def gated_branch(src, skeys, accumulate):
            grot = Rot([(0, 1), (2, 3), (4, 5), (6, 7)])
            for mp in range(8):
                wt, wk = wnext()
                for m2 in range(2):
                    mmi = mp * 2 + m2
                    for t in range(2):
                        bg, by = grot.next()
                        proj_fm(wt, wk, m2, 4, t, bg)
                        for kc in range(KC):
                            o = kc * 512 + (2 + m2) * 128
                            mm(ps[by][:], wt[:, o:o + 128], src[:, kc, t * TH:(t + 1) * TH], kc == 0, kc == KC - 1,
                               r=[wk, skeys(kc)], w=[PSK[by]], sig=(kc == KC - 1))
                        s = t
                        sc.op("act", lambda e: e.activation(out=rt[s], in_=ps[bg][:], func=AF.Sigmoid), r=[PSK[bg]], w=[("rt", s)])
                        zs = zT[:, mmi, t * TH:(t + 1) * TH]
                        if not accumulate:
                            sc.op("dve", lambda e: e.tensor_tensor(out=zs, in0=rt[s], in1=ps[by][:], op=ALU.mult),
                                  r=[("rt", s), PSK[by]], w=[("z", mmi)])
                        else:
                            sc.op("dve", lambda e: e.tensor_tensor(out=rt[s], in0=rt[s], in1=ps[by][:], op=ALU.mult),
                                  r=[("rt", s), PSK[by]], w=[("rt", s)])
                            sc.op("pool", lambda e: e.tensor_tensor(out=zs, in0=rt[s], in1=zs, op=ALU.add),
                                  r=[("rt", s), ("z", mmi)], w=[("z", mmi)])

        gated_branch(soT, lambda kc: ("so", kc), False)
        sc.barrier()

        gam = [1.0 - 2.0 ** (-5.0 - h) for h in range(8)]
        for h in range(8):
            wq, wqk = wnext()
            wg, wgk = wq, wqk
            if True:
                sc.dma("sp", krh, krT_sp[:, h * T:(h + 1) * T], r=["krT_sp"], w=["krh"], dsem=dq)
                sc.dma("sp", vh, v_sp.rearrange("p (n c) -> p n c", n=8)[:, :, h * 256:(h + 1) * 256], r=["v_sp"], w=["vh"], dsem=dq2)
                for t in range(2):
                    bn, bs_ = prot.next()
                    proj_fm(wq, wqk, 0, 4, t, bn)
                    proj_fm(wq, wqk, 1, 4, t, bs_)
                    rotary_from(bn, bs_, t, qr[:, t * TH:(t + 1) * TH], ["qr"])
                    sc.op("pool", lambda e: e.tensor_tensor(out=qd[:, t * TH:(t + 1) * TH].rearrange("p (n i) -> p n i", n=4),
                                                             in0=qr[:, t * TH:(t + 1) * TH].rearrange("p (n i) -> p n i", n=4),
                                                             in1=qdec[:, h * 128:(h + 1) * 128].unsqueeze(1).broadcast_to([P, 4, 128]),
                                                             op=ALU.mult), r=["qr", "qdec"], w=["qd"])
                khh = kh[h % 2]
                for half in range(2):
                    bnk = vrot.next()
                    for n4 in range(4):
                        n = half * 4 + n4
                        mm(ps[bnk][:, n4 * 128:(n4 + 1) * 128], krh[:, n * 128:(n + 1) * 128], ident[:], True, True,
                           r=["krh", "ident"], w=[PSK[bnk]], sig=(n4 == 3))
                    sc.op("act", lambda e: e.activation(out=khh[:, half * 4:(half + 1) * 4, :],
                                                        in_=ps[bnk][:].rearrange("p (n d) -> p n d", n=4), func=AF.Identity,
                                                        scale=kdec[:, h:h + 1]), r=[PSK[bnk], "kdec"], w=[("kh", h % 2)])
                g128 = gam[h] ** 128

                def emit_sc(n):
                    b_sc = vrot.next()
                    mm(ps[b_sc][:, 0:128], krh[:, n * 128:(n + 1) * 128], qr[:, n * 128:(n + 1) * 128], True, True,
                       r=["krh", "qr"], w=[PSK[b_sc]], sig=True)
                    sb_ = scb[n % 2]
                    sc.op("dve", lambda e: e.tensor_tensor(out=sb_, in0=ps[b_sc][:, 0:128], in1=retmask[:, h * 128:(h + 1) * 128], op=ALU.mult),
                          r=[PSK[b_sc], "retmask"], w=[("scb", n % 2)])

                emit_sc(0)
                for n in range(8):
                    b_kv = None
                    if n < 7:
                        b_kv = vrot.next()
                        mm(ps[b_kv][:, 0:256], khh[:, n, :], vh[:, n, :], True, True,
                           r=[("kh", h % 2), "vh"], w=[PSK[b_kv]], sig=True)
                    sb_ = scb[n % 2]
                    b_o = vrot.next()
                    for eh in range(2):
                        mm(ps[b_o][:, eh * 128:(eh + 1) * 128], vh[:, n, eh * 128:(eh + 1) * 128], sb_, True, False,
                           r=["vh", ("scb", n % 2)], w=[PSK[b_o]], sig=False)
                        mm(ps[b_o][:, eh * 128:(eh + 1) * 128], Sbf[:, h, eh * 128:(eh + 1) * 128], qd[:, n * 128:(n + 1) * 128], False, True,
                           r=[("Sbf", h), "qd"], w=[PSK[b_o]], sig=True)
                    if n < 7:
                        emit_sc(n + 1)
                        sc.op("dve", lambda e: e.scalar_tensor_tensor(out=Sst[:, h, :], in0=Sst[:, h, :], scalar=float(g128), in1=ps[b_kv][:, 0:256],
                                                                       op0=ALU.mult, op1=ALU.add), r=[("Sst", h), PSK[b_kv]], w=[("Sst", h)])
                        sc.op("act", lambda e: e.copy(out=Sbf[:, h, :], in_=Sst[:, h, :]), r=[("Sst", h)], w=[("Sbf", h)])
                    sc.op("act", lambda e: e.copy(out=ro[:, :, n * 128:(n + 1) * 128], in_=ps[b_o][:, 0:256].rearrange("p (e q) -> p e q", e=2)),
                          r=[PSK[b_o]], w=["ro"])
                rms_stats(lambda eh: ro[:, eh, :], 2, ones256, lambda eh: "ro", (0, 1))
                for eh in range(2):
                    for t in range(2):
                        bnk = vrot.next()
                        proj_fm(wg, wgk, 2 + eh, 4, t, bnk)
                        s = t
                        sc.op("act", lambda e: e.activation(out=sgr[s], in_=ps[bnk][:], func=AF.Silu), r=[PSK[bnk]], w=[("sgr", s)])
                        sc.op("dve", lambda e: e.tensor_tensor(out=sgr[s], in0=sgr[s], in1=rstd[:, t * TH:(t + 1) * TH], op=ALU.mult),
                              r=[("sgr", s), "rstd"], w=[("sgr", s)])
                        sc.op("pool", lambda e: e.tensor_tensor(out=roG[:, h * 2 + eh, t * TH:(t + 1) * TH], in0=sgr[s],
                                                                 in1=ro[:, eh, t * TH:(t + 1) * TH], op=ALU.mult),
                              r=[("sgr", s), "ro"], w=[("roG", h * 2 + eh)])
        debug_dump("roG", carve(XB, 0, 32768, BF16), [P, KC * T], [("roG", k) for k in range(KC)])
        debug_dump("krh", krh, [P, T], ["krh"])
        debug_dump("qr", qr, [P, T], ["qr"])
        debug_dump("qd", qd, [P, T], ["qd"])
        debug_dump("vh", carve(XB, 53248, 4096, BF16), [P, 2048], ["vh"])
        debug_dump("ro", carve(XB, 32768, 8192, F32), [P, 2048], ["ro"])
        debug_dump("Sst", carve(XB, 45056, 8192, F32), [P, 2048], [("Sst", h) for h in range(8)])
        debug_dump("kh", carve(XB, 57344 + 2048, 2048, BF16), [P, 1024], [("kh", 1)])
        sc.barrier()
        gated_branch(roG, lambda kc: ("roG", kc), True)
        sc.barrier()
        sc.dma("sp", xT[:].rearrange("p a b -> p (a b)"), xspill, r=["xspill"], w=[Xk(k) for k in range(KC)], dsem=dq)
        orot = Rot([0, 1, 2, 3, 4, 5, 6, 7])
        for mg in range(4):
            wt, wk = wnext()
            for m in range(4):
                mmi = mg * 4 + m
                for t in range(2):
                    bnk = orot.next()
                    for kc in range(KC):
                        o = kc * 512 + m * 128
                        mm(ps[bnk][:], wt[:, o:o + 128], zT[:, kc, t * TH:(t + 1) * TH], kc == 0, kc == KC - 1,
                           r=[wk, ("z", kc)], w=[PSK[bnk]], sig=(kc == KC - 1))
                    xs = xT[:, mmi, t * TH:(t + 1) * TH]
                    sc.op("dve", lambda e: e.scalar_tensor_tensor(out=xs, in0=ps[bnk][:], scalar=colG(1, mmi), in1=xs,
                                                                   op0=ALU.mult, op1=ALU.add), r=[PSK[bnk], ("cols", 1, "G"), Xk(mmi)], w=[Xk(mmi)])
        sc.barrier()

    if "mixer" in stages:
        mixer()
    debug_dump("x2", xT[:].rearrange("p a b -> p (a b)"), [P, KC * T], [Xk(k) for k in range(KC)])
    assert wpos[0] == W_FFN2, (wpos[0], W_FFN2)
    if "ffn2" in stages:
        run_ffn(2)

    rms_stats(lambda kc: xT[:, kc, :], KC, onesD, Xk, (0, 1))
    dout = [sc.new_dsem("out%d" % i) for i in range(2)]
    for kc in range(KC):
        tm = tmpf[kc % 2]
        sc.op("dve", lambda e: e.scalar_tensor_tensor(out=tm[:], in0=xT[:, kc, :], scalar=gains[:, 48 + kc:49 + kc], in1=rstd[:],
                                                       op0=ALU.mult, op1=ALU.mult), r=[Xk(kc), "rstd", "gains"], w=[("tmpf", kc % 2)])
        sc.dma("sp", yT_d[:, kc * T:(kc + 1) * T], tm[:], r=[("tmpf", kc % 2)], w=[("yT", kc)], dsem=dout[kc % 2])
    sc.final_wait("sp", [("yT", kc) for kc in range(KC)])
    E = sc.E["sp"]
    for d in dout:
        sc._wait(E, (d, d.cnt))
    for name, d in dbg_out.items():
        sc._wait(E, (d, d.cnt))
    assert wpos[0] == len(seq), (wpos[0], len(seq))
    stack.close()
    return nc


def _tile_lhs(w, blocks):
    nb = len(blocks)
    cols = np.concatenate(blocks)
    sub = w[:, cols].reshape(KC, P, nb * 128)
    return np.ascontiguousarray(sub.transpose(1, 0, 2)).reshape(P, KC * nb * 128)


def _prep_shared(inp):
    f = lambda a: np.asarray(a, dtype=np.float32)
    sh = {}
    w_ada = f(inp["w_ada"])[0]
    sh["wada"] = np.ascontiguousarray(w_ada.reshape(KC, P, 36, 512).transpose(2, 1, 0, 3)).reshape(36, P, WCOLS)
    for i, (gu, dn) in enumerate((("w_ffn1_gu", "w_ffn1_down"), ("w_ffn2_gu", "w_ffn2_down"))):
        wgu = f(inp[gu])[0]
        tiles = []
        for tt in range(22):
            blocks = []
            for j in (2 * tt, 2 * tt + 1):
                blocks.append(np.arange(j * 128, (j + 1) * 128))
                blocks.append(DFF + np.arange(j * 128, (j + 1) * 128))
            tiles.append(_tile_lhs(wgu, blocks))
        sh["wgu%d" % (i + 1)] = np.stack(tiles)
        wd = f(inp[dn])[0]
        dt = np.zeros((NCH * 4, P, MAXNJ * 512), np.float32)
        jb = 0
        for c, nj in enumerate(CHUNKS):
            rows = wd[jb * 128:(jb + nj) * 128].reshape(nj, P, 4, 512)
            for mg in range(4):
                dt[c * 4 + mg, :, :nj * 512] = rows[:, :, mg, :].transpose(1, 0, 2).reshape(P, nj * 512)
            jb += nj
        sh["wd%d" % (i + 1)] = dt
    w_in = f(inp["w_in"])[0]
    o_rq, o_rk, o_rv, o_rg, o_sq, o_sk, o_sv, o_gr, o_gs = 0, 1024, 2048, 4096, 6144, 8192, 8704, 9216, 11264
    sw = np.concatenate([np.arange(64, 128), np.arange(0, 64)])
    win = []

    def qk_tiles(off):
        out = []
        for tl in range(4):
            blocks = []
            for hh in range(2):
                h = tl * 2 + hh
                base = off + h * 128
                blocks.append(base + np.arange(128))
                blocks.append(base + sw)
            out.append(_tile_lhs(w_in, blocks))
        return out

    def rhs_tiles(off, ncols):
        out = []
        for tl in range(ncols // 512):
            sub = w_in[:, off + tl * 512: off + (tl + 1) * 512].reshape(KC, P, 512)
            out.append(np.ascontiguousarray(sub.transpose(1, 0, 2)).reshape(P, WCOLS))
        return out

    win += qk_tiles(o_rk)
    win += rhs_tiles(o_rv, 2048)
    win.append(_tile_lhs(w_in, [o_sk + j * 128 + np.arange(128) for j in range(4)]))
    win += rhs_tiles(o_sv, 512)
    for j in range(4):
        blocks = []
        for g in range(4):
            ha, hb = 8 * j + g, 8 * j + 4 + g
            blocks.append(np.concatenate([o_sq + ha * 64 + np.arange(64), o_sq + hb * 64 + np.arange(64)]))
        win.append(_tile_lhs(w_in, blocks))
    for h in range(8):
        win.append(_tile_lhs(w_in, [o_rq + h * 128 + np.arange(128), o_rq + h * 128 + sw,
                                    o_rg + (2 * h) * 128 + np.arange(128), o_rg + (2 * h + 1) * 128 + np.arange(128)]))
    sh["win"] = np.stack(win)
    assert sh["win"].shape[0] == 22
    w_swa = f(inp["w_swa_branch"])[0]
    perm = []
    for j in range(4):
        for g in range(4):
            for s in range(2):
                perm.append((8 * j + 4 * s + g) * 64 + np.arange(64))
    perm = np.concatenate(perm)
    w_swa_p = w_swa[perm]
    w_ret = f(inp["w_ret_branch"])[0]
    w_out = f(inp["w_out"])[0]
    wgb = []
    for goff, wmat in ((o_gs, w_swa_p), (o_gr, w_ret)):
        for mp in range(8):
            A = _tile_lhs(w_in, [goff + (2 * mp + b) * 128 + np.arange(128) for b in range(2)]).reshape(P, KC, 256)
            B = _tile_lhs(wmat, [(2 * mp + b) * 128 + np.arange(128) for b in range(2)]).reshape(P, KC, 256)
            wgb.append(np.concatenate([A, B], axis=2).reshape(P, WCOLS))
    sh["wgb"] = np.stack(wgb)
    sh["wbr"] = np.stack([_tile_lhs(w_out, [(mg * 4 + b) * 128 + np.arange(128) for b in range(4)]) for mg in range(4)])
    gam = np.array([1.0 - 2.0 ** (-5.0 - h) for h in range(8)], np.float64)
    i = np.arange(128)
    qdec = np.stack([gam[h] ** (i + 1.0) for h in range(8)]).reshape(1, 1024)
    sh["qdec"] = np.broadcast_to(qdec, (P, 1024)).astype(np.float32).copy()
    k_ = i[:, None]
    q_ = i[None, :]
    rm = np.concatenate([np.where(q_ >= k_, gam[h] ** np.maximum(q_ - k_, 0) * (128.0 ** -0.5), 0.0) for h in range(8)], axis=1)
    sh["retmask"] = rm.astype(np.float32)
    kd = np.zeros((P, 72), np.float64)
    for h in range(8):
        kd[:, h] = (128.0 ** -0.5) * gam[h] ** (127.0 - i)
        for n in range(8):
            kd[:, 8 + h * 8 + n] = (128.0 ** -0.5) * gam[h] ** (1023.0 - 128.0 * n - i)
    sh["kdec"] = kd.astype(np.float32)
    sm = np.zeros((P, 256), np.float32)
    sm[:, 0:128] = (k_ > q_)
    sm[:, 128:256] = (k_ <= q_)
    sh["swamask"] = sm
    sh["ident"] = np.eye(P, dtype=np.float32)
    gl = np.zeros((P, 64), np.float32)
    for c, nm in enumerate(("norm_ffn1", "norm_mix", "norm_ffn2")):
        gl[:, c * 16:(c + 1) * 16] = f(inp[nm])[0].reshape(KC, P).T
    gl[:, 48:64] = f(inp["norm_final"]).reshape(KC, P).T
    sh["gains"] = gl
    sk = f(inp["sinks"])[0]
    sl = np.zeros((P, 16), np.float32)
    for j in range(4):
        for g in range(4):
            sl[0:64, j * 4 + g] = sk[8 * j + g]
            sl[64:128, j * 4 + g] = sk[8 * j + 4 + g]
    sh["sinksl"] = sl
    return sh, gam


def _prep_core(inp, r, gam):
    f = lambda a: np.asarray(a, dtype=np.float32)
    b, q = r // 4, r % 4
    m = {}
    xs = f(inp["x"])[b, q * T:(q + 1) * T]
    m["xT"] = np.ascontiguousarray(xs.reshape(T, KC, P).transpose(2, 1, 0)).reshape(P, KC * T)
    m["cT"] = np.ascontiguousarray(f(inp["c"])[b].reshape(KC, P).T)
    m["badaT"] = np.ascontiguousarray(f(inp["b_ada"])[0].reshape(144, P).T)
    half = 64
    inv_freq = (1.0 / (np.float32(10000.0) ** (np.arange(half, dtype=np.float32) / np.float32(half)))).astype(np.float32)
    pos = np.arange(q * T, (q + 1) * T, dtype=np.float32)
    ang = (pos[None, :] * inv_freq[:, None]).astype(np.float32)
    cos = np.cos(ang).astype(np.float32)
    sin = np.sin(ang).astype(np.float32)
    rope = np.zeros((P, 2 * T), np.float32)
    rope[0:64, 0:T] = cos
    rope[64:128, 0:T] = cos
    rope[0:64, T:] = -sin
    rope[64:128, T:] = sin
    m["rope"] = rope
    sel = np.zeros((P, 16), np.float32)
    sel[:, q] = 1.0
    if q > 0:
        sel[:, 4 + q - 1] = 1.0
        sel[:, 8] = 1.0
    m["sel"] = sel
    cf = np.zeros((P, 32), np.float64)
    for i4 in range(q):
        for h in range(8):
            cf[:, i4 * 8 + h] = gam[h] ** (128.0 * 8 * (q - 1 - i4))
    m["coef"] = cf.astype(np.float32)
    return m


_CACHE = {}


def kernel(**inputs):
    sh, gam = _prep_shared(inputs)
    if "nc" not in _CACHE:
        _CACHE["nc"] = build_program()
    nc = _CACHE["nc"]
    in_maps = []
    for r in range(NCORES):
        m = dict(sh)
        m.update(_prep_core(inputs, r, gam))
        in_maps.append(m)
    res = run_bass_kernel_spmd(nc, in_maps, core_ids=list(range(NCORES)))
    B, S = 2, 4096
    out = np.zeros((B, S, D), np.float32)
    for r in range(NCORES):
        b, q = r // 4, r % 4
        yT = np.asarray(res.results[r]["yT"]).reshape(P, KC, T)
        out[b, q * T:(q + 1) * T, :] = yT.transpose(2, 1, 0).reshape(T, D)
    return out
```

```python
import numpy as np
import concourse.bass as bass
import concourse.mybir as mybir
from concourse.bass_utils import run_bass_kernel_spmd

F32 = mybir.dt.float32
BF16 = mybir.dt.bfloat16
AF = mybir.ActivationFunctionType
ALU = mybir.AluOpType

NCORES = 8
P = 128
D = 2048
KC = 16
T = 1024
TH = 512
DFF = 5632
NJ = 44
CHUNKS = [6, 6, 6, 4, 6, 6, 6, 4]
NCH = len(CHUNKS)
MAXNJ = max(CHUNKS)
EPS = 1e-6
WCOLS = 8192
NSLOT = 2
PF = 1


class Eng:
    def __init__(self, name, h, sem):
        self.name = name
        self.h = h
        self.sem = sem
        self.cnt = 0
        self.seen = {}
        self.pending = False


class DSem:
    def __init__(self, sem, name):
        self.sem = sem
        self.cnt = 0
        self.name = name


class Sched:
    def __init__(self, nc, stack):
        self.nc = nc
        self.stack = stack
        self.E = {}
        for name, h in (("pe", nc.tensor), ("act", nc.scalar), ("dve", nc.vector),
                        ("pool", nc.gpsimd), ("sp", nc.sync)):
            sem = stack.enter_context(nc.semaphore("sem_" + name))
            self.E[name] = Eng(name, h, sem)
        self.res_w = {}
        self.res_r = {}
        self.dsems = []

    def new_dsem(self, name):
        d = DSem(self.stack.enter_context(self.nc.semaphore("dsem_" + name)), name)
        self.dsems.append(d)
        return d

    def _wait(self, E, tok):
        S, v = tok
        if S is E and E.name == "pe":
            return
        if isinstance(S, Eng) and S.name == "pe" and S.pending and v > S.cnt:
            raise RuntimeError("wait on unsignaled PE op")
        if E.seen.get(S, 0) < v:
            E.h.wait_ge(S.sem, v)
            E.seen[S] = v

    def _deps(self, E, r, w):
        toks = []
        for k in r:
            t = self.res_w.get(k)
            if t is not None:
                toks.append(t)
        for k in w:
            t = self.res_w.get(k)
            if t is not None:
                toks.append(t)
            for S, v in self.res_r.get(k, {}).items():
                toks.append((S, v))
        for t in toks:
            self._wait(E, t)

    def _record(self, tok, r, w):
        S, v = tok
        for k in r:
            d = self.res_r.setdefault(k, {})
            if d.get(S, 0) < v:
                d[S] = v
        for k in w:
            self.res_w[k] = tok
            self.res_r[k] = {}

    def op(self, en, fn, r=(), w=(), sig=True):
        E = self.E[en]
        self._deps(E, r, w)
        ins = fn(E.h)
        if sig:
            E.cnt += 1
            ins.then_inc(E.sem, 1)
            E.pending = False
            tok = (E, E.cnt)
        else:
            E.pending = True
            tok = (E, E.cnt + 1)
        self._record(tok, r, w)
        return ins

    def dma(self, en, out, in_, r=(), w=(), dsem=None):
        E = self.E[en]
        self._deps(E, r, w)
        self._wait(E, (dsem, dsem.cnt))
        ins = E.h.dma_start(out=out, in_=in_)
        dsem.cnt += 16
        ins.then_inc(dsem.sem, 16)
        self._record((dsem, dsem.cnt), r, w)
        return ins

    def barrier(self):
        assert not self.E["pe"].pending
        for E in self.E.values():
            for F in self.E.values():
                if F is not E and F.cnt > 0:
                    self._wait(E, (F, F.cnt))
            for d in self.dsems:
                if d.cnt > 0:
                    self._wait(E, (d, d.cnt))

    def final_wait(self, en, keys):
        E = self.E[en]
        self._deps(E, keys, ())


class WStream:
    def __init__(self, sc, slots, nbase, seq):
        self.sc = sc
        self.slots = slots
        self.nbase = nbase
        self.seq = seq
        self.issued = 0
        self.pf = 1
        self.dsems = [sc.new_dsem("w%d" % i) for i in range(len(slots))]
        self.slot_tile = [-1] * len(slots)
        self.slot_of = {}

    def _try_issue(self, k, cur):
        ap, allow_extra = self.seq[k]
        cand = list(range(len(self.slots) if allow_extra else self.nbase))
        free = [c for c in cand if self.slot_tile[c] < cur]
        if not free:
            return False
        s = min(free, key=lambda c: self.slot_tile[c])
        ncols = ap.shape[-1]
        self.sc.dma("pool", self.slots[s][:, 0:ncols], ap, w=[("wslot", s)], dsem=self.dsems[s])
        self.slot_tile[s] = k
        self.slot_of[k] = s
        return True

    def get(self, i, pf=None):
        pf = self.pf if pf is None else pf
        while self.issued < min(i + pf + 1, len(self.seq)):
            if not self._try_issue(self.issued, i):
                assert self.issued > i, "no slot for the tile being consumed"
                break
            self.issued += 1
        s = self.slot_of[i]
        return self.slots[s], ("wslot", s)


class Rot:
    def __init__(self, items):
        self.items = list(items)
        self.i = 0

    def next(self):
        v = self.items[self.i % len(self.items)]
        self.i += 1
        return v


def build_program(dbg_names=(), stages=("ffn1", "mixer", "ffn2")):
    from contextlib import ExitStack
    nc = bass.Bass("TRN2", target_bir_lowering=False)
    stack = ExitStack()
    sc = Sched(nc, stack)

    def din(name, shape, dt=F32):
        return nc.dram_tensor(name, list(shape), dt, kind="ExternalInput").ap()

    xT_d = din("xT", [P, KC * T])
    cT_d = din("cT", [P, KC])
    badaT_d = din("badaT", [P, 144])
    gains_d = din("gains", [P, 64])
    rope_d = din("rope", [P, 2 * T])
    qdec_d = din("qdec", [P, 8 * 128])
    retmask_d = din("retmask", [P, 8 * 128])
    kdec_d = din("kdec", [P, 8 + 64])
    swamask_d = din("swamask", [P, 256])
    sel_d = din("sel", [P, 16])
    coef_d = din("coef", [P, 32])
    sinks_d = din("sinksl", [P, 16])
    ident_d = din("ident", [P, P])
    wada_d = din("wada", [36, P, WCOLS])
    wgu_d = [din("wgu1", [22, P, WCOLS]), din("wgu2", [22, P, WCOLS])]
    wd_d = [din("wd1", [NCH * 4, P, MAXNJ * 512]), din("wd2", [NCH * 4, P, MAXNJ * 512])]
    NWIN = 22
    win_d = din("win", [NWIN, P, WCOLS])
    wgb_d = din("wgb", [16, P, WCOLS])
    wbr_d = din("wbr", [4, P, WCOLS])
    yT_d = nc.dram_tensor("yT", [P, KC * T], F32, kind="ExternalOutput").ap()
    xspill = nc.dram_tensor("xspill", [P, KC * T], F32).ap()
    bounce = [nc.dram_tensor("bounce%d" % i, [4 * P, 1536], F32) for i in range(2)]
    gath = [nc.dram_tensor("gath%d" % i, [4 * P, 1536], F32) for i in range(2)]
    dbg_out = {}

    def sb(name, shape, dt=F32):
        return stack.enter_context(nc.sbuf_tensor(name, list(shape), dt))

    xT = sb("xT_sb", [P, KC, T])
    hT = sb("hT", [P, KC, T], BF16)
    wslots = [sb("wslot%d" % i, [P, WCOLS], BF16) for i in range(NSLOT)]
    modT = sb("modT", [P, 144])
    cols = sb("cols", [P, 9 * 16])
    gains = sb("gains_sb", [P, 64])
    rstd = sb("rstd", [P, T])
    tmpf = [sb("tmpf%d" % i, [P, T]) for i in range(2)]
    sqb = [sb("sqb%d" % i, [P, T], BF16) for i in range(2)]
    onesD = sb("onesD", [P, P], BF16)
    ident = sb("ident_sb", [P, P], BF16)
    YBYTES = 57344
    yar = sb("yar", [P, YBYTES // 2], BF16)
    XB = xT[:].rearrange("p a b -> p (a b)").bitcast(BF16)
    YB = yar[:]

    def carve(arena, off, nbytes, dt):
        assert off % 4 == 0 and nbytes % 4 == 0
        v = arena[:, off // 2:(off + nbytes) // 2]
        return v if dt == BF16 else v.bitcast(F32)

    ps = [stack.enter_context(nc.psum_tensor("ps%d" % i, [P, 512], F32)) for i in range(8)]
    PSK = [("ps", i) for i in range(8)]
    dq = sc.new_dsem("io")
    dq2 = sc.new_dsem("io2")

    def Xk(kc):
        return ("x", kc)

    def Hk(kc, t):
        return ("h", kc, t)

    seq = []
    NADA0 = 8
    for g in range(NADA0):
        seq.append((wada_d[g], True))

    def ffn_seq(f):
        gu_t = 0
        plan = []
        order = [("gu", 0)]
        for c in range(1, NCH):
            order += [("gu", c), ("d", c - 1)]
        order.append(("d", NCH - 1))
        for kind, c in order:
            if kind == "gu":
                for _ in range(CHUNKS[c] // 2):
                    plan.append(wgu_d[f][gu_t])
                    gu_t += 1
            else:
                for mg in range(4):
                    plan.append(wd_d[f][c * 4 + mg][:, 0:CHUNKS[c] * 512])
        return plan

    if "ffn1" in stages:
        for idx, tl_ in enumerate(ffn_seq(0)):
            seq.append((tl_, True))
            if idx < 24 - NADA0:
                seq.append((wada_d[NADA0 + idx], True))
    else:
        for g in range(NADA0, 24):
            seq.append((wada_d[g], True))
    if "mixer" not in stages:
        for g in range(24, 36):
            seq.append((wada_d[g], True))
    wi = 0

    def win_next(n=1):
        nonlocal wi
        out = [(win_d[wi + k], False) for k in range(n)]
        wi += n
        return out

    if "mixer" in stages:
        seq += win_next(4 + 4 + 1 + 1)
        for j in range(4):
            seq += win_next(1)
            for q3 in range(3):
                seq.append((wada_d[24 + j * 3 + q3], False))
        for mp in range(8):
            seq.append((wgb_d[mp], False))
        seq += win_next(8)
        for mp in range(8):
            seq.append((wgb_d[8 + mp], False))
        for mg in range(4):
            seq.append((wbr_d[mg], False))
        assert wi == NWIN
    W_FFN2 = len(seq)
    if "ffn2" in stages:
        for idx, tl_ in enumerate(ffn_seq(1)):
            seq.append((tl_, idx >= 3))
    yslot = carve(YB, 32768, 16384, BF16)
    ws = WStream(sc, wslots + [yslot], NSLOT, seq)
    ws.pf = 2
    wpos = [0]

    def wnext(pf=None):
        t, k = ws.get(wpos[0], pf)
        wpos[0] += 1
        return t, k

    def mm(out, lhsT, rhs, start, stop, r, w, sig):
        sc.op("pe", lambda e: e.matmul(out, lhsT=lhsT, rhs=rhs, start=start, stop=stop), r=r, w=w, sig=sig)

    def debug_dump(name, ap, shape, keys):
        if name not in dbg_names:
            return
        d = nc.dram_tensor("dbg_" + name, list(shape), ap.dtype, kind="ExternalOutput").ap()
        ds = sc.new_dsem("dbg_" + name)
        sc.dma("sp", d, ap, r=keys, dsem=ds)
        dbg_out[name] = ds

    sc.dma("sp", xT[:].rearrange("p a b -> p (a b)"), xT_d, w=[Xk(k) for k in range(KC)], dsem=dq)
    small = {}
    yoff = [0]

    def ycarve(nbytes, dt=F32):
        v = carve(YB, yoff[0], nbytes, dt)
        yoff[0] += nbytes
        return v

    for name, d_ap, n in (("cT", cT_d, KC),):
        t = ycarve(n * 4)
        sc.dma("sp", t, d_ap, w=[name], dsem=dq2)
        small[name] = t
    badaT_sb = sb("badaT_sb", [P, 144])
    sc.dma("sp", badaT_sb[:], badaT_d, w=["badaT"], dsem=dq2)
    small["badaT"] = badaT_sb[:]
    sinks_sb = sb("sinks_sb", [P, 16])
    sc.dma("sp", sinks_sb[:], sinks_d, w=["sinks"], dsem=dq2)
    sc.dma("sp", gains[:], gains_d, w=["gains"], dsem=dq2)
    identf = ycarve(P * 4)
    sc.dma("sp", identf, ident_d, w=["identf"], dsem=dq2)
    sc.op("dve", lambda e: e.tensor_copy(out=ident[:], in_=identf), r=["identf"], w=["ident"])
    sc.op("dve", lambda e: e.memset(onesD[:], 1.0 / D), w=["onesD"])
    ones1 = sb("ones1", [1, 1])
    sc.op("dve", lambda e: e.memset(ones1[:], 1.0), w=["ones1"])
    ones64 = sb("ones64", [P, 64], BF16)
    sc.op("dve", lambda e: e.memset(ones64[:], 1.0), w=["ones64"])
    ones256 = sb("ones256", [P, P], BF16)
    sc.op("dve", lambda e: e.memset(ones256[:], 1.0 / 256.0), w=["ones256"])
    cact = sb("cact", [P, KC], BF16)
    sc.op("act", lambda e: e.activation(out=cact[:], in_=small["cT"], func=AF.Silu), r=["cT"], w=["cact"])
    sinkE = sb("sinkE", [P, 16])
    sc.op("act", lambda e: e.activation(out=sinkE[:], in_=sinks_sb[:], func=AF.Exp), r=["sinks"], w=["sinkE"])

    rowseg = [carve(YB, 28672 + i * 2048, 2048, F32)[0:1, :] for i in range(2)]
    MODB = 7
    ada_g = [0]
    ada_nb = [2]

    def derive(i, parts):
        gi, sh, scl, gt, gscale = ((0, 0, 1, 2, 0.5), (16, 3, 4, 5, 1.0), (32, 6, 7, 8, 0.5))[i]
        A = cols[:, (3 * i) * 16:(3 * i + 1) * 16]
        B = cols[:, (3 * i + 1) * 16:(3 * i + 2) * 16]
        G = cols[:, (3 * i + 2) * 16:(3 * i + 3) * 16]
        if "AB" in parts:
            sc.op("dve", lambda e: e.scalar_tensor_tensor(out=A, in0=modT[:, scl * 16:(scl + 1) * 16], scalar=1.0,
                                                           in1=gains[:, gi:gi + 16], op0=ALU.add, op1=ALU.mult),
                  r=["modT", "gains"], w=[("cols", i, "A")])
            sc.op("dve", lambda e: e.tensor_copy(out=B, in_=modT[:, sh * 16:(sh + 1) * 16]), r=["modT"], w=[("cols", i, "B")])
        if "G" in parts:
            sc.op("dve", lambda e: e.tensor_scalar(out=G, in0=modT[:, gt * 16:(gt + 1) * 16], scalar1=gscale, scalar2=None,
                                                    op0=ALU.mult), r=["modT"], w=[("cols", i, "G")])

    def ada_tile(rowbank, colbank):
        g = ada_g[0]
        if g >= 36:
            return
        ada_g[0] += 1
        wt, wk = wnext()
        b = g % ada_nb[0]
        for kc in range(KC):
            mm(ps[rowbank][0:1, :], cact[:, kc:kc + 1], wt[:, kc * 512:(kc + 1) * 512], kc == 0, kc == KC - 1,
               r=[wk, "cact"], w=[PSK[rowbank]], sig=(kc == KC - 1))
        sc.op("act", lambda e: e.copy(out=rowseg[b], in_=ps[rowbank][0:1, :]), r=[PSK[rowbank]], w=[("rowseg", b)])
        for q4 in range(4):
            mm(ps[colbank][:, q4:q4 + 1], rowseg[b][:, q4 * 128:(q4 + 1) * 128], ones1[0:1, 0:1], True, True,
               r=[("rowseg", b), "ones1"], w=[PSK[colbank]], sig=(q4 == 3))
        sc.op("dve", lambda e: e.tensor_tensor(out=modT[:, g * 4:g * 4 + 4], in0=ps[colbank][:, 0:4], in1=small["badaT"][:, g * 4:g * 4 + 4], op=ALU.add),
              r=[PSK[colbank], "badaT"], w=["modT"])
        if g + 1 == 8:
            derive(0, "AB")
        elif g + 1 == 12:
            derive(0, "G")
        elif g + 1 == 24:
            derive(1, "ABG")
        elif g + 1 == 36:
            derive(2, "ABG")

    NADA_FFN1 = 24
    sqh = [sqb[0][:, 0:TH], sqb[0][:, TH:T], sqb[1][:, 0:TH], sqb[1][:, TH:T]]
    tmh = [tmpf[0][:, 0:TH], tmpf[0][:, TH:T], tmpf[1][:, 0:TH], tmpf[1][:, TH:T]]
    nm_cnt = [0]

    def norm_stats(t):
        for kc in range(KC):
            qi = kc % 4
            sc.op("act", lambda e: e.activation(out=sqh[qi], in_=xT[:, kc, t * TH:(t + 1) * TH], func=AF.Square),
                  r=[Xk(kc)], w=[("sqh", qi)])
            mm(ps[t][:], onesD[:], sqh[qi], kc == 0, kc == KC - 1, r=[("sqh", qi), "onesD"], w=[PSK[t]], sig=True)
        rs = rstd[:, t * TH:(t + 1) * TH]
        sc.op("dve", lambda e: e.tensor_scalar(out=rs, in0=ps[t][:], scalar1=EPS, scalar2=None, op0=ALU.add),
              r=[PSK[t]], w=[("rstd", t)])
        sc.op("act", lambda e: e.activation(out=rs, in_=rs, func=AF.Sqrt), r=[("rstd", t)], w=[("rstd", t)])
        sc.op("dve", lambda e: e.reciprocal(out=rs, in_=rs), r=[("rstd", t)], w=[("rstd", t)])

    if "ffn1" in stages:
        norm_stats(0)
        norm_stats(1)
    for g in range(NADA0):
        ada_tile(4 + g % 2, 6 + g % 2)
    if "ffn1" not in stages:
        while ada_g[0] < NADA_FFN1:
            ada_tile(ada_g[0] % 2, 2 + ada_g[0] % 2)
    if "mixer" not in stages:
        while ada_g[0] < 36:
            ada_tile(ada_g[0] % 2, 2 + ada_g[0] % 2)
    debug_dump("modT", modT[:], [P, 144], ["modT"])

    def colA(i, kc):
        return cols[:, (3 * i) * 16 + kc:(3 * i) * 16 + kc + 1]

    def colB(i, kc):
        return cols[:, (3 * i + 1) * 16 + kc:(3 * i + 1) * 16 + kc + 1]

    def colG(i, kc):
        return cols[:, (3 * i + 2) * 16 + kc:(3 * i + 2) * 16 + kc + 1]

    def rms_stats(src, nblk, ones_t, keyf, banks):
        for kc in range(nblk):
            q = sqb[kc % 2]
            sc.op("act", lambda e: e.activation(out=q[:], in_=src(kc), func=AF.Square), r=[keyf(kc)], w=[("sqb", kc % 2)])
            for t in range(2):
                mm(ps[banks[t]][:], ones_t[:], q[:, t * TH:(t + 1) * TH], kc == 0, kc == nblk - 1,
                   r=[("sqb", kc % 2), "onesD", "ones256"], w=[PSK[banks[t]]], sig=True)
        for t in range(2):
            rs = rstd[:, t * TH:(t + 1) * TH]
            sc.op("dve", lambda e: e.tensor_scalar(out=rs, in0=ps[banks[t]][:], scalar1=EPS, scalar2=None, op0=ALU.add),
                  r=[PSK[banks[t]]], w=["rstd"])
            sc.op("act", lambda e: e.activation(out=rs, in_=rs, func=AF.Sqrt), r=["rstd"], w=["rstd"])
            sc.op("dve", lambda e: e.reciprocal(out=rs, in_=rs), r=["rstd"], w=["rstd"])

    def norm_apply(i, t):
        rs = rstd[:, t * TH:(t + 1) * TH]
        for kc in range(KC):
            ti = nm_cnt[0] % 4
            nm_cnt[0] += 1
            sc.op("dve", lambda e: e.scalar_tensor_tensor(out=tmh[ti], in0=xT[:, kc, t * TH:(t + 1) * TH], scalar=colA(i, kc), in1=rs,
                                                           op0=ALU.mult, op1=ALU.mult),
                  r=[Xk(kc), ("rstd", t), ("cols", i, "A")], w=[("tmh", ti)])
            sc.op("act", lambda e: e.activation(out=hT[:, kc, t * TH:(t + 1) * TH], in_=tmh[ti], func=AF.Identity, bias=colB(i, kc), scale=1.0),
                  r=[("tmh", ti), ("cols", i, "B")], w=[Hk(kc, t)])

    def norm_mod(i, stats_done=False):
        for t in range(2):
            if not stats_done:
                norm_stats(t)
            norm_apply(i, t)

    def ffn(i, act, sgt, hook=lambda: None):
        gu_rot = Rot([(0, 1), (2, 3)])
        d_rot = Rot([4, 5])
        sg_rot = Rot([0, 1])
        jbase = [sum(CHUNKS[:c]) for c in range(NCH)]
        cur = {"tile": None, "key": None, "idx": -1}

        def GU(c):
            for jl in range(CHUNKS[c]):
                j = jbase[c] + jl
                if j // 2 != cur["idx"]:
                    if cur["idx"] >= 0:
                        hook()
                    cur["tile"], cur["key"] = wnext()
                    cur["idx"] = j // 2
                wt, wk = cur["tile"], cur["key"]
                wj = j % 2
                for t in range(2):
                    bg, bu = gu_rot.next()
                    for which, bnk in ((0, bg), (1, bu)):
                        for kc in range(KC):
                            o = kc * 512 + wj * 256 + which * 128
                            mm(ps[bnk][:], wt[:, o:o + 128], hT[:, kc, t * TH:(t + 1) * TH], kc == 0, kc == KC - 1,
                               r=[wk, Hk(kc, t)], w=[PSK[bnk]], sig=(kc == KC - 1))
                    s = sg_rot.next()
                    sc.op("act", lambda e: e.activation(out=sgt[s], in_=ps[bg][:], func=AF.Silu), r=[PSK[bg]], w=[("sgt", s)])
                    sc.op("dve", lambda e: e.tensor_tensor(out=act[c % 2][:, jl, t * TH:(t + 1) * TH], in0=sgt[s],
                                                            in1=ps[bu][:], op=ALU.mult),
                          r=[("sgt", s), PSK[bu]], w=[("act", c % 2, jl, t)])

        def DOWN(c):
            nj = CHUNKS[c]
            for mg in range(4):
                hook()
                wt, wk = wnext()
                for m in range(4):
                    mmi = mg * 4 + m
                    for t in range(2):
                        bnk = d_rot.next()
                        for jl in range(nj):
                            o = jl * 512 + m * 128
                            mm(ps[bnk][:], wt[:, o:o + 128], act[c % 2][:, jl, t * TH:(t + 1) * TH], jl == 0, jl == nj - 1,
                               r=[wk, ("act", c % 2, jl, t)], w=[PSK[bnk]], sig=(jl == nj - 1))
                        xs = xT[:, mmi, t * TH:(t + 1) * TH]
                        sc.op("dve", lambda e: e.scalar_tensor_tensor(out=xs, in0=ps[bnk][:], scalar=colG(i, mmi), in1=xs,
                                                                       op0=ALU.mult, op1=ALU.add),
                              r=[PSK[bnk], ("cols", i, "G"), Xk(mmi)], w=[Xk(mmi)])

        GU(0)
        for c in range(1, NCH):
            GU(c)
            DOWN(c - 1)
        DOWN(NCH - 1)
        hook()

    def run_ffn(i):
        sc.barrier()
        norm_mod(i, stats_done=(i == 0))
        a0 = carve(YB, 0, 12288, BF16).rearrange("p (j t) -> p j t", j=MAXNJ)
        a1 = carve(YB, 12288, 12288, BF16).rearrange("p (j t) -> p j t", j=MAXNJ)
        s0 = carve(YB, 24576, 2048, F32)
        s1 = carve(YB, 26624, 2048, F32)
        ws.pf = 2
        if i == 0:
            ffn(i, [a0, a1], [s0, s1], hook=lambda: (ada_tile(6, 7) if ada_g[0] < NADA_FFN1 else None))
            assert ada_g[0] == NADA_FFN1, ada_g[0]
        else:
            ffn(i, [a0, a1], [s0, s1])
        sc.barrier()

    if "ffn1" in stages:
        run_ffn(0)
    debug_dump("x1", xT[:].rearrange("p a b -> p (a b)"), [P, KC * T], [Xk(k) for k in range(KC)])

    def mixer():
        ws.pf = 1
        norm_mod(1)
        sc.dma("sp", xspill, xT[:].rearrange("p a b -> p (a b)"), r=[Xk(k) for k in range(KC)], w=["xspill"], dsem=dq)
        XALL = [Xk(k) for k in range(KC)]
        yo = [0]

        def ytab(d_ap, n, name):
            t = carve(YB, yo[0], n * 4, F32)
            yo[0] += n * 4
            sc.dma("sp", t, d_ap, w=[name], dsem=dq2)
            return t

        rope = ytab(rope_d, 2 * T, "rope")
        qdec = ytab(qdec_d, 1024, "qdec")
        retmask = ytab(retmask_d, 1024, "retmask")
        kdec = ytab(kdec_d, 72, "kdec")
        swamask = ytab(swamask_d, 256, "swamask")
        sel = ytab(sel_d, 16, "sel")
        coef = ytab(coef_d, 32, "coef")
        assert yo[0] <= 18432
        cosT = rope[:, 0:T]
        sinT = rope[:, T:2 * T]
        ZOFF = 18432
        zT = carve(YB, ZOFF, 32768, BF16).rearrange("p (k t) -> p k t", k=KC)
        Sbf = carve(YB, 51200, 4096, BF16).rearrange("p (h e) -> p h e", h=8)
        khL = carve(YB, 55296, 1024, BF16).rearrange("p (n d) -> p n d", n=4)
        maskp0 = carve(YB, 56320, 512, F32)
        scb = [carve(YB, 56832 + i * 256, 256, BF16) for i in range(2)]
        svt = carve(YB, ZOFF, 9216, BF16).rearrange("p (n c) -> p n c", n=9)
        payload = carve(YB, ZOFF + 9216, 12288, F32)
        slot = carve(YB, ZOFF + 21504, 6144, F32)
        halo = carve(YB, ZOFF + 27648, 4096, F32)
        NPB = 3
        probs = [carve(YB, ZOFF + 9216 + i * 2048, 2048, BF16).rearrange("p (k q) -> p k q", k=2) for i in range(NPB)]
        ebuf = [carve(YB, ZOFF + 9216 + 6144 + i * 1024, 1024, BF16) for i in range(2)]
        dtmp = [carve(YB, ZOFF + 9216 + 8192 + i * 2048, 2048, F32) for i in range(2)]
        krT = carve(XB, 0, 16384, BF16).rearrange("p (h t) -> p h t", h=8)
        vtm = carve(XB, 16384, 32768, BF16).rearrange("p (n c) -> p n c", n=8)
        skT = carve(XB, 55296, 9216, BF16).rearrange("p (j t) -> p j t", j=4)
        soT = carve(XB, 0, 32768, BF16).rearrange("p (k t) -> p k t", k=KC)
        sqT = carve(XB, 32768, 8192, BF16).rearrange("p (g t) -> p g t", g=4)
        sq0 = carve(XB, 40960, 4096, BF16).rearrange("p (k q) -> p k q", k=16)
        Sst = carve(XB, 45056, 8192, F32).rearrange("p (h e) -> p h e", h=8)
        roG = carve(XB, 0, 32768, BF16).rearrange("p (k t) -> p k t", k=KC)
        ro = carve(XB, 32768, 8192, F32).rearrange("p (e t) -> p e t", e=2)
        krh = carve(XB, 40960, 2048, BF16)
        qr = carve(XB, 43008, 2048, BF16)
        vh = carve(XB, 53248, 4096, BF16).rearrange("p (n e) -> p n e", n=8)
        kh = [carve(XB, 57344 + i * 2048, 2048, BF16).rearrange("p (n d) -> p n d", n=8) for i in range(2)]
        qd = carve(XB, 61440, 2048, BF16)
        rt = [tmpf[0][:, 0:TH], tmpf[0][:, TH:T]]
        rowseg[0] = carve(XB, 53248, 2048, F32)[0:1, :]
        ada_nb[0] = 1
        sgr = [tmpf[1][:, 0:TH], tmpf[1][:, TH:T]]
        krT_sp = nc.dram_tensor("krT_sp", [P, 8192], BF16).ap()
        v_sp = nc.dram_tensor("v_sp", [P, 8 * 2048], BF16).ap()
        K_krT = lambda h: ("krT", h)
        K_v = lambda n: ("v", n)

        def rotary_from(bn, bs_, t, dst, dkeys):
            sc.op("dve", lambda e: e.tensor_tensor(out=rt[0], in0=ps[bn][:], in1=cosT[:, t * TH:(t + 1) * TH], op=ALU.mult),
                  r=[PSK[bn], "rope"], w=[("rt", 0)])
            sc.op("dve", lambda e: e.tensor_tensor(out=rt[1], in0=ps[bs_][:], in1=sinT[:, t * TH:(t + 1) * TH], op=ALU.mult),
                  r=[PSK[bs_], "rope"], w=[("rt", 1)])
            sc.op("pool", lambda e: e.tensor_tensor(out=dst, in0=rt[0], in1=rt[1], op=ALU.add),
                  r=[("rt", 0), ("rt", 1)], w=dkeys)

        def proj_fm(wt, wk, blk, nblk, t, bnk):
            for kc in range(KC):
                o = kc * (nblk * 128) + blk * 128
                mm(ps[bnk][:], wt[:, o:o + 128], hT[:, kc, t * TH:(t + 1) * TH], kc == 0, kc == KC - 1,
                   r=[wk, Hk(kc, t)], w=[PSK[bnk]], sig=(kc == KC - 1))

        prot = Rot([(0, 1), (2, 3)])
        vrot = Rot([4, 5, 6, 7])
        for tl in range(4):
            wt, wk = wnext()
            for hh in range(2):
                h = tl * 2 + hh
                for t in range(2):
                    bn, bs_ = prot.next()
                    proj_fm(wt, wk, hh * 2, 4, t, bn)
                    proj_fm(wt, wk, hh * 2 + 1, 4, t, bs_)
                    rotary_from(bn, bs_, t, krT[:, h, t * TH:(t + 1) * TH], [K_krT(h)] + XALL)
        for tl in range(4):
            wt, wk = wnext()
            for n in range(8):
                bnk = vrot.next()
                for kc in range(KC):
                    mm(ps[bnk][:], hT[:, kc, n * 128:(n + 1) * 128], wt[:, kc * 512:(kc + 1) * 512], kc == 0, kc == KC - 1,
                       r=[wk, Hk(kc, n // 4)], w=[PSK[bnk]], sig=(kc == KC - 1))
                sc.op("act", lambda e: e.copy(out=vtm[:, n, tl * 512:(tl + 1) * 512], in_=ps[bnk][:]), r=[PSK[bnk]], w=[K_v(n)] + XALL)
        wt, wk = wnext()
        for j in range(4):
            for t in range(2):
                bnk = prot.next()[0]
                proj_fm(wt, wk, j, 4, t, bnk)
                sc.op("act", lambda e: e.copy(out=skT[:, j, 128 + t * TH:128 + (t + 1) * TH], in_=ps[bnk][:]), r=[PSK[bnk]], w=[("skT", j)] + XALL)
        wt, wk = wnext()
        for n in range(8):
            bnk = vrot.next()
            for kc in range(KC):
                mm(ps[bnk][:], hT[:, kc, n * 128:(n + 1) * 128], wt[:, kc * 512:(kc + 1) * 512], kc == 0, kc == KC - 1,
                   r=[wk, Hk(kc, n // 4)], w=[PSK[bnk]], sig=(kc == KC - 1))
            sc.op("act", lambda e: e.copy(out=svt[:, n + 1, :], in_=ps[bnk][:]), r=[PSK[bnk]], w=[("svt", n + 1)])

        for h in range(8):
            lb = h // 2
            lcol = (h % 2) * 256
            for half in range(2):
                bnk = vrot.next()
                for n4 in range(4):
                    n = half * 4 + n4
                    mm(ps[bnk][:, n4 * 128:(n4 + 1) * 128], krT[:, h, n * 128:(n + 1) * 128], ident[:], True, True,
                       r=[K_krT(h), "ident"], w=[PSK[bnk]], sig=(n4 == 3))
                kl = kdec[:, 8 + h * 8 + half * 4: 8 + h * 8 + half * 4 + 4]
                sc.op("dve", lambda e: e.tensor_tensor(out=khL, in0=ps[bnk][:].rearrange("p (n d) -> p n d", n=4),
                                                        in1=kl.unsqueeze(2).broadcast_to([P, 4, 128]), op=ALU.mult),
                      r=[PSK[bnk], "kdec"], w=["khL"])
                for n4 in range(4):
                    n = half * 4 + n4
                    mm(ps[lb][:, lcol:lcol + 256], khL[:, n4, :], vtm[:, n, h * 256:(h + 1) * 256], n == 0, n == 7,
                       r=["khL", K_v(n)], w=[PSK[lb]], sig=True)
            if h % 2 == 1:
                sc.op("act", lambda e: e.copy(out=payload[:, lb * 512:(lb + 1) * 512], in_=ps[lb][:]), r=[PSK[lb]], w=["payload"])
        sc.op("act", lambda e: e.copy(out=payload[:, 2048:2560].rearrange("p (j t) -> p j t", j=4), in_=skT[:, :, 1024:1152]),
              r=[("skT", j) for j in range(4)], w=["payload"])
        sc.op("act", lambda e: e.copy(out=payload[:, 2560:3072], in_=svt[:, 8, :]), r=[("svt", 8)], w=["payload"])
        for i4 in range(4):
            for hf in range(2):
                sc.op("dve", lambda e: e.tensor_scalar(out=slot, in0=payload[:, hf * 1536:(hf + 1) * 1536], scalar1=sel[:, i4:i4 + 1],
                                                        scalar2=None, op0=ALU.mult), r=["payload", "sel"], w=["slot"])
                sc.dma("sp", bounce[hf].ap()[i4 * P:(i4 + 1) * P, :], slot, r=["slot"], w=["bounce"], dsem=dq)
        sc.dma("sp", krT_sp, carve(XB, 0, 16384, BF16), r=[K_krT(h) for h in range(8)], w=["krT_sp"], dsem=dq2)
        sc.dma("sp", v_sp, carve(XB, 16384, 32768, BF16), r=[K_v(n) for n in range(8)], w=["v_sp"], dsem=dq2)
        ccsem = stack.enter_context(nc.semaphore("ccsem"))
        ccd = DSem(ccsem, "cc")
        Epool = sc.E["pool"]
        sc._deps(Epool, ["bounce"], ["gath"])
        for hf in range(2):
            nc.gpsimd.collective_compute("AllReduce", ALU.add, replica_groups=[[0, 1, 2, 3], [4, 5, 6, 7]],
                                         ins=[bounce[hf].ap().opt()], outs=[gath[hf].ap().opt()]).then_inc(ccsem)
        ccd.cnt = 2
        sc._record((ccd, 2), ["bounce"], ["gath"])
        sc.barrier()

        lrot = Rot([(0, 1), (2, 3)])
        nrot = Rot([(4, 5), (6, 7)])
        g4 = lambda ap: ap.rearrange("p (g q) -> p g q", g=4)

        prrot = Rot(list(range(NPB)))
        dcnt = [0]

        def swa_steps(blocks, hook_every=0):
            SKEW = 2
            steps = []
            for blk in blocks:
                bnum, bden = nrot.next()
                for s in range(2):
                    steps.append((blk, s, bnum, bden))
            st = {}

            def front(i):
                (j, n, qsrc, qkeys, mprev, mkeys), s, bnum, bden = steps[i]
                r0, r1 = s * 64, (s + 1) * 64
                pb = prrot.next()
                blg = lrot.next()
                st[i] = pb
                for kb in range(2):
                    mm(g4(ps[blg[kb]][:]), skT[r0:r1, j, (n + kb) * 128:(n + kb + 1) * 128], qsrc[r0:r1],
                       True, True, r=[("skT", j)] + qkeys, w=[PSK[blg[kb]]], sig=True)
                    sc.op("act", lambda e: e.activation(out=ebuf[kb], in_=ps[blg[kb]][:], func=AF.Exp, scale=0.125),
                          r=[PSK[blg[kb]]], w=[("ebuf", kb)])
                    mk = mprev if kb == 0 else swamask[:, 128:256]
                    sc.op("dve" if kb == 0 else "pool",
                          lambda e: e.tensor_tensor(out=g4(probs[pb][:, kb, :]), in0=g4(ebuf[kb]),
                                                    in1=mk.unsqueeze(1).broadcast_to([P, 4, 128]), op=ALU.mult),
                          r=[("ebuf", kb), "swamask"] + mkeys, w=[("probs", pb, kb)])

            def back(i):
                (j, n, qsrc, qkeys, mprev, mkeys), s, bnum, bden = steps[i]
                r0, r1 = s * 64, (s + 1) * 64
                pb = st.pop(i)
                hk = 2 * j + s
                for kb in range(2):
                    mm(ps[bnum][r0:r1, :], svt[:, n + kb, hk * 64:(hk + 1) * 64], probs[pb][:, kb, :], kb == 0, kb == 1,
                       r=[("svt", n + kb), ("probs", pb, kb)], w=[PSK[bnum]], sig=(kb == 1))
                for kb in range(2):
                    mm(ps[bden][r0:r1, :], ones64[:], probs[pb][:, kb, :], kb == 0, kb == 1,
                       r=["ones64", ("probs", pb, kb)], w=[PSK[bden]], sig=(kb == 1))
                if s == 1:
                    di = dcnt[0] % 2
                    dcnt[0] += 1
                    dt_ = dtmp[di]
                    sc.op("dve", lambda e: e.tensor_tensor(out=g4(dt_), in0=g4(ps[bden][:]),
                                                            in1=sinkE[:, j * 4:(j + 1) * 4].unsqueeze(2).broadcast_to([P, 4, 128]), op=ALU.add),
                          r=[PSK[bden], "sinkE"], w=[("dtmp", di)])
                    sc.op("dve", lambda e: e.reciprocal(out=dt_, in_=dt_), r=[("dtmp", di)], w=[("dtmp", di)])
                    sc.op("dve", lambda e: e.tensor_tensor(out=soT[:, j * 4:(j + 1) * 4, n * 128:(n + 1) * 128],
                                                            in0=g4(ps[bnum][:]), in1=g4(dt_), op=ALU.mult),
                          r=[PSK[bnum], ("dtmp", di)], w=[("so", j * 4 + g) for g in range(4)])

            N = len(steps)
            nh = 0
            for i in range(N + SKEW):
                if i < N:
                    front(i)
                if i - SKEW >= 0:
                    back(i - SKEW)
                if hook_every and i % hook_every == hook_every - 1 and nh < 3:
                    nh += 1
                    bk = vrot.next()
                    ada_tile(bk, bk)
            while hook_every and nh < 3:
                nh += 1
                bk = vrot.next()
                ada_tile(bk, bk)

        def receive():
            first = {}

            def acc(eng, dst, src, scal, key, rkeys):
                if key not in first:
                    first[key] = 1
                    sc.op(eng, lambda e: e.tensor_scalar(out=dst, in0=src, scalar1=scal, scalar2=None, op0=ALU.mult),
                          r=["slot"] + rkeys, w=[key])
                else:
                    sc.op(eng, lambda e: e.scalar_tensor_tensor(out=dst, in0=src, scalar=scal, in1=dst, op0=ALU.mult, op1=ALU.add),
                          r=["slot", key] + rkeys, w=[key])

            for i4 in range(4):
                for hf in range(2):
                    sc.dma("sp", slot, gath[hf].ap()[i4 * P:(i4 + 1) * P, :], r=["gath"], w=["slot"], dsem=dq)
                    heads = range(6) if hf == 0 else range(6, 8)
                    for h in heads:
                        o = h * 256 - hf * 1536
                        acc("dve", Sst[:, h, :], slot[:, o:o + 256], coef[:, i4 * 8 + h:i4 * 8 + h + 1], ("Sst", h), ["coef"])
                    if hf == 1:
                        acc("dve", halo, slot[:, 512:1536], sel[:, 4 + i4:5 + i4], "halo", ["sel"])
            sc.op("act", lambda e: e.copy(out=skT[:, :, 0:128], in_=halo[:, 0:512].rearrange("p (j t) -> p j t", j=4)),
                  r=["halo"], w=[("skT", j) for j in range(4)])
            sc.op("act", lambda e: e.copy(out=svt[:, 0, :], in_=halo[:, 512:1024]), r=["halo"], w=[("svt", 0)])
            for h in range(8):
                sc.op("act", lambda e: e.copy(out=Sbf[:, h, :], in_=Sst[:, h, :]), r=[("Sst", h)], w=[("Sbf", h)])
            sc.op("dve", lambda e: e.tensor_scalar(out=maskp0, in0=swamask[:, 0:128], scalar1=sel[:, 8:9], scalar2=None, op0=ALU.mult),
                  r=["swamask", "sel"], w=["maskp0"])

        for j in range(4):
            if j == 3:
                receive()
            wt, wk = wnext()
            for g in range(4):
                for t in range(2):
                    bnk = vrot.next()
                    proj_fm(wt, wk, g, 4, t, bnk)
                    sc.op("act", lambda e: e.copy(out=sqT[:, g, t * TH:(t + 1) * TH], in_=ps[bnk][:]), r=[PSK[bnk]], w=[("sqT", g)])
            sc.op("pool", lambda e: e.tensor_copy(out=sq0[:, j * 4:(j + 1) * 4, :], in_=sqT[:, :, 0:128]),
                  r=[("sqT", g) for g in range(4)], w=[("sq0", j)])
            swa_steps([(j, n, sqT[:, :, n * 128:(n + 1) * 128], [("sqT", g) for g in range(4)], swamask[:, 0:128], []) for n in range(1, 8)],
                      hook_every=4)

        swa_steps([(j, 0, sq0[:, j * 4:(j + 1) * 4, :], [("sq0", j)], maskp0, ["maskp0"]) for j in range(4)])
        debug_dump("soT", carve(XB, 0, 32768, BF16), [P, KC * T], [("so", k) for k in range(KC)])
        sc.barrier()

        def gated_branch(src, skeys, accumulate):
            grot = Rot([(0, 1), (2, 3), (4, 5), (6, 7)])
            for mp in range(8):
                wt, wk = wnext()
                for m2 in range(2):
                    mmi = mp * 2 + m2
                    for t in range(2):
                        bg, by = grot.next()
                        proj_fm(wt, wk, m2, 4, t, bg)
                        for kc in range(KC):
                            o = kc * 512 + (2 + m2) * 128
                            mm(ps[by][:], wt[:, o:o + 128], src[:, kc, t * TH:(t + 1) * TH], kc == 0, kc == KC - 1,
                               r=[wk, skeys(kc)], w=[PSK[by]], sig=(kc == KC - 1))
                        s = t
                        sc.op("act", lambda e: e.activation(out=rt[s], in_=ps[bg][:], func=AF.Sigmoid), r=[PSK[bg]], w=[("rt", s)])
                        zs = zT[:, mmi, t * TH:(t + 1) * TH]
                        if not accumulate:
                            sc.op("dve", lambda e: e.tensor_tensor(out=zs, in0=rt[s], in1=ps[by][:], op=ALU.mult),
                                  r=[("rt", s), PSK[by]], w=[("z", mmi)])
                        else:
                            sc.op("dve", lambda e: e.tensor_tensor(out=rt[s], in0=rt[s], in1=ps[by][:], op=ALU.mult),
                                  r=[("rt", s), PSK[by]], w=[("rt", s)])
                            sc.op("pool", lambda e: e.tensor_tensor(out=zs, in0=rt[s], in1=zs, op=ALU.add),
                                  r=[("rt", s), ("z", mmi)], w=[("z", mmi)])

        gated_branch(soT, lambda kc: ("so", kc), False)
        sc.barrier()

        gam = [1.0 - 2.0 ** (-5.0 - h) for h in range(8)]
        for h in range(8):
            wq, wqk = wnext()
            wg, wgk = wq, wqk
            if True:
                sc.dma("sp", krh, krT_sp[:, h * T:(h + 1) * T], r=["krT_sp"], w=["krh"], dsem=dq)
                sc.dma("sp", vh, v_sp.rearrange("p (n c) -> p n c", n=8)[:, :, h * 256:(h + 1) * 256], r=["v_sp"], w=["vh"], dsem=dq2)
                for t in range(2):
                    bn, bs_ = prot.next()
                    proj_fm(wq, wqk, 0, 4, t, bn)
                    proj_fm(wq, wqk, 1, 4, t, bs_)
                    rotary_from(bn, bs_, t, qr[:, t * TH:(t + 1) * TH], ["qr"])
                    sc.op("pool", lambda e: e.tensor_tensor(out=qd[:, t * TH:(t + 1) * TH].rearrange("p (n i) -> p n i", n=4),
                                                             in0=qr[:, t * TH:(t + 1) * TH].rearrange("p (n i) -> p n i", n=4),
                                                             in1=qdec[:, h * 128:(h + 1) * 128].unsqueeze(1).broadcast_to([P, 4, 128]),
                                                             op=ALU.mult), r=["qr", "qdec"], w=["qd"])
                khh = kh[h % 2]
                for half in range(2):
                    bnk = vrot.next()
                    for n4 in range(4):
                        n = half * 4 + n4
                        mm(ps[bnk][:, n4 * 128:(n4 + 1) * 128], krh[:, n * 128:(n + 1) * 128], ident[:], True, True,
                           r=["krh", "ident"], w=[PSK[bnk]], sig=(n4 == 3))
                    sc.op("act", lambda e: e.activation(out=khh[:, half * 4:(half + 1) * 4, :],
                                                        in_=ps[bnk][:].rearrange("p (n d) -> p n d", n=4), func=AF.Identity,
                                                        scale=kdec[:, h:h + 1]), r=[PSK[bnk], "kdec"], w=[("kh", h % 2)])
                g128 = gam[h] ** 128

                def emit_sc(n):
                    b_sc = vrot.next()
                    mm(ps[b_sc][:, 0:128], krh[:, n * 128:(n + 1) * 128], qr[:, n * 128:(n + 1) * 128], True, True,
                       r=["krh", "qr"], w=[PSK[b_sc]], sig=True)
                    sb_ = scb[n % 2]
                    sc.op("dve", lambda e: e.tensor_tensor(out=sb_, in0=ps[b_sc][:, 0:128], in1=retmask[:, h * 128:(h + 1) * 128], op=ALU.mult),
                          r=[PSK[b_sc], "retmask"], w=[("scb", n % 2)])

                emit_sc(0)
                for n in range(8):
                    b_kv = None
                    if n < 7:
                        b_kv = vrot.next()
                        mm(ps[b_kv][:, 0:256], khh[:, n, :], vh[:, n, :], True, True,
                           r=[("kh", h % 2), "vh"], w=[PSK[b_kv]], sig=True)
                    sb_ = scb[n % 2]
                    b_o = vrot.next()
                    for eh in range(2):
                        mm(ps[b_o][:, eh * 128:(eh + 1) * 128], vh[:, n, eh * 128:(eh + 1) * 128], sb_, True, False,
                           r=["vh", ("scb", n % 2)], w=[PSK[b_o]], sig=False)
                        mm(ps[b_o][:, eh * 128:(eh + 1) * 128], Sbf[:, h, eh * 128:(eh + 1) * 128], qd[:, n * 128:(n + 1) * 128], False, True,
                           r=[("Sbf", h), "qd"], w=[PSK[b_o]], sig=True)
                    if n < 7:
                        emit_sc(n + 1)
                        sc.op("dve", lambda e: e.scalar_tensor_tensor(out=Sst[:, h, :], in0=Sst[:, h, :], scalar=float(g128), in1=ps[b_kv][:, 0:256],
                                                                       op0=ALU.mult, op1=ALU.add), r=[("Sst", h), PSK[b_kv]], w=[("Sst", h)])
                        sc.op("act", lambda e: e.copy(out=Sbf[:, h, :], in_=Sst[:, h, :]), r=[("Sst", h)], w=[("Sbf", h)])
                    sc.op("act", lambda e: e.copy(out=ro[:, :, n * 128:(n + 1) * 128], in_=ps[b_o][:, 0:256].rearrange("p (e q) -> p e q", e=2)),
                          r=[PSK[b_o]], w=["ro"])
                rms_stats(lambda eh: ro[:, eh, :], 2, ones256, lambda eh: "ro", (0, 1))
                for eh in range(2):
                    for t in range(2):
                        bnk = vrot.next()
                        proj_fm(wg, wgk, 2 + eh, 4, t, bnk)
                        s = t
                        sc.op("act", lambda e: e.activation(out=sgr[s], in_=ps[bnk][:], func=AF.Silu), r=[PSK[bnk]], w=[("sgr", s)])
                        sc.op("dve", lambda e: e.tensor_tensor(out=sgr[s], in0=sgr[s], in1=rstd[:, t * TH:(t + 1) * TH], op=ALU.mult),
                              r=[("sgr", s), "rstd"], w=[("sgr", s)])
                        sc.op("pool", lambda e: e.tensor_tensor(out=roG[:, h * 2 + eh, t * TH:(t + 1) * TH], in0=sgr[s],
                                                                 in1=ro[:, eh, t * TH:(t + 1) * TH], op=ALU.mult),
                              r=[("sgr", s), "ro"], w=[("roG", h * 2 + eh)])
        debug_dump("roG", carve(XB, 0, 32768, BF16), [P, KC * T], [("roG", k) for k in range(KC)])
        debug_dump("krh", krh, [P, T], ["krh"])
        debug_dump("qr", qr, [P, T], ["qr"])
        debug_dump("qd", qd, [P, T], ["qd"])
        debug_dump("vh", carve(XB, 53248, 4096, BF16), [P, 2048], ["vh"])
        debug_dump("ro", carve(XB, 32768, 8192, F32), [P, 2048], ["ro"])
        debug_dump("Sst", carve(XB, 45056, 8192, F32), [P, 2048], [("Sst", h) for h in range(8)])
        debug_dump("kh", carve(XB, 57344 + 2048, 2048, BF16), [P, 1024], [("kh", 1)])
        sc.barrier()
        gated_branch(roG, lambda kc: ("roG", kc), True)
        sc.barrier()
        sc.dma("sp", xT[:].rearrange("p a b -> p (a b)"), xspill, r=["xspill"], w=[Xk(k) for k in range(KC)], dsem=dq)
        orot = Rot([0, 1, 2, 3, 4, 5, 6, 7])
        for mg in range(4):
            wt, wk = wnext()
            for m in range(4):
                mmi = mg * 4 + m
                for t in range(2):
                    bnk = orot.next()
                    for kc in range(KC):
                        o = kc * 512 + m * 128
                        mm(ps[bnk][:], wt[:, o:o + 128], zT[:, kc, t * TH:(t + 1) * TH], kc == 0, kc == KC - 1,
                           r=[wk, ("z", kc)], w=[PSK[bnk]], sig=(kc == KC - 1))
                    xs = xT[:, mmi, t * TH:(t + 1) * TH]
                    sc.op("dve", lambda e: e.scalar_tensor_tensor(out=xs, in0=ps[bnk][:], scalar=colG(1, mmi), in1=xs,
                                                                   op0=ALU.mult, op1=ALU.add), r=[PSK[bnk], ("cols", 1, "G"), Xk(mmi)], w=[Xk(mmi)])
        sc.barrier()

    if "mixer" in stages:
        mixer()
    debug_dump("x2", xT[:].rearrange("p a b -> p (a b)"), [P, KC * T], [Xk(k) for k in range(KC)])
    assert wpos[0] == W_FFN2, (wpos[0], W_FFN2)
    if "ffn2" in stages:
        run_ffn(2)

    sc.barrier()
    dout = [sc.new_dsem("out%d" % i) for i in range(4)]
    for t in range(2):
        norm_stats(t)
        rs = rstd[:, t * TH:(t + 1) * TH]
        for kc in range(KC):
            ti = nm_cnt[0] % 4
            nm_cnt[0] += 1
            sc.op("dve", lambda e: e.scalar_tensor_tensor(out=tmh[ti], in0=xT[:, kc, t * TH:(t + 1) * TH], scalar=gains[:, 48 + kc:49 + kc], in1=rs,
                                                           op0=ALU.mult, op1=ALU.mult), r=[Xk(kc), ("rstd", t), "gains"], w=[("tmh", ti)])
            sc.dma("sp", yT_d[:, kc * T + t * TH:kc * T + (t + 1) * TH], tmh[ti], r=[("tmh", ti)], w=[("yT", kc, t)], dsem=dout[ti])
    sc.final_wait("sp", [("yT", kc, t) for kc in range(KC) for t in range(2)])
    E = sc.E["sp"]
    for d in dout:
        sc._wait(E, (d, d.cnt))
    for name, d in dbg_out.items():
        sc._wait(E, (d, d.cnt))
    assert wpos[0] == len(seq), (wpos[0], len(seq))
    stack.close()
    return nc


def _tile_lhs(w, blocks):
    nb = len(blocks)
    cols = np.concatenate(blocks)
    sub = w[:, cols].reshape(KC, P, nb * 128)
    return np.ascontiguousarray(sub.transpose(1, 0, 2)).reshape(P, KC * nb * 128)


def _prep_shared(inp):
    f = lambda a: np.asarray(a, dtype=np.float32)
    sh = {}
    w_ada = f(inp["w_ada"])[0]
    sh["wada"] = np.ascontiguousarray(w_ada.reshape(KC, P, 36, 512).transpose(2, 1, 0, 3)).reshape(36, P, WCOLS)
    for i, (gu, dn) in enumerate((("w_ffn1_gu", "w_ffn1_down"), ("w_ffn2_gu", "w_ffn2_down"))):
        wgu = f(inp[gu])[0]
        tiles = []
        for tt in range(22):
            blocks = []
            for j in (2 * tt, 2 * tt + 1):
                blocks.append(np.arange(j * 128, (j + 1) * 128))
                blocks.append(DFF + np.arange(j * 128, (j + 1) * 128))
            tiles.append(_tile_lhs(wgu, blocks))
        sh["wgu%d" % (i + 1)] = np.stack(tiles)
        wd = f(inp[dn])[0]
        dt = np.zeros((NCH * 4, P, MAXNJ * 512), np.float32)
        jb = 0
        for c, nj in enumerate(CHUNKS):
            rows = wd[jb * 128:(jb + nj) * 128].reshape(nj, P, 4, 512)
            for mg in range(4):
                dt[c * 4 + mg, :, :nj * 512] = rows[:, :, mg, :].transpose(1, 0, 2).reshape(P, nj * 512)
            jb += nj
        sh["wd%d" % (i + 1)] = dt
    w_in = f(inp["w_in"])[0]
    o_rq, o_rk, o_rv, o_rg, o_sq, o_sk, o_sv, o_gr, o_gs = 0, 1024, 2048, 4096, 6144, 8192, 8704, 9216, 11264
    sw = np.concatenate([np.arange(64, 128), np.arange(0, 64)])
    win = []

    def qk_tiles(off):
        out = []
        for tl in range(4):
            blocks = []
            for hh in range(2):
                h = tl * 2 + hh
                base = off + h * 128
                blocks.append(base + np.arange(128))
                blocks.append(base + sw)
            out.append(_tile_lhs(w_in, blocks))
        return out

    def rhs_tiles(off, ncols):
        out = []
        for tl in range(ncols // 512):
            sub = w_in[:, off + tl * 512: off + (tl + 1) * 512].reshape(KC, P, 512)
            out.append(np.ascontiguousarray(sub.transpose(1, 0, 2)).reshape(P, WCOLS))
        return out

    win += qk_tiles(o_rk)
    win += rhs_tiles(o_rv, 2048)
    win.append(_tile_lhs(w_in, [o_sk + j * 128 + np.arange(128) for j in range(4)]))
    win += rhs_tiles(o_sv, 512)
    for j in range(4):
        blocks = []
        for g in range(4):
            ha, hb = 8 * j + g, 8 * j + 4 + g
            blocks.append(np.concatenate([o_sq + ha * 64 + np.arange(64), o_sq + hb * 64 + np.arange(64)]))
        win.append(_tile_lhs(w_in, blocks))
    for h in range(8):
        win.append(_tile_lhs(w_in, [o_rq + h * 128 + np.arange(128), o_rq + h * 128 + sw,
                                    o_rg + (2 * h) * 128 + np.arange(128), o_rg + (2 * h + 1) * 128 + np.arange(128)]))
    sh["win"] = np.stack(win)
    assert sh["win"].shape[0] == 22
    w_swa = f(inp["w_swa_branch"])[0]
    perm = []
    for j in range(4):
        for g in range(4):
            for s in range(2):
                perm.append((8 * j + 4 * s + g) * 64 + np.arange(64))
    perm = np.concatenate(perm)
    w_swa_p = w_swa[perm]
    w_ret = f(inp["w_ret_branch"])[0]
    w_out = f(inp["w_out"])[0]
    wgb = []
    for goff, wmat in ((o_gs, w_swa_p), (o_gr, w_ret)):
        for mp in range(8):
            A = _tile_lhs(w_in, [goff + (2 * mp + b) * 128 + np.arange(128) for b in range(2)]).reshape(P, KC, 256)
            B = _tile_lhs(wmat, [(2 * mp + b) * 128 + np.arange(128) for b in range(2)]).reshape(P, KC, 256)
            wgb.append(np.concatenate([A, B], axis=2).reshape(P, WCOLS))
    sh["wgb"] = np.stack(wgb)
    sh["wbr"] = np.stack([_tile_lhs(w_out, [(mg * 4 + b) * 128 + np.arange(128) for b in range(4)]) for mg in range(4)])
    gam = np.array([1.0 - 2.0 ** (-5.0 - h) for h in range(8)], np.float64)
    i = np.arange(128)
    qdec = np.stack([gam[h] ** (i + 1.0) for h in range(8)]).reshape(1, 1024)
    sh["qdec"] = np.broadcast_to(qdec, (P, 1024)).astype(np.float32).copy()
    k_ = i[:, None]
    q_ = i[None, :]
    rm = np.concatenate([np.where(q_ >= k_, gam[h] ** np.maximum(q_ - k_, 0) * (128.0 ** -0.5), 0.0) for h in range(8)], axis=1)
    sh["retmask"] = rm.astype(np.float32)
    kd = np.zeros((P, 72), np.float64)
    for h in range(8):
        kd[:, h] = (128.0 ** -0.5) * gam[h] ** (127.0 - i)
        for n in range(8):
            kd[:, 8 + h * 8 + n] = (128.0 ** -0.5) * gam[h] ** (1023.0 - 128.0 * n - i)
    sh["kdec"] = kd.astype(np.float32)
    sm = np.zeros((P, 256), np.float32)
    sm[:, 0:128] = (k_ > q_)
    sm[:, 128:256] = (k_ <= q_)
    sh["swamask"] = sm
    sh["ident"] = np.eye(P, dtype=np.float32)
    gl = np.zeros((P, 64), np.float32)
    for c, nm in enumerate(("norm_ffn1", "norm_mix", "norm_ffn2")):
        gl[:, c * 16:(c + 1) * 16] = f(inp[nm])[0].reshape(KC, P).T
    gl[:, 48:64] = f(inp["norm_final"]).reshape(KC, P).T
    sh["gains"] = gl
    sk = f(inp["sinks"])[0]
    sl = np.zeros((P, 16), np.float32)
    for j in range(4):
        for g in range(4):
            sl[0:64, j * 4 + g] = sk[8 * j + g]
            sl[64:128, j * 4 + g] = sk[8 * j + 4 + g]
    sh["sinksl"] = sl
    return sh, gam


def _prep_core(inp, r, gam):
    f = lambda a: np.asarray(a, dtype=np.float32)
    b, q = r // 4, r % 4
    m = {}
    xs = f(inp["x"])[b, q * T:(q + 1) * T]
    m["xT"] = np.ascontiguousarray(xs.reshape(T, KC, P).transpose(2, 1, 0)).reshape(P, KC * T)
    m["cT"] = np.ascontiguousarray(f(inp["c"])[b].reshape(KC, P).T)
    m["badaT"] = np.ascontiguousarray(f(inp["b_ada"])[0].reshape(144, P).T)
    half = 64
    inv_freq = (1.0 / (np.float32(10000.0) ** (np.arange(half, dtype=np.float32) / np.float32(half)))).astype(np.float32)
    pos = np.arange(q * T, (q + 1) * T, dtype=np.float32)
    ang = (pos[None, :] * inv_freq[:, None]).astype(np.float32)
    cos = np.cos(ang).astype(np.float32)
    sin = np.sin(ang).astype(np.float32)
    rope = np.zeros((P, 2 * T), np.float32)
    rope[0:64, 0:T] = cos
    rope[64:128, 0:T] = cos
    rope[0:64, T:] = -sin
    rope[64:128, T:] = sin
    m["rope"] = rope
    sel = np.zeros((P, 16), np.float32)
    sel[:, q] = 1.0
    if q > 0:
        sel[:, 4 + q - 1] = 1.0
        sel[:, 8] = 1.0
    m["sel"] = sel
    cf = np.zeros((P, 32), np.float64)
    for i4 in range(q):
        for h in range(8):
            cf[:, i4 * 8 + h] = gam[h] ** (128.0 * 8 * (q - 1 - i4))
    m["coef"] = cf.astype(np.float32)
    return m


_CACHE = {}


def kernel(**inputs):
    sh, gam = _prep_shared(inputs)
    if "nc" not in _CACHE:
        _CACHE["nc"] = build_program()
    nc = _CACHE["nc"]
    in_maps = []
    for r in range(NCORES):
        m = dict(sh)
        m.update(_prep_core(inputs, r, gam))
        in_maps.append(m)
    res = run_bass_kernel_spmd(nc, in_maps, core_ids=list(range(NCORES)))
    B, S = 2, 4096
    out = np.zeros((B, S, D), np.float32)
    for r in range(NCORES):
        b, q = r // 4, r % 4
        yT = np.asarray(res.results[r]["yT"]).reshape(P, KC, T)
        out[b, q * T:(q + 1) * T, :] = yT.transpose(2, 1, 0).reshape(T, D)
    return out
```

```python
import numpy as np
import concourse.bass as bass
import concourse.mybir as mybir
from concourse.bass_utils import run_bass_kernel_spmd

F32 = mybir.dt.float32
BF16 = mybir.dt.bfloat16
AF = mybir.ActivationFunctionType
ALU = mybir.AluOpType

NCORES = 8
P = 128
D = 2048
KC = 16
T = 1024
TH = 512
DFF = 5632
NJ = 44
CHUNKS = [6, 6, 6, 4, 6, 6, 6, 4]
NCH = len(CHUNKS)
MAXNJ = max(CHUNKS)
EPS = 1e-6
WCOLS = 8192
NSLOT = 2
PF = 1


class Eng:
    def __init__(self, name, h, sem):
        self.name = name
        self.h = h
        self.sem = sem
        self.cnt = 0
        self.seen = {}
        self.pending = False


class DSem:
    def __init__(self, sem, name):
        self.sem = sem
        self.cnt = 0
        self.name = name


class Sched:
    def __init__(self, nc, stack):
        self.nc = nc
        self.stack = stack
        self.E = {}
        for name, h in (("pe", nc.tensor), ("act", nc.scalar), ("dve", nc.vector),
                        ("pool", nc.gpsimd), ("sp", nc.sync)):
            sem = stack.enter_context(nc.semaphore("sem_" + name))
            self.E[name] = Eng(name, h, sem)
        self.res_w = {}
        self.res_r = {}
        self.dsems = []

    def new_dsem(self, name):
        d = DSem(self.stack.enter_context(self.nc.semaphore("dsem_" + name)), name)
        self.dsems.append(d)
        return d

    def _wait(self, E, tok):
        S, v = tok
        if S is E and E.name == "pe":
            return
        if isinstance(S, Eng) and S.name == "pe" and S.pending and v > S.cnt:
            raise RuntimeError("wait on unsignaled PE op")
        if E.seen.get(S, 0) < v:
            E.h.wait_ge(S.sem, v)
            E.seen[S] = v

    def _deps(self, E, r, w):
        toks = []
        for k in r:
            t = self.res_w.get(k)
            if t is not None:
                toks.append(t)
        for k in w:
            t = self.res_w.get(k)
            if t is not None:
                toks.append(t)
            for S, v in self.res_r.get(k, {}).items():
                toks.append((S, v))
        for t in toks:
            self._wait(E, t)

    def _record(self, tok, r, w):
        S, v = tok
        for k in r:
            d = self.res_r.setdefault(k, {})
            if d.get(S, 0) < v:
                d[S] = v
        for k in w:
            self.res_w[k] = tok
            self.res_r[k] = {}

    def op(self, en, fn, r=(), w=(), sig=True):
        E = self.E[en]
        self._deps(E, r, w)
        ins = fn(E.h)
        if sig:
            E.cnt += 1
            ins.then_inc(E.sem, 1)
            E.pending = False
            tok = (E, E.cnt)
        else:
            E.pending = True
            tok = (E, E.cnt + 1)
        self._record(tok, r, w)
        return ins

    def dma(self, en, out, in_, r=(), w=(), dsem=None):
        E = self.E[en]
        self._deps(E, r, w)
        self._wait(E, (dsem, dsem.cnt))
        ins = E.h.dma_start(out=out, in_=in_)
        dsem.cnt += 16
        ins.then_inc(dsem.sem, 16)
        self._record((dsem, dsem.cnt), r, w)
        return ins

    def barrier(self):
        assert not self.E["pe"].pending
        for E in self.E.values():
            for F in self.E.values():
                if F is not E and F.cnt > 0:
                    self._wait(E, (F, F.cnt))
            for d in self.dsems:
                if d.cnt > 0:
                    self._wait(E, (d, d.cnt))

    def final_wait(self, en, keys):
        E = self.E[en]
        self._deps(E, keys, ())


class WStream:
    def __init__(self, sc, slots, nbase, seq):
        self.sc = sc
        self.slots = slots
        self.nbase = nbase
        self.seq = seq
        self.issued = 0
        self.pf = 1
        self.dsems = [sc.new_dsem("w%d" % i) for i in range(len(slots))]
        self.slot_tile = [-1] * len(slots)
        self.slot_of = {}

    def _try_issue(self, k, cur):
        ap, allow_extra = self.seq[k]
        cand = list(range(len(self.slots) if allow_extra else self.nbase))
        free = [c for c in cand if self.slot_tile[c] < cur]
        if not free:
            return False
        s = min(free, key=lambda c: self.slot_tile[c])
        ncols = ap.shape[-1]
        self.sc.dma("pool", self.slots[s][:, 0:ncols], ap, w=[("wslot", s)], dsem=self.dsems[s])
        self.slot_tile[s] = k
        self.slot_of[k] = s
        return True

    def get(self, i, pf=None):
        pf = self.pf if pf is None else pf
        while self.issued < min(i + pf + 1, len(self.seq)):
            if not self._try_issue(self.issued, i):
                assert self.issued > i, "no slot for the tile being consumed"
                break
            self.issued += 1
        s = self.slot_of[i]
        return self.slots[s], ("wslot", s)


class Rot:
    def __init__(self, items):
        self.items = list(items)
        self.i = 0

    def next(self):
        v = self.items[self.i % len(self.items)]
        self.i += 1
        return v


def build_program(dbg_names=(), stages=("ffn1", "mixer", "ffn2")):
    from contextlib import ExitStack
    nc = bass.Bass("TRN2", target_bir_lowering=False)
    stack = ExitStack()
    sc = Sched(nc, stack)

    def din(name, shape, dt=F32):
        return nc.dram_tensor(name, list(shape), dt, kind="ExternalInput").ap()

    xT_d = din("xT", [P, KC * T])
    cT_d = din("cT", [P, KC])
    badaT_d = din("badaT", [P, 144])
    gains_d = din("gains", [P, 64])
    rope_d = din("rope", [P, 2 * T])
    qdec_d = din("qdec", [P, 8 * 128])
    retmask_d = din("retmask", [P, 8 * 128])
    kdec_d = din("kdec", [P, 8 + 64])
    swamask_d = din("swamask", [P, 256])
    sel_d = din("sel", [P, 16])
    coef_d = din("coef", [P, 32])
    sinks_d = din("sinksl", [P, 16])
    ident_d = din("ident", [P, P])
    wada_d = din("wada", [36, P, WCOLS])
    wgu_d = [din("wgu1", [22, P, WCOLS]), din("wgu2", [22, P, WCOLS])]
    wd_d = [din("wd1", [NCH * 4, P, MAXNJ * 512]), din("wd2", [NCH * 4, P, MAXNJ * 512])]
    NWIN = 22
    win_d = din("win", [NWIN, P, WCOLS])
    wgb_d = din("wgb", [16, P, WCOLS])
    wbr_d = din("wbr", [4, P, WCOLS])
    yT_d = nc.dram_tensor("yT", [P, KC * T], F32, kind="ExternalOutput").ap()
    xspill = nc.dram_tensor("xspill", [P, KC * T], F32).ap()
    bounce = [nc.dram_tensor("bounce%d" % i, [4 * P, 1536], F32) for i in range(2)]
    gath = [nc.dram_tensor("gath%d" % i, [4 * P, 1536], F32) for i in range(2)]
    dbg_out = {}

    def sb(name, shape, dt=F32):
        return stack.enter_context(nc.sbuf_tensor(name, list(shape), dt))

    xT = sb("xT_sb", [P, KC, T])
    hT = sb("hT", [P, KC, T], BF16)
    wslots = [sb("wslot%d" % i, [P, WCOLS], BF16) for i in range(NSLOT)]
    modT = sb("modT", [P, 144])
    cols = sb("cols", [P, 9 * 16])
    gains = sb("gains_sb", [P, 64])
    rstd = sb("rstd", [P, T])
    tmpf = [sb("tmpf%d" % i, [P, T]) for i in range(2)]
    sqb = [sb("sqb%d" % i, [P, T], BF16) for i in range(2)]
    onesD = sb("onesD", [P, P], BF16)
    ident = sb("ident_sb", [P, P], BF16)
    YBYTES = 57344
    yar = sb("yar", [P, YBYTES // 2], BF16)
    XB = xT[:].rearrange("p a b -> p (a b)").bitcast(BF16)
    YB = yar[:]

    def carve(arena, off, nbytes, dt):
        assert off % 4 == 0 and nbytes % 4 == 0
        v = arena[:, off // 2:(off + nbytes) // 2]
        return v if dt == BF16 else v.bitcast(F32)

    ps = [stack.enter_context(nc.psum_tensor("ps%d" % i, [P, 512], F32)) for i in range(8)]
    PSK = [("ps", i) for i in range(8)]
    dq = sc.new_dsem("io")
    dq2 = sc.new_dsem("io2")

    def Xk(kc):
        return ("x", kc)

    def Hk(kc, t):
        return ("h", kc, t)

    seq = []
    NADA0 = 8
    for g in range(NADA0):
        seq.append((wada_d[g], True))

    def ffn_seq(f):
        gu_t = 0
        plan = []
        order = [("gu", 0)]
        for c in range(1, NCH):
            order += [("gu", c), ("d", c - 1)]
        order.append(("d", NCH - 1))
        for kind, c in order:
            if kind == "gu":
                for _ in range(CHUNKS[c] // 2):
                    plan.append(wgu_d[f][gu_t])
                    gu_t += 1
            else:
                for mg in range(4):
                    plan.append(wd_d[f][c * 4 + mg][:, 0:CHUNKS[c] * 512])
        return plan

    if "ffn1" in stages:
        for idx, tl_ in enumerate(ffn_seq(0)):
            seq.append((tl_, True))
            if idx < 24 - NADA0:
                seq.append((wada_d[NADA0 + idx], True))
    else:
        for g in range(NADA0, 24):
            seq.append((wada_d[g], True))
    if "mixer" not in stages:
        for g in range(24, 36):
            seq.append((wada_d[g], True))
    wi = 0

    def win_next(n=1):
        nonlocal wi
        out = [(win_d[wi + k], False) for k in range(n)]
        wi += n
        return out

    if "mixer" in stages:
        seq += win_next(4 + 4 + 1 + 1)
        for j in range(4):
            seq += win_next(1)
            for q3 in range(3):
                seq.append((wada_d[24 + j * 3 + q3], False))
        for mp in range(8):
            seq.append((wgb_d[mp], False))
        seq += win_next(8)
        for mp in range(8):
            seq.append((wgb_d[8 + mp], False))
        for mg in range(4):
            seq.append((wbr_d[mg], False))
        assert wi == NWIN
    W_FFN2 = len(seq)
    if "ffn2" in stages:
        for idx, tl_ in enumerate(ffn_seq(1)):
            seq.append((tl_, idx >= 3))
    yslot = carve(YB, 32768, 16384, BF16)
    ws = WStream(sc, wslots + [yslot], NSLOT, seq)
    ws.pf = 2
    wpos = [0]

    def wnext(pf=None):
        t, k = ws.get(wpos[0], pf)
        wpos[0] += 1
        return t, k

    def mm(out, lhsT, rhs, start, stop, r, w, sig):
        sc.op("pe", lambda e: e.matmul(out, lhsT=lhsT, rhs=rhs, start=start, stop=stop), r=r, w=w, sig=sig)

    def debug_dump(name, ap, shape, keys):
        if name not in dbg_names:
            return
        d = nc.dram_tensor("dbg_" + name, list(shape), ap.dtype, kind="ExternalOutput").ap()
        ds = sc.new_dsem("dbg_" + name)
        sc.dma("sp", d, ap, r=keys, dsem=ds)
        dbg_out[name] = ds

    sc.dma("sp", xT[:].rearrange("p a b -> p (a b)"), xT_d, w=[Xk(k) for k in range(KC)], dsem=dq)
    small = {}
    yoff = [0]

    def ycarve(nbytes, dt=F32):
        v = carve(YB, yoff[0], nbytes, dt)
        yoff[0] += nbytes
        return v

    for name, d_ap, n in (("cT", cT_d, KC),):
        t = ycarve(n * 4)
        sc.dma("sp", t, d_ap, w=[name], dsem=dq2)
        small[name] = t
    badaT_sb = sb("badaT_sb", [P, 144])
    sc.dma("sp", badaT_sb[:], badaT_d, w=["badaT"], dsem=dq2)
    small["badaT"] = badaT_sb[:]
    sinks_sb = sb("sinks_sb", [P, 16])
    sc.dma("sp", sinks_sb[:], sinks_d, w=["sinks"], dsem=dq2)
    sc.dma("sp", gains[:], gains_d, w=["gains"], dsem=dq2)
    identf = ycarve(P * 4)
    sc.dma("sp", identf, ident_d, w=["identf"], dsem=dq2)
    sc.op("dve", lambda e: e.tensor_copy(out=ident[:], in_=identf), r=["identf"], w=["ident"])
    sc.op("dve", lambda e: e.memset(onesD[:], 1.0 / D), w=["onesD"])
    ones1 = sb("ones1", [1, 1])
    sc.op("dve", lambda e: e.memset(ones1[:], 1.0), w=["ones1"])
    ones64 = sb("ones64", [P, 64], BF16)
    sc.op("dve", lambda e: e.memset(ones64[:], 1.0), w=["ones64"])
    ones256 = sb("ones256", [P, P], BF16)
    sc.op("dve", lambda e: e.memset(ones256[:], 1.0 / 256.0), w=["ones256"])
    cact = sb("cact", [P, KC], BF16)
    sc.op("act", lambda e: e.activation(out=cact[:], in_=small["cT"], func=AF.Silu), r=["cT"], w=["cact"])
    sinkE = sb("sinkE", [P, 16])
    sc.op("act", lambda e: e.activation(out=sinkE[:], in_=sinks_sb[:], func=AF.Exp), r=["sinks"], w=["sinkE"])

    rowseg = [carve(YB, 28672 + i * 2048, 2048, F32)[0:1, :] for i in range(2)]
    MODB = 7
    ada_g = [0]
    ada_nb = [2]

    def derive(i, parts):
        gi, sh, scl, gt, gscale = ((0, 0, 1, 2, 0.5), (16, 3, 4, 5, 1.0), (32, 6, 7, 8, 0.5))[i]
        A = cols[:, (3 * i) * 16:(3 * i + 1) * 16]
        B = cols[:, (3 * i + 1) * 16:(3 * i + 2) * 16]
        G = cols[:, (3 * i + 2) * 16:(3 * i + 3) * 16]
        if "AB" in parts:
            sc.op("dve", lambda e: e.scalar_tensor_tensor(out=A, in0=modT[:, scl * 16:(scl + 1) * 16], scalar=1.0,
                                                           in1=gains[:, gi:gi + 16], op0=ALU.add, op1=ALU.mult),
                  r=["modT", "gains"], w=[("cols", i, "A")])
            sc.op("dve", lambda e: e.tensor_copy(out=B, in_=modT[:, sh * 16:(sh + 1) * 16]), r=["modT"], w=[("cols", i, "B")])
        if "G" in parts:
            sc.op("dve", lambda e: e.tensor_scalar(out=G, in0=modT[:, gt * 16:(gt + 1) * 16], scalar1=gscale, scalar2=None,
                                                    op0=ALU.mult), r=["modT"], w=[("cols", i, "G")])

    def ada_tile(rowbank, colbank):
        g = ada_g[0]
        if g >= 36:
            return
        ada_g[0] += 1
        wt, wk = wnext()
        b = g % ada_nb[0]
        for kc in range(KC):
            mm(ps[rowbank][0:1, :], cact[:, kc:kc + 1], wt[:, kc * 512:(kc + 1) * 512], kc == 0, kc == KC - 1,
               r=[wk, "cact"], w=[PSK[rowbank]], sig=(kc == KC - 1))
        sc.op("act", lambda e: e.copy(out=rowseg[b], in_=ps[rowbank][0:1, :]), r=[PSK[rowbank]], w=[("rowseg", b)])
        for q4 in range(4):
            mm(ps[colbank][:, q4:q4 + 1], rowseg[b][:, q4 * 128:(q4 + 1) * 128], ones1[0:1, 0:1], True, True,
               r=[("rowseg", b), "ones1"], w=[PSK[colbank]], sig=(q4 == 3))
        sc.op("dve", lambda e: e.tensor_tensor(out=modT[:, g * 4:g * 4 + 4], in0=ps[colbank][:, 0:4], in1=small["badaT"][:, g * 4:g * 4 + 4], op=ALU.add),
              r=[PSK[colbank], "badaT"], w=["modT"])
        if g + 1 == 8:
            derive(0, "AB")
        elif g + 1 == 12:
            derive(0, "G")
        elif g + 1 == 24:
            derive(1, "ABG")
        elif g + 1 == 36:
            derive(2, "ABG")

    NADA_FFN1 = 24
    sqh = [sqb[0][:, 0:TH], sqb[0][:, TH:T], sqb[1][:, 0:TH], sqb[1][:, TH:T]]
    tmh = [tmpf[0][:, 0:TH], tmpf[0][:, TH:T], tmpf[1][:, 0:TH], tmpf[1][:, TH:T]]
    nm_cnt = [0]

    def norm_stats(t):
        for kc in range(KC):
            qi = kc % 4
            sc.op("act", lambda e: e.activation(out=sqh[qi], in_=xT[:, kc, t * TH:(t + 1) * TH], func=AF.Square),
                  r=[Xk(kc)], w=[("sqh", qi)])
            mm(ps[t][:], onesD[:], sqh[qi], kc == 0, kc == KC - 1, r=[("sqh", qi), "onesD"], w=[PSK[t]], sig=True)
        rs = rstd[:, t * TH:(t + 1) * TH]
        sc.op("dve", lambda e: e.tensor_scalar(out=rs, in0=ps[t][:], scalar1=EPS, scalar2=None, op0=ALU.add),
              r=[PSK[t]], w=[("rstd", t)])
        sc.op("act", lambda e: e.activation(out=rs, in_=rs, func=AF.Ln), r=[("rstd", t)], w=[("rstd", t)])
        sc.op("act", lambda e: e.activation(out=rs, in_=rs, func=AF.Exp, scale=-0.5), r=[("rstd", t)], w=[("rstd", t)])

    if "ffn1" in stages:
        norm_stats(0)
        norm_stats(1)
    for g in range(NADA0):
        ada_tile(4 + g % 2, 6 + g % 2)
    if "ffn1" not in stages:
        while ada_g[0] < NADA_FFN1:
            ada_tile(ada_g[0] % 2, 2 + ada_g[0] % 2)
    if "mixer" not in stages:
        while ada_g[0] < 36:
            ada_tile(ada_g[0] % 2, 2 + ada_g[0] % 2)
    debug_dump("modT", modT[:], [P, 144], ["modT"])

    def colA(i, kc):
        return cols[:, (3 * i) * 16 + kc:(3 * i) * 16 + kc + 1]

    def colB(i, kc):
        return cols[:, (3 * i + 1) * 16 + kc:(3 * i + 1) * 16 + kc + 1]

    def colG(i, kc):
        return cols[:, (3 * i + 2) * 16 + kc:(3 * i + 2) * 16 + kc + 1]

    def rms_stats(src, nblk, ones_t, keyf, banks):
        for kc in range(nblk):
            q = sqb[kc % 2]
            sc.op("act", lambda e: e.activation(out=q[:], in_=src(kc), func=AF.Square), r=[keyf(kc)], w=[("sqb", kc % 2)])
            for t in range(2):
                mm(ps[banks[t]][:], ones_t[:], q[:, t * TH:(t + 1) * TH], kc == 0, kc == nblk - 1,
                   r=[("sqb", kc % 2), "onesD", "ones256"], w=[PSK[banks[t]]], sig=True)
        for t in range(2):
            rs = rstd[:, t * TH:(t + 1) * TH]
            sc.op("dve", lambda e: e.tensor_scalar(out=rs, in0=ps[banks[t]][:], scalar1=EPS, scalar2=None, op0=ALU.add),
                  r=[PSK[banks[t]]], w=["rstd"])
            sc.op("act", lambda e: e.activation(out=rs, in_=rs, func=AF.Ln), r=["rstd"], w=["rstd"])
            sc.op("act", lambda e: e.activation(out=rs, in_=rs, func=AF.Exp, scale=-0.5), r=["rstd"], w=["rstd"])

    def norm_apply(i, t):
        rs = rstd[:, t * TH:(t + 1) * TH]
        for kc in range(KC):
            ti = nm_cnt[0] % 4
            nm_cnt[0] += 1
            sc.op("dve", lambda e: e.scalar_tensor_tensor(out=tmh[ti], in0=xT[:, kc, t * TH:(t + 1) * TH], scalar=colA(i, kc), in1=rs,
                                                           op0=ALU.mult, op1=ALU.mult),
                  r=[Xk(kc), ("rstd", t), ("cols", i, "A")], w=[("tmh", ti)])
            sc.op("act", lambda e: e.activation(out=hT[:, kc, t * TH:(t + 1) * TH], in_=tmh[ti], func=AF.Identity, bias=colB(i, kc), scale=1.0),
                  r=[("tmh", ti), ("cols", i, "B")], w=[Hk(kc, t)])

    def norm_mod(i, stats_done=False):
        for t in range(2):
            if not stats_done:
                norm_stats(t)
            norm_apply(i, t)

    def ffn(i, act, sgt, hook=lambda: None):
        gu_rot = Rot([(0, 1), (2, 3)])
        d_rot = Rot([4, 5])
        sg_rot = Rot([0, 1])
        jbase = [sum(CHUNKS[:c]) for c in range(NCH)]
        cur = {"tile": None, "key": None, "idx": -1}

        def GU(c):
            for jl in range(CHUNKS[c]):
                j = jbase[c] + jl
                if j // 2 != cur["idx"]:
                    if cur["idx"] >= 0:
                        hook()
                    cur["tile"], cur["key"] = wnext()
                    cur["idx"] = j // 2
                wt, wk = cur["tile"], cur["key"]
                wj = j % 2
                for t in range(2):
                    bg, bu = gu_rot.next()
                    for which, bnk in ((0, bg), (1, bu)):
                        for kc in range(KC):
                            o = kc * 512 + wj * 256 + which * 128
                            mm(ps[bnk][:], wt[:, o:o + 128], hT[:, kc, t * TH:(t + 1) * TH], kc == 0, kc == KC - 1,
                               r=[wk, Hk(kc, t)], w=[PSK[bnk]], sig=(kc == KC - 1))
                    s = sg_rot.next()
                    sc.op("act", lambda e: e.activation(out=sgt[s], in_=ps[bg][:], func=AF.Silu), r=[PSK[bg]], w=[("sgt", s)])
                    sc.op("dve", lambda e: e.tensor_tensor(out=act[c % 2][:, jl, t * TH:(t + 1) * TH], in0=sgt[s],
                                                            in1=ps[bu][:], op=ALU.mult),
                          r=[("sgt", s), PSK[bu]], w=[("act", c % 2, jl, t)])

        def DOWN(c):
            nj = CHUNKS[c]
            for mg in range(4):
                hook()
                wt, wk = wnext()
                for m in range(4):
                    mmi = mg * 4 + m
                    for t in range(2):
                        bnk = d_rot.next()
                        for jl in range(nj):
                            o = jl * 512 + m * 128
                            mm(ps[bnk][:], wt[:, o:o + 128], act[c % 2][:, jl, t * TH:(t + 1) * TH], jl == 0, jl == nj - 1,
                               r=[wk, ("act", c % 2, jl, t)], w=[PSK[bnk]], sig=(jl == nj - 1))
                        xs = xT[:, mmi, t * TH:(t + 1) * TH]
                        sc.op("dve", lambda e: e.scalar_tensor_tensor(out=xs, in0=ps[bnk][:], scalar=colG(i, mmi), in1=xs,
                                                                       op0=ALU.mult, op1=ALU.add),
                              r=[PSK[bnk], ("cols", i, "G"), Xk(mmi)], w=[Xk(mmi)])

        GU(0)
        for c in range(1, NCH):
            GU(c)
            DOWN(c - 1)
        DOWN(NCH - 1)
        hook()

    def run_ffn(i):
        sc.barrier()
        norm_mod(i, stats_done=(i == 0))
        a0 = carve(YB, 0, 12288, BF16).rearrange("p (j t) -> p j t", j=MAXNJ)
        a1 = carve(YB, 12288, 12288, BF16).rearrange("p (j t) -> p j t", j=MAXNJ)
        s0 = carve(YB, 24576, 2048, F32)
        s1 = carve(YB, 26624, 2048, F32)
        ws.pf = 2
        if i == 0:
            ffn(i, [a0, a1], [s0, s1], hook=lambda: (ada_tile(6, 7) if ada_g[0] < NADA_FFN1 else None))
            assert ada_g[0] == NADA_FFN1, ada_g[0]
        else:
            ffn(i, [a0, a1], [s0, s1])
        sc.barrier()

    if "ffn1" in stages:
        run_ffn(0)
    debug_dump("x1", xT[:].rearrange("p a b -> p (a b)"), [P, KC * T], [Xk(k) for k in range(KC)])

    def mixer():
        ws.pf = 1
        norm_mod(1)
        sc.dma("sp", xspill, xT[:].rearrange("p a b -> p (a b)"), r=[Xk(k) for k in range(KC)], w=["xspill"], dsem=dq)
        XALL = [Xk(k) for k in range(KC)]
        yo = [0]

        def ytab(d_ap, n, name):
            t = carve(YB, yo[0], n * 4, F32)
            yo[0] += n * 4
            sc.dma("sp", t, d_ap, w=[name], dsem=dq2)
            return t

        rope = ytab(rope_d, 2 * T, "rope")
        qdec = ytab(qdec_d, 1024, "qdec")
        retmask = ytab(retmask_d, 1024, "retmask")
        kdec = ytab(kdec_d, 72, "kdec")
        swamask = ytab(swamask_d, 256, "swamask")
        sel = ytab(sel_d, 16, "sel")
        coef = ytab(coef_d, 32, "coef")
        assert yo[0] <= 18432
        cosT = rope[:, 0:T]
        sinT = rope[:, T:2 * T]
        ZOFF = 18432
        zT = carve(YB, ZOFF, 32768, BF16).rearrange("p (k t) -> p k t", k=KC)
        Sbf = carve(YB, 51200, 4096, BF16).rearrange("p (h e) -> p h e", h=8)
        khL = carve(YB, 55296, 1024, BF16).rearrange("p (n d) -> p n d", n=4)
        maskp0 = carve(YB, 56320, 512, F32)
        scb = [carve(YB, 56832 + i * 256, 256, BF16) for i in range(2)]
        svt = carve(YB, ZOFF, 9216, BF16).rearrange("p (n c) -> p n c", n=9)
        payload = carve(YB, ZOFF + 9216, 12288, F32)
        slot = carve(YB, ZOFF + 21504, 6144, F32)
        halo = carve(YB, ZOFF + 27648, 4096, F32)
        NPB = 3
        probs = [carve(YB, ZOFF + 9216 + i * 2048, 2048, BF16).rearrange("p (k q) -> p k q", k=2) for i in range(NPB)]
        ebuf = [carve(YB, ZOFF + 9216 + 6144 + i * 1024, 1024, BF16) for i in range(2)]
        dtmp = [carve(YB, ZOFF + 9216 + 8192 + i * 2048, 2048, F32) for i in range(2)]
        krT = carve(XB, 0, 16384, BF16).rearrange("p (h t) -> p h t", h=8)
        vtm = carve(XB, 16384, 32768, BF16).rearrange("p (n c) -> p n c", n=8)
        skT = carve(XB, 55296, 9216, BF16).rearrange("p (j t) -> p j t", j=4)
        soT = carve(XB, 0, 32768, BF16).rearrange("p (k t) -> p k t", k=KC)
        sqT = carve(XB, 32768, 8192, BF16).rearrange("p (g t) -> p g t", g=4)
        sq0 = carve(XB, 40960, 4096, BF16).rearrange("p (k q) -> p k q", k=16)
        Sst = carve(XB, 45056, 8192, F32).rearrange("p (h e) -> p h e", h=8)
        roG = carve(XB, 0, 32768, BF16).rearrange("p (k t) -> p k t", k=KC)
        ro = carve(XB, 32768, 8192, F32).rearrange("p (e t) -> p e t", e=2)
        krh = carve(XB, 40960, 2048, BF16)
        qr = carve(XB, 43008, 2048, BF16)
        vh = carve(XB, 53248, 4096, BF16).rearrange("p (n e) -> p n e", n=8)
        kh = [carve(XB, 57344 + i * 2048, 2048, BF16).rearrange("p (n d) -> p n d", n=8) for i in range(2)]
        qd = carve(XB, 61440, 2048, BF16)
        rt = [tmpf[0][:, 0:TH], tmpf[0][:, TH:T]]
        rowseg[0] = carve(XB, 53248, 2048, F32)[0:1, :]
        ada_nb[0] = 1
        sgr = [tmpf[1][:, 0:TH], tmpf[1][:, TH:T]]
        krT_sp = nc.dram_tensor("krT_sp", [P, 8192], BF16).ap()
        v_sp = nc.dram_tensor("v_sp", [P, 8 * 2048], BF16).ap()
        K_krT = lambda h: ("krT", h)
        K_v = lambda n: ("v", n)

        def rotary_from(bn, bs_, t, dst, dkeys):
            sc.op("dve", lambda e: e.tensor_tensor(out=rt[0], in0=ps[bn][:], in1=cosT[:, t * TH:(t + 1) * TH], op=ALU.mult),
                  r=[PSK[bn], "rope"], w=[("rt", 0)])
            sc.op("dve", lambda e: e.tensor_tensor(out=rt[1], in0=ps[bs_][:], in1=sinT[:, t * TH:(t + 1) * TH], op=ALU.mult),
                  r=[PSK[bs_], "rope"], w=[("rt", 1)])
            sc.op("pool", lambda e: e.tensor_tensor(out=dst, in0=rt[0], in1=rt[1], op=ALU.add),
                  r=[("rt", 0), ("rt", 1)], w=dkeys)

        def proj_fm(wt, wk, blk, nblk, t, bnk):
            for kc in range(KC):
                o = kc * (nblk * 128) + blk * 128
                mm(ps[bnk][:], wt[:, o:o + 128], hT[:, kc, t * TH:(t + 1) * TH], kc == 0, kc == KC - 1,
                   r=[wk, Hk(kc, t)], w=[PSK[bnk]], sig=(kc == KC - 1))

        prot = Rot([(0, 1), (2, 3)])
        vrot = Rot([4, 5, 6, 7])
        for tl in range(4):
            wt, wk = wnext()
            for hh in range(2):
                h = tl * 2 + hh
                for t in range(2):
                    bn, bs_ = prot.next()
                    proj_fm(wt, wk, hh * 2, 4, t, bn)
                    proj_fm(wt, wk, hh * 2 + 1, 4, t, bs_)
                    rotary_from(bn, bs_, t, krT[:, h, t * TH:(t + 1) * TH], [K_krT(h)] + XALL)
        for tl in range(4):
            wt, wk = wnext()
            for n in range(8):
                bnk = vrot.next()
                for kc in range(KC):
                    mm(ps[bnk][:], hT[:, kc, n * 128:(n + 1) * 128], wt[:, kc * 512:(kc + 1) * 512], kc == 0, kc == KC - 1,
                       r=[wk, Hk(kc, n // 4)], w=[PSK[bnk]], sig=(kc == KC - 1))
                sc.op("act", lambda e: e.copy(out=vtm[:, n, tl * 512:(tl + 1) * 512], in_=ps[bnk][:]), r=[PSK[bnk]], w=[K_v(n)] + XALL)
        wt, wk = wnext()
        for j in range(4):
            for t in range(2):
                bnk = prot.next()[0]
                proj_fm(wt, wk, j, 4, t, bnk)
                sc.op("act", lambda e: e.copy(out=skT[:, j, 128 + t * TH:128 + (t + 1) * TH], in_=ps[bnk][:]), r=[PSK[bnk]], w=[("skT", j)] + XALL)
        wt, wk = wnext()
        for n in range(8):
            bnk = vrot.next()
            for kc in range(KC):
                mm(ps[bnk][:], hT[:, kc, n * 128:(n + 1) * 128], wt[:, kc * 512:(kc + 1) * 512], kc == 0, kc == KC - 1,
                   r=[wk, Hk(kc, n // 4)], w=[PSK[bnk]], sig=(kc == KC - 1))
            sc.op("act", lambda e: e.copy(out=svt[:, n + 1, :], in_=ps[bnk][:]), r=[PSK[bnk]], w=[("svt", n + 1)])

        for h in range(8):
            lb = h // 2
            lcol = (h % 2) * 256
            for half in range(2):
                bnk = vrot.next()
                for n4 in range(4):
                    n = half * 4 + n4
                    mm(ps[bnk][:, n4 * 128:(n4 + 1) * 128], krT[:, h, n * 128:(n + 1) * 128], ident[:], True, True,
                       r=[K_krT(h), "ident"], w=[PSK[bnk]], sig=(n4 == 3))
                kl = kdec[:, 8 + h * 8 + half * 4: 8 + h * 8 + half * 4 + 4]
                sc.op("dve", lambda e: e.tensor_tensor(out=khL, in0=ps[bnk][:].rearrange("p (n d) -> p n d", n=4),
                                                        in1=kl.unsqueeze(2).broadcast_to([P, 4, 128]), op=ALU.mult),
                      r=[PSK[bnk], "kdec"], w=["khL"])
                for n4 in range(4):
                    n = half * 4 + n4
                    mm(ps[lb][:, lcol:lcol + 256], khL[:, n4, :], vtm[:, n, h * 256:(h + 1) * 256], n == 0, n == 7,
                       r=["khL", K_v(n)], w=[PSK[lb]], sig=True)
            if h % 2 == 1:
                sc.op("act", lambda e: e.copy(out=payload[:, lb * 512:(lb + 1) * 512], in_=ps[lb][:]), r=[PSK[lb]], w=["payload"])
        sc.op("act", lambda e: e.copy(out=payload[:, 2048:2560].rearrange("p (j t) -> p j t", j=4), in_=skT[:, :, 1024:1152]),
              r=[("skT", j) for j in range(4)], w=["payload"])
        sc.op("act", lambda e: e.copy(out=payload[:, 2560:3072], in_=svt[:, 8, :]), r=[("svt", 8)], w=["payload"])
        for i4 in range(4):
            for hf in range(2):
                sc.op("dve", lambda e: e.tensor_scalar(out=slot, in0=payload[:, hf * 1536:(hf + 1) * 1536], scalar1=sel[:, i4:i4 + 1],
                                                        scalar2=None, op0=ALU.mult), r=["payload", "sel"], w=["slot"])
                sc.dma("sp", bounce[hf].ap()[i4 * P:(i4 + 1) * P, :], slot, r=["slot"], w=["bounce"], dsem=dq)
        sc.dma("sp", krT_sp, carve(XB, 0, 16384, BF16), r=[K_krT(h) for h in range(8)], w=["krT_sp"], dsem=dq2)
        sc.dma("sp", v_sp, carve(XB, 16384, 32768, BF16), r=[K_v(n) for n in range(8)], w=["v_sp"], dsem=dq2)
        ccsem = stack.enter_context(nc.semaphore("ccsem"))
        ccd = DSem(ccsem, "cc")
        Epool = sc.E["pool"]
        sc._deps(Epool, ["bounce"], ["gath"])
        for hf in range(2):
            nc.gpsimd.collective_compute("AllReduce", ALU.add, replica_groups=[[0, 1, 2, 3], [4, 5, 6, 7]],
                                         ins=[bounce[hf].ap().opt()], outs=[gath[hf].ap().opt()]).then_inc(ccsem)
        ccd.cnt = 2
        sc._record((ccd, 2), ["bounce"], ["gath"])
        sc.barrier()

        lrot = Rot([(0, 1), (2, 3)])
        nrot = Rot([(4, 5), (6, 7)])
        g4 = lambda ap: ap.rearrange("p (g q) -> p g q", g=4)

        prrot = Rot(list(range(NPB)))
        dcnt = [0]

        def swa_steps(blocks, hook_every=0):
            SKEW = 2
            steps = []
            for blk in blocks:
                bnum, bden = nrot.next()
                for s in range(2):
                    steps.append((blk, s, bnum, bden))
            st = {}

            def front(i):
                (j, n, qsrc, qkeys, mprev, mkeys), s, bnum, bden = steps[i]
                r0, r1 = s * 64, (s + 1) * 64
                pb = prrot.next()
                blg = lrot.next()
                st[i] = pb
                for kb in range(2):
                    mm(g4(ps[blg[kb]][:]), skT[r0:r1, j, (n + kb) * 128:(n + kb + 1) * 128], qsrc[r0:r1],
                       True, True, r=[("skT", j)] + qkeys, w=[PSK[blg[kb]]], sig=True)
                    sc.op("act", lambda e: e.activation(out=ebuf[kb], in_=ps[blg[kb]][:], func=AF.Exp, scale=0.125),
                          r=[PSK[blg[kb]]], w=[("ebuf", kb)])
                    mk = mprev if kb == 0 else swamask[:, 128:256]
                    sc.op("dve" if kb == 0 else "pool",
                          lambda e: e.tensor_tensor(out=g4(probs[pb][:, kb, :]), in0=g4(ebuf[kb]),
                                                    in1=mk.unsqueeze(1).broadcast_to([P, 4, 128]), op=ALU.mult),
                          r=[("ebuf", kb), "swamask"] + mkeys, w=[("probs", pb, kb)])

            def back(i):
                (j, n, qsrc, qkeys, mprev, mkeys), s, bnum, bden = steps[i]
                r0, r1 = s * 64, (s + 1) * 64
                pb = st.pop(i)
                hk = 2 * j + s
                for kb in range(2):
                    mm(ps[bnum][r0:r1, :], svt[:, n + kb, hk * 64:(hk + 1) * 64], probs[pb][:, kb, :], kb == 0, kb == 1,
                       r=[("svt", n + kb), ("probs", pb, kb)], w=[PSK[bnum]], sig=(kb == 1))
                for kb in range(2):
                    mm(ps[bden][r0:r1, :], ones64[:], probs[pb][:, kb, :], kb == 0, kb == 1,
                       r=["ones64", ("probs", pb, kb)], w=[PSK[bden]], sig=(kb == 1))
                if s == 1:
                    di = dcnt[0] % 2
                    dcnt[0] += 1
                    dt_ = dtmp[di]
                    for g in range(4):
                        sc.op("act", lambda e: e.activation(out=dt_[:, g * 128:(g + 1) * 128], in_=ps[bden][:, g * 128:(g + 1) * 128], func=AF.Ln,
                                                            bias=sinkE[:, j * 4 + g:j * 4 + g + 1], scale=1.0),
                              r=[PSK[bden], "sinkE"], w=[("dtmp", di)])
                    sc.op("act", lambda e: e.activation(out=dt_, in_=dt_, func=AF.Exp, scale=-1.0), r=[("dtmp", di)], w=[("dtmp", di)])
                    sc.op("dve", lambda e: e.tensor_tensor(out=soT[:, j * 4:(j + 1) * 4, n * 128:(n + 1) * 128],
                                                            in0=g4(ps[bnum][:]), in1=g4(dt_), op=ALU.mult),
                          r=[PSK[bnum], ("dtmp", di)], w=[("so", j * 4 + g) for g in range(4)])

            N = len(steps)
            nh = 0
            for i in range(N + SKEW):
                if i < N:
                    front(i)
                if i - SKEW >= 0:
                    back(i - SKEW)
                if hook_every and i % hook_every == hook_every - 1 and nh < 3:
                    nh += 1
                    bk = vrot.next()
                    ada_tile(bk, bk)
            while hook_every and nh < 3:
                nh += 1
                bk = vrot.next()
                ada_tile(bk, bk)

        def receive():
            first = {}

            def acc(eng, dst, src, scal, key, rkeys):
                if key not in first:
                    first[key] = 1
                    sc.op(eng, lambda e: e.tensor_scalar(out=dst, in0=src, scalar1=scal, scalar2=None, op0=ALU.mult),
                          r=["slot"] + rkeys, w=[key])
                else:
                    sc.op(eng, lambda e: e.scalar_tensor_tensor(out=dst, in0=src, scalar=scal, in1=dst, op0=ALU.mult, op1=ALU.add),
                          r=["slot", key] + rkeys, w=[key])

            for i4 in range(4):
                for hf in range(2):
                    sc.dma("sp", slot, gath[hf].ap()[i4 * P:(i4 + 1) * P, :], r=["gath"], w=["slot"], dsem=dq)
                    heads = range(6) if hf == 0 else range(6, 8)
                    for h in heads:
                        o = h * 256 - hf * 1536
                        acc("dve", Sst[:, h, :], slot[:, o:o + 256], coef[:, i4 * 8 + h:i4 * 8 + h + 1], ("Sst", h), ["coef"])
                    if hf == 1:
                        acc("dve", halo, slot[:, 512:1536], sel[:, 4 + i4:5 + i4], "halo", ["sel"])
            sc.op("act", lambda e: e.copy(out=skT[:, :, 0:128], in_=halo[:, 0:512].rearrange("p (j t) -> p j t", j=4)),
                  r=["halo"], w=[("skT", j) for j in range(4)])
            sc.op("act", lambda e: e.copy(out=svt[:, 0, :], in_=halo[:, 512:1024]), r=["halo"], w=[("svt", 0)])
            for h in range(8):
                sc.op("act", lambda e: e.copy(out=Sbf[:, h, :], in_=Sst[:, h, :]), r=[("Sst", h)], w=[("Sbf", h)])
            sc.op("dve", lambda e: e.tensor_scalar(out=maskp0, in0=swamask[:, 0:128], scalar1=sel[:, 8:9], scalar2=None, op0=ALU.mult),
                  r=["swamask", "sel"], w=["maskp0"])

        for j in range(4):
            if j == 3:
                receive()
            wt, wk = wnext()
            for g in range(4):
                for t in range(2):
                    bnk = vrot.next()
                    proj_fm(wt, wk, g, 4, t, bnk)
                    sc.op("act", lambda e: e.copy(out=sqT[:, g, t * TH:(t + 1) * TH], in_=ps[bnk][:]), r=[PSK[bnk]], w=[("sqT", g)])
            sc.op("pool", lambda e: e.tensor_copy(out=sq0[:, j * 4:(j + 1) * 4, :], in_=sqT[:, :, 0:128]),
                  r=[("sqT", g) for g in range(4)], w=[("sq0", j)])
            swa_steps([(j, n, sqT[:, :, n * 128:(n + 1) * 128], [("sqT", g) for g in range(4)], swamask[:, 0:128], []) for n in range(1, 8)],
                      hook_every=4)

        swa_steps([(j, 0, sq0[:, j * 4:(j + 1) * 4, :], [("sq0", j)], maskp0, ["maskp0"]) for j in range(4)])
        debug_dump("soT", carve(XB, 0, 32768, BF16), [P, KC * T], [("so", k) for k in range(KC)])
        sc.barrier()

        def gated_branch(src, skeys, accumulate):
            grot = Rot([(0, 1), (2, 3), (4, 5), (6, 7)])
            for mp in range(8):
                wt, wk = wnext()
                for m2 in range(2):
                    mmi = mp * 2 + m2
                    for t in range(2):
                        bg, by = grot.next()
                        proj_fm(wt, wk, m2, 4, t, bg)
                        for kc in range(KC):
                            o = kc * 512 + (2 + m2) * 128
                            mm(ps[by][:], wt[:, o:o + 128], src[:, kc, t * TH:(t + 1) * TH], kc == 0, kc == KC - 1,
                               r=[wk, skeys(kc)], w=[PSK[by]], sig=(kc == KC - 1))
                        s = t
                        sc.op("act", lambda e: e.activation(out=rt[s], in_=ps[bg][:], func=AF.Sigmoid), r=[PSK[bg]], w=[("rt", s)])
                        zs = zT[:, mmi, t * TH:(t + 1) * TH]
                        if not accumulate:
                            sc.op("dve", lambda e: e.tensor_tensor(out=zs, in0=rt[s], in1=ps[by][:], op=ALU.mult),
                                  r=[("rt", s), PSK[by]], w=[("z", mmi)])
                        else:
                            sc.op("dve", lambda e: e.tensor_tensor(out=rt[s], in0=rt[s], in1=ps[by][:], op=ALU.mult),
                                  r=[("rt", s), PSK[by]], w=[("rt", s)])
                            sc.op("pool", lambda e: e.tensor_tensor(out=zs, in0=rt[s], in1=zs, op=ALU.add),
                                  r=[("rt", s), ("z", mmi)], w=[("z", mmi)])

        gated_branch(soT, lambda kc: ("so", kc), False)
        sc.barrier()

        gam = [1.0 - 2.0 ** (-5.0 - h) for h in range(8)]
        for h in range(8):
            wq, wqk = wnext()
            wg, wgk = wq, wqk
            if True:
                sc.dma("sp", krh, krT_sp[:, h * T:(h + 1) * T], r=["krT_sp"], w=["krh"], dsem=dq)
                sc.dma("sp", vh, v_sp.rearrange("p (n c) -> p n c", n=8)[:, :, h * 256:(h + 1) * 256], r=["v_sp"], w=["vh"], dsem=dq2)
                for t in range(2):
                    bn, bs_ = prot.next()
                    proj_fm(wq, wqk, 0, 4, t, bn)
                    proj_fm(wq, wqk, 1, 4, t, bs_)
                    rotary_from(bn, bs_, t, qr[:, t * TH:(t + 1) * TH], ["qr"])
                    sc.op("pool", lambda e: e.tensor_tensor(out=qd[:, t * TH:(t + 1) * TH].rearrange("p (n i) -> p n i", n=4),
                                                             in0=qr[:, t * TH:(t + 1) * TH].rearrange("p (n i) -> p n i", n=4),
                                                             in1=qdec[:, h * 128:(h + 1) * 128].unsqueeze(1).broadcast_to([P, 4, 128]),
                                                             op=ALU.mult), r=["qr", "qdec"], w=["qd"])
                khh = kh[h % 2]
                for half in range(2):
                    bnk = vrot.next()
                    for n4 in range(4):
                        n = half * 4 + n4
                        mm(ps[bnk][:, n4 * 128:(n4 + 1) * 128], krh[:, n * 128:(n + 1) * 128], ident[:], True, True,
                           r=["krh", "ident"], w=[PSK[bnk]], sig=(n4 == 3))
                    sc.op("act", lambda e: e.activation(out=khh[:, half * 4:(half + 1) * 4, :],
                                                        in_=ps[bnk][:].rearrange("p (n d) -> p n d", n=4), func=AF.Identity,
                                                        scale=kdec[:, h:h + 1]), r=[PSK[bnk], "kdec"], w=[("kh", h % 2)])
                g128 = gam[h] ** 128

                def emit_sc(n):
                    b_sc = vrot.next()
                    mm(ps[b_sc][:, 0:128], krh[:, n * 128:(n + 1) * 128], qr[:, n * 128:(n + 1) * 128], True, True,
                       r=["krh", "qr"], w=[PSK[b_sc]], sig=True)
                    sb_ = scb[n % 2]
                    sc.op("dve", lambda e: e.tensor_tensor(out=sb_, in0=ps[b_sc][:, 0:128], in1=retmask[:, h * 128:(h + 1) * 128], op=ALU.mult),
                          r=[PSK[b_sc], "retmask"], w=[("scb", n % 2)])

                emit_sc(0)
                for n in range(8):
                    b_kv = None
                    if n < 7:
                        b_kv = vrot.next()
                        mm(ps[b_kv][:, 0:256], khh[:, n, :], vh[:, n, :], True, True,
                           r=[("kh", h % 2), "vh"], w=[PSK[b_kv]], sig=True)
                    sb_ = scb[n % 2]
                    b_o = vrot.next()
                    for eh in range(2):
                        mm(ps[b_o][:, eh * 128:(eh + 1) * 128], vh[:, n, eh * 128:(eh + 1) * 128], sb_, True, False,
                           r=["vh", ("scb", n % 2)], w=[PSK[b_o]], sig=False)
                        mm(ps[b_o][:, eh * 128:(eh + 1) * 128], Sbf[:, h, eh * 128:(eh + 1) * 128], qd[:, n * 128:(n + 1) * 128], False, True,
                           r=[("Sbf", h), "qd"], w=[PSK[b_o]], sig=True)
                    if n < 7:
                        emit_sc(n + 1)
                        sc.op("dve", lambda e: e.scalar_tensor_tensor(out=Sst[:, h, :], in0=Sst[:, h, :], scalar=float(g128), in1=ps[b_kv][:, 0:256],
                                                                       op0=ALU.mult, op1=ALU.add), r=[("Sst", h), PSK[b_kv]], w=[("Sst", h)])
                        sc.op("act", lambda e: e.copy(out=Sbf[:, h, :], in_=Sst[:, h, :]), r=[("Sst", h)], w=[("Sbf", h)])
                    sc.op("act", lambda e: e.copy(out=ro[:, :, n * 128:(n + 1) * 128], in_=ps[b_o][:, 0:256].rearrange("p (e q) -> p e q", e=2)),
                          r=[PSK[b_o]], w=["ro"])
                rms_stats(lambda eh: ro[:, eh, :], 2, ones256, lambda eh: "ro", (0, 1))
                for eh in range(2):
                    for t in range(2):
                        bnk = vrot.next()
                        proj_fm(wg, wgk, 2 + eh, 4, t, bnk)
                        s = t
                        sc.op("act", lambda e: e.activation(out=sgr[s], in_=ps[bnk][:], func=AF.Silu), r=[PSK[bnk]], w=[("sgr", s)])
                        sc.op("dve", lambda e: e.tensor_tensor(out=sgr[s], in0=sgr[s], in1=rstd[:, t * TH:(t + 1) * TH], op=ALU.mult),
                              r=[("sgr", s), "rstd"], w=[("sgr", s)])
                        sc.op("pool", lambda e: e.tensor_tensor(out=roG[:, h * 2 + eh, t * TH:(t + 1) * TH], in0=sgr[s],
                                                                 in1=ro[:, eh, t * TH:(t + 1) * TH], op=ALU.mult),
                              r=[("sgr", s), "ro"], w=[("roG", h * 2 + eh)])
        debug_dump("roG", carve(XB, 0, 32768, BF16), [P, KC * T], [("roG", k) for k in range(KC)])
        debug_dump("krh", krh, [P, T], ["krh"])
        debug_dump("qr", qr, [P, T], ["qr"])
        debug_dump("qd", qd, [P, T], ["qd"])
        debug_dump("vh", carve(XB, 53248, 4096, BF16), [P, 2048], ["vh"])
        debug_dump("ro", carve(XB, 32768, 8192, F32), [P, 2048], ["ro"])
        debug_dump("Sst", carve(XB, 45056, 8192, F32), [P, 2048], [("Sst", h) for h in range(8)])
        debug_dump("kh", carve(XB, 57344 + 2048, 2048, BF16), [P, 1024], [("kh", 1)])
        sc.barrier()
        gated_branch(roG, lambda kc: ("roG", kc), True)
        sc.barrier()
        sc.dma("sp", xT[:].rearrange("p a b -> p (a b)"), xspill, r=["xspill"], w=[Xk(k) for k in range(KC)], dsem=dq)
        orot = Rot([0, 1, 2, 3, 4, 5, 6, 7])
        for mg in range(4):
            wt, wk = wnext()
            for m in range(4):
                mmi = mg * 4 + m
                for t in range(2):
                    bnk = orot.next()
                    for kc in range(KC):
                        o = kc * 512 + m * 128
                        mm(ps[bnk][:], wt[:, o:o + 128], zT[:, kc, t * TH:(t + 1) * TH], kc == 0, kc == KC - 1,
                           r=[wk, ("z", kc)], w=[PSK[bnk]], sig=(kc == KC - 1))
                    xs = xT[:, mmi, t * TH:(t + 1) * TH]
                    sc.op("dve", lambda e: e.scalar_tensor_tensor(out=xs, in0=ps[bnk][:], scalar=colG(1, mmi), in1=xs,
                                                                   op0=ALU.mult, op1=ALU.add), r=[PSK[bnk], ("cols", 1, "G"), Xk(mmi)], w=[Xk(mmi)])
        sc.barrier()

    if "mixer" in stages:
        mixer()
    debug_dump("x2", xT[:].rearrange("p a b -> p (a b)"), [P, KC * T], [Xk(k) for k in range(KC)])
    assert wpos[0] == W_FFN2, (wpos[0], W_FFN2)
    if "ffn2" in stages:
        run_ffn(2)

    sc.barrier()
    dout = [sc.new_dsem("out%d" % i) for i in range(4)]
    for t in range(2):
        norm_stats(t)
        rs = rstd[:, t * TH:(t + 1) * TH]
        for kc in range(KC):
            ti = nm_cnt[0] % 4
            nm_cnt[0] += 1
            sc.op("dve", lambda e: e.scalar_tensor_tensor(out=tmh[ti], in0=xT[:, kc, t * TH:(t + 1) * TH], scalar=gains[:, 48 + kc:49 + kc], in1=rs,
                                                           op0=ALU.mult, op1=ALU.mult), r=[Xk(kc), ("rstd", t), "gains"], w=[("tmh", ti)])
            sc.dma("sp", yT_d[:, kc * T + t * TH:kc * T + (t + 1) * TH], tmh[ti], r=[("tmh", ti)], w=[("yT", kc, t)], dsem=dout[ti])
    sc.final_wait("sp", [("yT", kc, t) for kc in range(KC) for t in range(2)])
    E = sc.E["sp"]
    for d in dout:
        sc._wait(E, (d, d.cnt))
    for name, d in dbg_out.items():
        sc._wait(E, (d, d.cnt))
    assert wpos[0] == len(seq), (wpos[0], len(seq))
    stack.close()
    return nc


def _tile_lhs(w, blocks):
    nb = len(blocks)
    cols = np.concatenate(blocks)
    sub = w[:, cols].reshape(KC, P, nb * 128)
    return np.ascontiguousarray(sub.transpose(1, 0, 2)).reshape(P, KC * nb * 128)


def _prep_shared(inp):
    f = lambda a: np.asarray(a, dtype=np.float32)
    sh = {}
    w_ada = f(inp["w_ada"])[0]
    sh["wada"] = np.ascontiguousarray(w_ada.reshape(KC, P, 36, 512).transpose(2, 1, 0, 3)).reshape(36, P, WCOLS)
    for i, (gu, dn) in enumerate((("w_ffn1_gu", "w_ffn1_down"), ("w_ffn2_gu", "w_ffn2_down"))):
        wgu = f(inp[gu])[0]
        tiles = []
        for tt in range(22):
            blocks = []
            for j in (2 * tt, 2 * tt + 1):
                blocks.append(np.arange(j * 128, (j + 1) * 128))
                blocks.append(DFF + np.arange(j * 128, (j + 1) * 128))
            tiles.append(_tile_lhs(wgu, blocks))
        sh["wgu%d" % (i + 1)] = np.stack(tiles)
        wd = f(inp[dn])[0]
        dt = np.zeros((NCH * 4, P, MAXNJ * 512), np.float32)
        jb = 0
        for c, nj in enumerate(CHUNKS):
            rows = wd[jb * 128:(jb + nj) * 128].reshape(nj, P, 4, 512)
            for mg in range(4):
                dt[c * 4 + mg, :, :nj * 512] = rows[:, :, mg, :].transpose(1, 0, 2).reshape(P, nj * 512)
            jb += nj
        sh["wd%d" % (i + 1)] = dt
    w_in = f(inp["w_in"])[0]
    o_rq, o_rk, o_rv, o_rg, o_sq, o_sk, o_sv, o_gr, o_gs = 0, 1024, 2048, 4096, 6144, 8192, 8704, 9216, 11264
    sw = np.concatenate([np.arange(64, 128), np.arange(0, 64)])
    win = []

    def qk_tiles(off):
        out = []
        for tl in range(4):
            blocks = []
            for hh in range(2):
                h = tl * 2 + hh
                base = off + h * 128
                blocks.append(base + np.arange(128))
                blocks.append(base + sw)
            out.append(_tile_lhs(w_in, blocks))
        return out

    def rhs_tiles(off, ncols):
        out = []
        for tl in range(ncols // 512):
            sub = w_in[:, off + tl * 512: off + (tl + 1) * 512].reshape(KC, P, 512)
            out.append(np.ascontiguousarray(sub.transpose(1, 0, 2)).reshape(P, WCOLS))
        return out

    win += qk_tiles(o_rk)
    win += rhs_tiles(o_rv, 2048)
    win.append(_tile_lhs(w_in, [o_sk + j * 128 + np.arange(128) for j in range(4)]))
    win += rhs_tiles(o_sv, 512)
    for j in range(4):
        blocks = []
        for g in range(4):
            ha, hb = 8 * j + g, 8 * j + 4 + g
            blocks.append(np.concatenate([o_sq + ha * 64 + np.arange(64), o_sq + hb * 64 + np.arange(64)]))
        win.append(_tile_lhs(w_in, blocks))
    for h in range(8):
        win.append(_tile_lhs(w_in, [o_rq + h * 128 + np.arange(128), o_rq + h * 128 + sw,
                                    o_rg + (2 * h) * 128 + np.arange(128), o_rg + (2 * h + 1) * 128 + np.arange(128)]))
    sh["win"] = np.stack(win)
    assert sh["win"].shape[0] == 22
    w_swa = f(inp["w_swa_branch"])[0]
    perm = []
    for j in range(4):
        for g in range(4):
            for s in range(2):
                perm.append((8 * j + 4 * s + g) * 64 + np.arange(64))
    perm = np.concatenate(perm)
    w_swa_p = w_swa[perm]
    w_ret = f(inp["w_ret_branch"])[0]
    w_out = f(inp["w_out"])[0]
    wgb = []
    for goff, wmat in ((o_gs, w_swa_p), (o_gr, w_ret)):
        for mp in range(8):
            A = _tile_lhs(w_in, [goff + (2 * mp + b) * 128 + np.arange(128) for b in range(2)]).reshape(P, KC, 256)
            B = _tile_lhs(wmat, [(2 * mp + b) * 128 + np.arange(128) for b in range(2)]).reshape(P, KC, 256)
            wgb.append(np.concatenate([A, B], axis=2).reshape(P, WCOLS))
    sh["wgb"] = np.stack(wgb)
    sh["wbr"] = np.stack([_tile_lhs(w_out, [(mg * 4 + b) * 128 + np.arange(128) for b in range(4)]) for mg in range(4)])
    gam = np.array([1.0 - 2.0 ** (-5.0 - h) for h in range(8)], np.float64)
    i = np.arange(128)
    qdec = np.stack([gam[h] ** (i + 1.0) for h in range(8)]).reshape(1, 1024)
    sh["qdec"] = np.broadcast_to(qdec, (P, 1024)).astype(np.float32).copy()
    k_ = i[:, None]
    q_ = i[None, :]
    rm = np.concatenate([np.where(q_ >= k_, gam[h] ** np.maximum(q_ - k_, 0) * (128.0 ** -0.5), 0.0) for h in range(8)], axis=1)
    sh["retmask"] = rm.astype(np.float32)
    kd = np.zeros((P, 72), np.float64)
    for h in range(8):
        kd[:, h] = (128.0 ** -0.5) * gam[h] ** (127.0 - i)
        for n in range(8):
            kd[:, 8 + h * 8 + n] = (128.0 ** -0.5) * gam[h] ** (1023.0 - 128.0 * n - i)
    sh["kdec"] = kd.astype(np.float32)
    sm = np.zeros((P, 256), np.float32)
    sm[:, 0:128] = (k_ > q_)
    sm[:, 128:256] = (k_ <= q_)
    sh["swamask"] = sm
    sh["ident"] = np.eye(P, dtype=np.float32)
    gl = np.zeros((P, 64), np.float32)
    for c, nm in enumerate(("norm_ffn1", "norm_mix", "norm_ffn2")):
        gl[:, c * 16:(c + 1) * 16] = f(inp[nm])[0].reshape(KC, P).T
    gl[:, 48:64] = f(inp["norm_final"]).reshape(KC, P).T
    sh["gains"] = gl
    sk = f(inp["sinks"])[0]
    sl = np.zeros((P, 16), np.float32)
    for j in range(4):
        for g in range(4):
            sl[0:64, j * 4 + g] = sk[8 * j + g]
            sl[64:128, j * 4 + g] = sk[8 * j + 4 + g]
    sh["sinksl"] = sl
    return sh, gam


def _prep_core(inp, r, gam):
    f = lambda a: np.asarray(a, dtype=np.float32)
    b, q = r // 4, r % 4
    m = {}
    xs = f(inp["x"])[b, q * T:(q + 1) * T]
    m["xT"] = np.ascontiguousarray(xs.reshape(T, KC, P).transpose(2, 1, 0)).reshape(P, KC * T)
    m["cT"] = np.ascontiguousarray(f(inp["c"])[b].reshape(KC, P).T)
    m["badaT"] = np.ascontiguousarray(f(inp["b_ada"])[0].reshape(144, P).T)
    half = 64
    inv_freq = (1.0 / (np.float32(10000.0) ** (np.arange(half, dtype=np.float32) / np.float32(half)))).astype(np.float32)
    pos = np.arange(q * T, (q + 1) * T, dtype=np.float32)
    ang = (pos[None, :] * inv_freq[:, None]).astype(np.float32)
    cos = np.cos(ang).astype(np.float32)
    sin = np.sin(ang).astype(np.float32)
    rope = np.zeros((P, 2 * T), np.float32)
    rope[0:64, 0:T] = cos
    rope[64:128, 0:T] = cos
    rope[0:64, T:] = -sin
    rope[64:128, T:] = sin
    m["rope"] = rope
    sel = np.zeros((P, 16), np.float32)
    sel[:, q] = 1.0
    if q > 0:
        sel[:, 4 + q - 1] = 1.0
        sel[:, 8] = 1.0
    m["sel"] = sel
    cf = np.zeros((P, 32), np.float64)
    for i4 in range(q):
        for h in range(8):
            cf[:, i4 * 8 + h] = gam[h] ** (128.0 * 8 * (q - 1 - i4))
    m["coef"] = cf.astype(np.float32)
    return m


_CACHE = {}


def kernel(**inputs):
    sh, gam = _prep_shared(inputs)
    if "nc" not in _CACHE:
        _CACHE["nc"] = build_program()
    nc = _CACHE["nc"]
    in_maps = []
    for r in range(NCORES):
        m = dict(sh)
        m.update(_prep_core(inputs, r, gam))
        in_maps.append(m)
    res = run_bass_kernel_spmd(nc, in_maps, core_ids=list(range(NCORES)))
    B, S = 2, 4096
    out = np.zeros((B, S, D), np.float32)
    for r in range(NCORES):
        b, q = r // 4, r % 4
        yT = np.asarray(res.results[r]["yT"]).reshape(P, KC, T)
        out[b, q * T:(q + 1) * T, :] = yT.transpose(2, 1, 0).reshape(T, D)
    return out
```

```python
import numpy as np
import concourse.bass as bass
import concourse.mybir as mybir
from concourse.bass_utils import run_bass_kernel_spmd

F32 = mybir.dt.float32
BF16 = mybir.dt.bfloat16
AF = mybir.ActivationFunctionType
ALU = mybir.AluOpType

NCORES = 8
P = 128
D = 2048
KC = 16
T = 1024
TH = 512
DFF = 5632
NJ = 44
CHUNKS = [6, 6, 6, 4, 6, 6, 6, 4]
NCH = len(CHUNKS)
MAXNJ = max(CHUNKS)
EPS = 1e-6
WCOLS = 8192
NSLOT = 2
PF = 1


class Eng:
    def __init__(self, name, h, sem):
        self.name = name
        self.h = h
        self.sem = sem
        self.cnt = 0
        self.seen = {}
        self.pending = False


class DSem:
    def __init__(self, sem, name):
        self.sem = sem
        self.cnt = 0
        self.name = name


class Sched:
    def __init__(self, nc, stack):
        self.nc = nc
        self.stack = stack
        self.E = {}
        for name, h in (("pe", nc.tensor), ("act", nc.scalar), ("dve", nc.vector),
                        ("pool", nc.gpsimd), ("sp", nc.sync)):
            sem = stack.enter_context(nc.semaphore("sem_" + name))
            self.E[name] = Eng(name, h, sem)
        self.res_w = {}
        self.res_r = {}
        self.dsems = []

    def new_dsem(self, name):
        d = DSem(self.stack.enter_context(self.nc.semaphore("dsem_" + name)), name)
        self.dsems.append(d)
        return d

    def _wait(self, E, tok):
        S, v = tok
        if S is E and E.name == "pe":
            return
        if isinstance(S, Eng) and S.name == "pe" and S.pending and v > S.cnt:
            raise RuntimeError("wait on unsignaled PE op")
        if E.seen.get(S, 0) < v:
            E.h.wait_ge(S.sem, v)
            E.seen[S] = v

    def _deps(self, E, r, w):
        toks = []
        for k in r:
            t = self.res_w.get(k)
            if t is not None:
                toks.append(t)
        for k in w:
            t = self.res_w.get(k)
            if t is not None:
                toks.append(t)
            for S, v in self.res_r.get(k, {}).items():
                toks.append((S, v))
        for t in toks:
            self._wait(E, t)

    def _record(self, tok, r, w):
        S, v = tok
        for k in r:
            d = self.res_r.setdefault(k, {})
            if d.get(S, 0) < v:
                d[S] = v
        for k in w:
            self.res_w[k] = tok
            self.res_r[k] = {}

    def op(self, en, fn, r=(), w=(), sig=True):
        E = self.E[en]
        self._deps(E, r, w)
        ins = fn(E.h)
        if sig:
            E.cnt += 1
            ins.then_inc(E.sem, 1)
            E.pending = False
            tok = (E, E.cnt)
        else:
            E.pending = True
            tok = (E, E.cnt + 1)
        self._record(tok, r, w)
        return ins

    def dma(self, en, out, in_, r=(), w=(), dsem=None):
        E = self.E[en]
        self._deps(E, r, w)
        self._wait(E, (dsem, dsem.cnt))
        ins = E.h.dma_start(out=out, in_=in_)
        dsem.cnt += 16
        ins.then_inc(dsem.sem, 16)
        self._record((dsem, dsem.cnt), r, w)
        return ins

    def barrier(self):
        assert not self.E["pe"].pending
        for E in self.E.values():
            for F in self.E.values():
                if F is not E and F.cnt > 0:
                    self._wait(E, (F, F.cnt))
            for d in self.dsems:
                if d.cnt > 0:
                    self._wait(E, (d, d.cnt))

    def final_wait(self, en, keys):
        E = self.E[en]
        self._deps(E, keys, ())


class WStream:
    def __init__(self, sc, slots, nbase, seq):
        self.sc = sc
        self.slots = slots
        self.nbase = nbase
        self.seq = seq
        self.issued = 0
        self.pf = 1
        self.dsems = [sc.new_dsem("w%d" % i) for i in range(len(slots))]
        self.slot_tile = [-1] * len(slots)
        self.slot_of = {}

    def _try_issue(self, k, cur):
        ap, allow_extra = self.seq[k]
        cand = list(range(len(self.slots) if allow_extra else self.nbase))
        free = [c for c in cand if self.slot_tile[c] < cur]
        if not free:
            return False
        s = min(free, key=lambda c: self.slot_tile[c])
        ncols = ap.shape[-1]
        self.sc.dma("pool", self.slots[s][:, 0:ncols], ap, w=[("wslot", s)], dsem=self.dsems[s])
        self.slot_tile[s] = k
        self.slot_of[k] = s
        return True

    def get(self, i, pf=None):
        pf = self.pf if pf is None else pf
        while self.issued < min(i + pf + 1, len(self.seq)):
            if not self._try_issue(self.issued, i):
                assert self.issued > i, "no slot for the tile being consumed"
                break
            self.issued += 1
        s = self.slot_of[i]
        return self.slots[s], ("wslot", s)


class Rot:
    def __init__(self, items):
        self.items = list(items)
        self.i = 0

    def next(self):
        v = self.items[self.i % len(self.items)]
        self.i += 1
        return v


def build_program(dbg_names=(), stages=("ffn1", "mixer", "ffn2")):
    from contextlib import ExitStack
    nc = bass.Bass("TRN2", target_bir_lowering=False)
    stack = ExitStack()
    sc = Sched(nc, stack)

    def din(name, shape, dt=F32):
        return nc.dram_tensor(name, list(shape), dt, kind="ExternalInput").ap()

    xT_d = din("xT", [P, KC * T])
    cT_d = din("cT", [P, KC])
    badaT_d = din("badaT", [P, 144])
    gains_d = din("gains", [P, 64])
    rope_d = din("rope", [P, 2 * T])
    qdec_d = din("qdec", [P, 8 * 128])
    retmask_d = din("retmask", [P, 8 * 128])
    kdec_d = din("kdec", [P, 8 + 64])
    swamask_d = din("swamask", [P, 256])
    sel_d = din("sel", [P, 16])
    coef_d = din("coef", [P, 32])
    sinks_d = din("sinksl", [P, 16])
    ident_d = din("ident", [P, P])
    wada_d = din("wada", [36, P, WCOLS])
    wgu_d = [din("wgu1", [22, P, WCOLS]), din("wgu2", [22, P, WCOLS])]
    wd_d = [din("wd1", [NCH * 4, P, MAXNJ * 512]), din("wd2", [NCH * 4, P, MAXNJ * 512])]
    NWIN = 22
    win_d = din("win", [NWIN, P, WCOLS])
    wgb_d = din("wgb", [16, P, WCOLS])
    wbr_d = din("wbr", [4, P, WCOLS])
    yT_d = nc.dram_tensor("yT", [P, KC * T], F32, kind="ExternalOutput").ap()
    xspill = nc.dram_tensor("xspill", [P, KC * T], F32).ap()
    bounce = [nc.dram_tensor("bounce%d" % i, [4 * P, 1536], F32) for i in range(2)]
    gath = [nc.dram_tensor("gath%d" % i, [4 * P, 1536], F32) for i in range(2)]
    dbg_out = {}

    def sb(name, shape, dt=F32):
        return stack.enter_context(nc.sbuf_tensor(name, list(shape), dt))

    xT = sb("xT_sb", [P, KC, T])
    hT = sb("hT", [P, KC, T], BF16)
    wslots = [sb("wslot%d" % i, [P, WCOLS], BF16) for i in range(NSLOT)]
    modT = sb("modT", [P, 144])
    cols = sb("cols", [P, 9 * 16])
    gains = sb("gains_sb", [P, 64])
    rstd = sb("rstd", [P, T])
    tmpf = [sb("tmpf%d" % i, [P, T]) for i in range(2)]
    sqb = [sb("sqb%d" % i, [P, T], BF16) for i in range(2)]
    onesD = sb("onesD", [P, P], BF16)
    ident = sb("ident_sb", [P, P], BF16)
    YBYTES = 57344
    yar = sb("yar", [P, YBYTES // 2], BF16)
    XB = xT[:].rearrange("p a b -> p (a b)").bitcast(BF16)
    YB = yar[:]

    def carve(arena, off, nbytes, dt):
        assert off % 4 == 0 and nbytes % 4 == 0
        v = arena[:, off // 2:(off + nbytes) // 2]
        return v if dt == BF16 else v.bitcast(F32)

    ps = [stack.enter_context(nc.psum_tensor("ps%d" % i, [P, 512], F32)) for i in range(8)]
    PSK = [("ps", i) for i in range(8)]
    dq = sc.new_dsem("io")
    dq2 = sc.new_dsem("io2")

    def Xk(kc):
        return ("x", kc)

    def Hk(kc, t):
        return ("h", kc, t)

    seq = []
    NADA0 = 8
    for g in range(NADA0):
        seq.append((wada_d[g], True))

    def ffn_seq(f):
        gu_t = 0
        plan = []
        order = [("gu", 0)]
        for c in range(1, NCH):
            order += [("gu", c), ("d", c - 1)]
        order.append(("d", NCH - 1))
        for kind, c in order:
            if kind == "gu":
                for _ in range(CHUNKS[c] // 2):
                    plan.append(wgu_d[f][gu_t])
                    gu_t += 1
            else:
                for mg in range(4):
                    plan.append(wd_d[f][c * 4 + mg][:, 0:CHUNKS[c] * 512])
        return plan

    if "ffn1" in stages:
        for idx, tl_ in enumerate(ffn_seq(0)):
            seq.append((tl_, True))
            if idx < 20 - NADA0:
                seq.append((wada_d[NADA0 + idx], True))
    else:
        for g in range(NADA0, 20):
            seq.append((wada_d[g], True))
    if "mixer" not in stages:
        for g in range(20, 36):
            seq.append((wada_d[g], True))
    wi = 0

    def win_next(n=1):
        nonlocal wi
        out = [(win_d[wi + k], False) for k in range(n)]
        wi += n
        return out

    if "mixer" in stages:
        seq += win_next(4 + 4 + 1 + 1)
        for j in range(4):
            seq += win_next(1)
            for q3 in range(4):
                seq.append((wada_d[20 + j * 4 + q3], False))
        for mp in range(8):
            seq.append((wgb_d[mp], False))
        seq += win_next(8)
        for mp in range(8):
            seq.append((wgb_d[8 + mp], False))
        for mg in range(4):
            seq.append((wbr_d[mg], False))
        assert wi == NWIN
    W_FFN2 = len(seq)
    if "ffn2" in stages:
        for idx, tl_ in enumerate(ffn_seq(1)):
            seq.append((tl_, idx >= 3))
    yslot = carve(YB, 32768, 16384, BF16)
    ws = WStream(sc, wslots + [yslot], NSLOT, seq)
    ws.pf = 2
    wpos = [0]

    def wnext(pf=None):
        t, k = ws.get(wpos[0], pf)
        wpos[0] += 1
        return t, k

    def mm(out, lhsT, rhs, start, stop, r, w, sig):
        sc.op("pe", lambda e: e.matmul(out, lhsT=lhsT, rhs=rhs, start=start, stop=stop), r=r, w=w, sig=sig)

    def debug_dump(name, ap, shape, keys):
        if name not in dbg_names:
            return
        d = nc.dram_tensor("dbg_" + name, list(shape), ap.dtype, kind="ExternalOutput").ap()
        ds = sc.new_dsem("dbg_" + name)
        sc.dma("sp", d, ap, r=keys, dsem=ds)
        dbg_out[name] = ds

    sc.dma("sp", xT[:].rearrange("p a b -> p (a b)"), xT_d, w=[Xk(k) for k in range(KC)], dsem=dq)
    small = {}
    yoff = [0]

    def ycarve(nbytes, dt=F32):
        v = carve(YB, yoff[0], nbytes, dt)
        yoff[0] += nbytes
        return v

    for name, d_ap, n in (("cT", cT_d, KC),):
        t = ycarve(n * 4)
        sc.dma("sp", t, d_ap, w=[name], dsem=dq2)
        small[name] = t
    badaT_sb = sb("badaT_sb", [P, 144])
    sc.dma("sp", badaT_sb[:], badaT_d, w=["badaT"], dsem=dq2)
    small["badaT"] = badaT_sb[:]
    sinks_sb = sb("sinks_sb", [P, 16])
    sc.dma("sp", sinks_sb[:], sinks_d, w=["sinks"], dsem=dq2)
    sc.dma("sp", gains[:], gains_d, w=["gains"], dsem=dq2)
    identf = ycarve(P * 4)
    sc.dma("sp", identf, ident_d, w=["identf"], dsem=dq2)
    sc.op("dve", lambda e: e.tensor_copy(out=ident[:], in_=identf), r=["identf"], w=["ident"])
    sc.op("dve", lambda e: e.memset(onesD[:], 1.0 / D), w=["onesD"])
    ones1 = sb("ones1", [1, 1])
    sc.op("dve", lambda e: e.memset(ones1[:], 1.0), w=["ones1"])
    ones64 = sb("ones64", [P, 64], BF16)
    sc.op("dve", lambda e: e.memset(ones64[:], 1.0), w=["ones64"])
    ones256 = sb("ones256", [P, P], BF16)
    sc.op("dve", lambda e: e.memset(ones256[:], 1.0 / 256.0), w=["ones256"])
    cact = sb("cact", [P, KC], BF16)
    sc.op("act", lambda e: e.activation(out=cact[:], in_=small["cT"], func=AF.Silu), r=["cT"], w=["cact"])
    sinkE = sb("sinkE", [P, 16])
    sc.op("act", lambda e: e.activation(out=sinkE[:], in_=sinks_sb[:], func=AF.Exp), r=["sinks"], w=["sinkE"])

    rowseg = [carve(YB, 28672 + i * 2048, 2048, F32)[0:1, :] for i in range(2)]
    MODB = 7
    ada_g = [0]
    ada_nb = [2]

    def derive(i, parts):
        gi, sh, scl, gt, gscale = ((0, 0, 1, 2, 0.5), (16, 3, 4, 5, 1.0), (32, 6, 7, 8, 0.5))[i]
        A = cols[:, (3 * i) * 16:(3 * i + 1) * 16]
        B = cols[:, (3 * i + 1) * 16:(3 * i + 2) * 16]
        G = cols[:, (3 * i + 2) * 16:(3 * i + 3) * 16]
        if "AB" in parts:
            sc.op("dve", lambda e: e.scalar_tensor_tensor(out=A, in0=modT[:, scl * 16:(scl + 1) * 16], scalar=1.0,
                                                           in1=gains[:, gi:gi + 16], op0=ALU.add, op1=ALU.mult),
                  r=["modT", "gains"], w=[("cols", i, "A")])
            sc.op("dve", lambda e: e.tensor_copy(out=B, in_=modT[:, sh * 16:(sh + 1) * 16]), r=["modT"], w=[("cols", i, "B")])
        if "G" in parts:
            sc.op("dve", lambda e: e.tensor_scalar(out=G, in0=modT[:, gt * 16:(gt + 1) * 16], scalar1=gscale, scalar2=None,
                                                    op0=ALU.mult), r=["modT"], w=[("cols", i, "G")])

    def ada_tile(rowbank, colbank):
        g = ada_g[0]
        if g >= 36:
            return
        ada_g[0] += 1
        wt, wk = wnext()
        b = g % ada_nb[0]
        for kc in range(KC):
            mm(ps[rowbank][0:1, :], cact[:, kc:kc + 1], wt[:, kc * 512:(kc + 1) * 512], kc == 0, kc == KC - 1,
               r=[wk, "cact"], w=[PSK[rowbank]], sig=(kc == KC - 1))
        sc.op("act", lambda e: e.copy(out=rowseg[b], in_=ps[rowbank][0:1, :]), r=[PSK[rowbank]], w=[("rowseg", b)])
        for q4 in range(4):
            mm(ps[colbank][:, q4:q4 + 1], rowseg[b][:, q4 * 128:(q4 + 1) * 128], ones1[0:1, 0:1], True, True,
               r=[("rowseg", b), "ones1"], w=[PSK[colbank]], sig=(q4 == 3))
        sc.op("dve", lambda e: e.tensor_tensor(out=modT[:, g * 4:g * 4 + 4], in0=ps[colbank][:, 0:4], in1=small["badaT"][:, g * 4:g * 4 + 4], op=ALU.add),
              r=[PSK[colbank], "badaT"], w=["modT"])
        if g + 1 == 8:
            derive(0, "AB")
        elif g + 1 == 12:
            derive(0, "G")
        elif g + 1 == 20:
            derive(1, "AB")
        elif g + 1 == 24:
            derive(1, "G")
        elif g + 1 == 36:
            derive(2, "ABG")

    NADA_FFN1 = 20
    sqh = [sqb[0][:, 0:TH], sqb[0][:, TH:T], sqb[1][:, 0:TH], sqb[1][:, TH:T]]
    tmh = [tmpf[0][:, 0:TH], tmpf[0][:, TH:T], tmpf[1][:, 0:TH], tmpf[1][:, TH:T]]
    nm_cnt = [0]

    def norm_stats(t):
        for kc in range(KC):
            qi = kc % 4
            sc.op("act", lambda e: e.activation(out=sqh[qi], in_=xT[:, kc, t * TH:(t + 1) * TH], func=AF.Square),
                  r=[Xk(kc)], w=[("sqh", qi)])
            mm(ps[t][:], onesD[:], sqh[qi], kc == 0, kc == KC - 1, r=[("sqh", qi), "onesD"], w=[PSK[t]], sig=True)
        rs = rstd[:, t * TH:(t + 1) * TH]
        sc.op("dve", lambda e: e.tensor_scalar(out=rs, in0=ps[t][:], scalar1=EPS, scalar2=None, op0=ALU.add),
              r=[PSK[t]], w=[("rstd", t)])
        sc.op("act", lambda e: e.activation(out=rs, in_=rs, func=AF.Ln), r=[("rstd", t)], w=[("rstd", t)])
        sc.op("act", lambda e: e.activation(out=rs, in_=rs, func=AF.Exp, scale=-0.5), r=[("rstd", t)], w=[("rstd", t)])

    if "ffn1" in stages:
        norm_stats(0)
        norm_stats(1)
    for g in range(NADA0):
        ada_tile(4 + g % 2, 6 + g % 2)
    if "ffn1" not in stages:
        while ada_g[0] < NADA_FFN1:
            ada_tile(ada_g[0] % 2, 2 + ada_g[0] % 2)
    if "mixer" not in stages:
        while ada_g[0] < 36:
            ada_tile(ada_g[0] % 2, 2 + ada_g[0] % 2)
    debug_dump("modT", modT[:], [P, 144], ["modT"])

    def colA(i, kc):
        return cols[:, (3 * i) * 16 + kc:(3 * i) * 16 + kc + 1]

    def colB(i, kc):
        return cols[:, (3 * i + 1) * 16 + kc:(3 * i + 1) * 16 + kc + 1]

    def colG(i, kc):
        return cols[:, (3 * i + 2) * 16 + kc:(3 * i + 2) * 16 + kc + 1]

    def rms_stats(src, nblk, ones_t, keyf, banks):
        for kc in range(nblk):
            q = sqb[kc % 2]
            sc.op("act", lambda e: e.activation(out=q[:], in_=src(kc), func=AF.Square), r=[keyf(kc)], w=[("sqb", kc % 2)])
            for t in range(2):
                mm(ps[banks[t]][:], ones_t[:], q[:, t * TH:(t + 1) * TH], kc == 0, kc == nblk - 1,
                   r=[("sqb", kc % 2), "onesD", "ones256"], w=[PSK[banks[t]]], sig=True)
        for t in range(2):
            rs = rstd[:, t * TH:(t + 1) * TH]
            sc.op("dve", lambda e: e.tensor_scalar(out=rs, in0=ps[banks[t]][:], scalar1=EPS, scalar2=None, op0=ALU.add),
                  r=[PSK[banks[t]]], w=["rstd"])
            sc.op("act", lambda e: e.activation(out=rs, in_=rs, func=AF.Ln), r=["rstd"], w=["rstd"])
            sc.op("act", lambda e: e.activation(out=rs, in_=rs, func=AF.Exp, scale=-0.5), r=["rstd"], w=["rstd"])

    def norm_apply(i, t):
        rs = rstd[:, t * TH:(t + 1) * TH]
        for kc in range(KC):
            ti = nm_cnt[0] % 4
            nm_cnt[0] += 1
            sc.op("dve", lambda e: e.scalar_tensor_tensor(out=tmh[ti], in0=xT[:, kc, t * TH:(t + 1) * TH], scalar=colA(i, kc), in1=rs,
                                                           op0=ALU.mult, op1=ALU.mult),
                  r=[Xk(kc), ("rstd", t), ("cols", i, "A")], w=[("tmh", ti)])
            sc.op("act", lambda e: e.activation(out=hT[:, kc, t * TH:(t + 1) * TH], in_=tmh[ti], func=AF.Identity, bias=colB(i, kc), scale=1.0),
                  r=[("tmh", ti), ("cols", i, "B")], w=[Hk(kc, t)])

    def norm_mod(i, stats_done=False):
        for t in range(2):
            if not stats_done:
                norm_stats(t)
            norm_apply(i, t)

    def ffn(i, act, sgt, hook=lambda: None):
        gu_rot = Rot([(0, 1), (2, 3)])
        d_rot = Rot([4, 5])
        sg_rot = Rot([0, 1])
        jbase = [sum(CHUNKS[:c]) for c in range(NCH)]
        cur = {"tile": None, "key": None, "idx": -1}

        def GU(c):
            for jl in range(CHUNKS[c]):
                j = jbase[c] + jl
                if j // 2 != cur["idx"]:
                    if cur["idx"] >= 0:
                        hook()
                    cur["tile"], cur["key"] = wnext()
                    cur["idx"] = j // 2
                wt, wk = cur["tile"], cur["key"]
                wj = j % 2
                for t in range(2):
                    bg, bu = gu_rot.next()
                    for which, bnk in ((0, bg), (1, bu)):
                        for kc in range(KC):
                            o = kc * 512 + wj * 256 + which * 128
                            mm(ps[bnk][:], wt[:, o:o + 128], hT[:, kc, t * TH:(t + 1) * TH], kc == 0, kc == KC - 1,
                               r=[wk, Hk(kc, t)], w=[PSK[bnk]], sig=(kc == KC - 1))
                    s = sg_rot.next()
                    sc.op("act", lambda e: e.activation(out=sgt[s], in_=ps[bg][:], func=AF.Silu), r=[PSK[bg]], w=[("sgt", s)])
                    sc.op("dve", lambda e: e.tensor_tensor(out=act[c % 2][:, jl, t * TH:(t + 1) * TH], in0=sgt[s],
                                                            in1=ps[bu][:], op=ALU.mult),
                          r=[("sgt", s), PSK[bu]], w=[("act", c % 2, jl, t)])

        def DOWN(c):
            nj = CHUNKS[c]
            for mg in range(4):
                hook()
                wt, wk = wnext()
                for m in range(4):
                    mmi = mg * 4 + m
                    for t in range(2):
                        bnk = d_rot.next()
                        for jl in range(nj):
                            o = jl * 512 + m * 128
                            mm(ps[bnk][:], wt[:, o:o + 128], act[c % 2][:, jl, t * TH:(t + 1) * TH], jl == 0, jl == nj - 1,
                               r=[wk, ("act", c % 2, jl, t)], w=[PSK[bnk]], sig=(jl == nj - 1))
                        xs = xT[:, mmi, t * TH:(t + 1) * TH]
                        sc.op("dve", lambda e: e.scalar_tensor_tensor(out=xs, in0=ps[bnk][:], scalar=colG(i, mmi), in1=xs,
                                                                       op0=ALU.mult, op1=ALU.add),
                              r=[PSK[bnk], ("cols", i, "G"), Xk(mmi)], w=[Xk(mmi)])

        GU(0)
        for c in range(1, NCH):
            GU(c)
            DOWN(c - 1)
        DOWN(NCH - 1)
        hook()

    def run_ffn(i):
        sc.barrier()
        norm_mod(i, stats_done=(i == 0))
        a0 = carve(YB, 0, 12288, BF16).rearrange("p (j t) -> p j t", j=MAXNJ)
        a1 = carve(YB, 12288, 12288, BF16).rearrange("p (j t) -> p j t", j=MAXNJ)
        s0 = carve(YB, 24576, 2048, F32)
        s1 = carve(YB, 26624, 2048, F32)
        ws.pf = 2
        if i == 0:
            ffn(i, [a0, a1], [s0, s1], hook=lambda: (ada_tile(6, 7) if ada_g[0] < NADA_FFN1 else None))
            assert ada_g[0] == NADA_FFN1, ada_g[0]
        else:
            ffn(i, [a0, a1], [s0, s1])
        sc.barrier()

    if "ffn1" in stages:
        run_ffn(0)
    debug_dump("x1", xT[:].rearrange("p a b -> p (a b)"), [P, KC * T], [Xk(k) for k in range(KC)])

    def mixer():
        ws.pf = 1
        norm_mod(1)
        sc.dma("sp", xspill, xT[:].rearrange("p a b -> p (a b)"), r=[Xk(k) for k in range(KC)], w=["xspill"], dsem=dq)
        XALL = [Xk(k) for k in range(KC)]
        yo = [0]

        def ytab(d_ap, n, name):
            t = carve(YB, yo[0], n * 4, F32)
            yo[0] += n * 4
            sc.dma("sp", t, d_ap, w=[name], dsem=dq2)
            return t

        rope = ytab(rope_d, 2 * T, "rope")
        qdec = ytab(qdec_d, 1024, "qdec")
        retmask = ytab(retmask_d, 1024, "retmask")
        kdec = ytab(kdec_d, 72, "kdec")
        swamask = ytab(swamask_d, 256, "swamask")
        sel = ytab(sel_d, 16, "sel")
        coef = ytab(coef_d, 32, "coef")
        assert yo[0] <= 18432
        cosT = rope[:, 0:T]
        sinT = rope[:, T:2 * T]
        ZOFF = 18432
        zT = carve(YB, ZOFF, 32768, BF16).rearrange("p (k t) -> p k t", k=KC)
        Sbf = carve(YB, 51200, 4096, BF16).rearrange("p (h e) -> p h e", h=8)
        khL = carve(YB, 55296, 1024, BF16).rearrange("p (n d) -> p n d", n=4)
        maskp0 = carve(YB, 56320, 512, F32)
        scb = [carve(YB, 56832 + i * 256, 256, BF16) for i in range(2)]
        svt = carve(YB, ZOFF, 9216, BF16).rearrange("p (n c) -> p n c", n=9)
        payload = carve(YB, ZOFF + 9216, 12288, F32)
        slot = carve(YB, ZOFF + 21504, 6144, F32)
        halo = carve(YB, ZOFF + 27648, 4096, F32)
        NPB = 3
        probs = [carve(YB, ZOFF + 9216 + i * 2048, 2048, BF16).rearrange("p (k q) -> p k q", k=2) for i in range(NPB)]
        ebuf = [carve(YB, ZOFF + 9216 + 6144 + i * 1024, 1024, BF16) for i in range(2)]
        dtmp = [carve(YB, ZOFF + 9216 + 8192 + i * 2048, 2048, F32) for i in range(2)]
        krT = carve(XB, 0, 16384, BF16).rearrange("p (h t) -> p h t", h=8)
        vtm = carve(XB, 16384, 32768, BF16).rearrange("p (n c) -> p n c", n=8)
        skT = carve(XB, 55296, 9216, BF16).rearrange("p (j t) -> p j t", j=4)
        soT = carve(XB, 0, 32768, BF16).rearrange("p (k t) -> p k t", k=KC)
        sqT = carve(XB, 32768, 8192, BF16).rearrange("p (g t) -> p g t", g=4)
        sq0 = carve(XB, 40960, 4096, BF16).rearrange("p (k q) -> p k q", k=16)
        Sst = carve(XB, 45056, 8192, F32).rearrange("p (h e) -> p h e", h=8)
        roG = carve(XB, 0, 32768, BF16).rearrange("p (k t) -> p k t", k=KC)
        ro = carve(XB, 32768, 8192, F32).rearrange("p (e t) -> p e t", e=2)
        krh = carve(XB, 40960, 2048, BF16)
        qr = carve(XB, 43008, 2048, BF16)
        vh = carve(XB, 53248, 4096, BF16).rearrange("p (n e) -> p n e", n=8)
        kh = [carve(XB, 57344 + i * 2048, 2048, BF16).rearrange("p (n d) -> p n d", n=8) for i in range(2)]
        qd = carve(XB, 61440, 2048, BF16)
        rt = [tmpf[0][:, 0:TH], tmpf[0][:, TH:T]]
        rowseg[0] = carve(XB, 53248, 2048, F32)[0:1, :]
        ada_nb[0] = 1
        sgr = [tmpf[1][:, 0:TH], tmpf[1][:, TH:T]]
        krT_sp = nc.dram_tensor("krT_sp", [P, 8192], BF16).ap()
        v_sp = nc.dram_tensor("v_sp", [P, 8 * 2048], BF16).ap()
        K_krT = lambda h: ("krT", h)
        K_v = lambda n: ("v", n)

        def rotary_from(bn, bs_, t, dst, dkeys):
            sc.op("dve", lambda e: e.tensor_tensor(out=rt[0], in0=ps[bn][:], in1=cosT[:, t * TH:(t + 1) * TH], op=ALU.mult),
                  r=[PSK[bn], "rope"], w=[("rt", 0)])
            sc.op("dve", lambda e: e.tensor_tensor(out=rt[1], in0=ps[bs_][:], in1=sinT[:, t * TH:(t + 1) * TH], op=ALU.mult),
                  r=[PSK[bs_], "rope"], w=[("rt", 1)])
            sc.op("pool", lambda e: e.tensor_tensor(out=dst, in0=rt[0], in1=rt[1], op=ALU.add),
                  r=[("rt", 0), ("rt", 1)], w=dkeys)

        def proj_fm(wt, wk, blk, nblk, t, bnk):
            for kc in range(KC):
                o = kc * (nblk * 128) + blk * 128
                mm(ps[bnk][:], wt[:, o:o + 128], hT[:, kc, t * TH:(t + 1) * TH], kc == 0, kc == KC - 1,
                   r=[wk, Hk(kc, t)], w=[PSK[bnk]], sig=(kc == KC - 1))

        prot = Rot([(0, 1), (2, 3)])
        vrot = Rot([4, 5, 6, 7])
        for tl in range(4):
            wt, wk = wnext()
            for hh in range(2):
                h = tl * 2 + hh
                for t in range(2):
                    bn, bs_ = prot.next()
                    proj_fm(wt, wk, hh * 2, 4, t, bn)
                    proj_fm(wt, wk, hh * 2 + 1, 4, t, bs_)
                    rotary_from(bn, bs_, t, krT[:, h, t * TH:(t + 1) * TH], [K_krT(h)] + XALL)
        for tl in range(4):
            wt, wk = wnext()
            for n in range(8):
                bnk = vrot.next()
                for kc in range(KC):
                    mm(ps[bnk][:], hT[:, kc, n * 128:(n + 1) * 128], wt[:, kc * 512:(kc + 1) * 512], kc == 0, kc == KC - 1,
                       r=[wk, Hk(kc, n // 4)], w=[PSK[bnk]], sig=(kc == KC - 1))
                sc.op("act", lambda e: e.copy(out=vtm[:, n, tl * 512:(tl + 1) * 512], in_=ps[bnk][:]), r=[PSK[bnk]], w=[K_v(n)] + XALL)
        wt, wk = wnext()
        for j in range(4):
            for t in range(2):
                bnk = prot.next()[0]
                proj_fm(wt, wk, j, 4, t, bnk)
                sc.op("act", lambda e: e.copy(out=skT[:, j, 128 + t * TH:128 + (t + 1) * TH], in_=ps[bnk][:]), r=[PSK[bnk]], w=[("skT", j)] + XALL)
        wt, wk = wnext()
        for n in range(8):
            bnk = vrot.next()
            for kc in range(KC):
                mm(ps[bnk][:], hT[:, kc, n * 128:(n + 1) * 128], wt[:, kc * 512:(kc + 1) * 512], kc == 0, kc == KC - 1,
                   r=[wk, Hk(kc, n // 4)], w=[PSK[bnk]], sig=(kc == KC - 1))
            sc.op("act", lambda e: e.copy(out=svt[:, n + 1, :], in_=ps[bnk][:]), r=[PSK[bnk]], w=[("svt", n + 1)])

        for h in range(8):
            lb = h // 2
            lcol = (h % 2) * 256
            for half in range(2):
                bnk = vrot.next()
                for n4 in range(4):
                    n = half * 4 + n4
                    mm(ps[bnk][:, n4 * 128:(n4 + 1) * 128], krT[:, h, n * 128:(n + 1) * 128], ident[:], True, True,
                       r=[K_krT(h), "ident"], w=[PSK[bnk]], sig=(n4 == 3))
                kl = kdec[:, 8 + h * 8 + half * 4: 8 + h * 8 + half * 4 + 4]
                sc.op("dve", lambda e: e.tensor_tensor(out=khL, in0=ps[bnk][:].rearrange("p (n d) -> p n d", n=4),
                                                        in1=kl.unsqueeze(2).broadcast_to([P, 4, 128]), op=ALU.mult),
                      r=[PSK[bnk], "kdec"], w=["khL"])
                for n4 in range(4):
                    n = half * 4 + n4
                    mm(ps[lb][:, lcol:lcol + 256], khL[:, n4, :], vtm[:, n, h * 256:(h + 1) * 256], n == 0, n == 7,
                       r=["khL", K_v(n)], w=[PSK[lb]], sig=True)
            if h % 2 == 1:
                sc.op("act", lambda e: e.copy(out=payload[:, lb * 512:(lb + 1) * 512], in_=ps[lb][:]), r=[PSK[lb]], w=["payload"])
        sc.op("act", lambda e: e.copy(out=payload[:, 2048:2560].rearrange("p (j t) -> p j t", j=4), in_=skT[:, :, 1024:1152]),
              r=[("skT", j) for j in range(4)], w=["payload"])
        sc.op("act", lambda e: e.copy(out=payload[:, 2560:3072], in_=svt[:, 8, :]), r=[("svt", 8)], w=["payload"])
        for i4 in range(4):
            for hf in range(2):
                sc.op("dve", lambda e: e.tensor_scalar(out=slot, in0=payload[:, hf * 1536:(hf + 1) * 1536], scalar1=sel[:, i4:i4 + 1],
                                                        scalar2=None, op0=ALU.mult), r=["payload", "sel"], w=["slot"])
                sc.dma("sp", bounce[hf].ap()[i4 * P:(i4 + 1) * P, :], slot, r=["slot"], w=["bounce"], dsem=dq)
        sc.dma("sp", krT_sp, carve(XB, 0, 16384, BF16), r=[K_krT(h) for h in range(8)], w=["krT_sp"], dsem=dq2)
        sc.dma("sp", v_sp, carve(XB, 16384, 32768, BF16), r=[K_v(n) for n in range(8)], w=["v_sp"], dsem=dq2)
        ccsem = stack.enter_context(nc.semaphore("ccsem"))
        ccd = DSem(ccsem, "cc")
        Epool = sc.E["pool"]
        sc._deps(Epool, ["bounce"], ["gath"])
        for hf in range(2):
            nc.gpsimd.collective_compute("AllReduce", ALU.add, replica_groups=[[0, 1, 2, 3], [4, 5, 6, 7]],
                                         ins=[bounce[hf].ap().opt()], outs=[gath[hf].ap().opt()]).then_inc(ccsem)
        ccd.cnt = 2
        sc._record((ccd, 2), ["bounce"], ["gath"])
        sc.barrier()

        lrot = Rot([(0, 1), (2, 3)])
        nrot = Rot([(4, 5), (6, 7)])
        g4 = lambda ap: ap.rearrange("p (g q) -> p g q", g=4)

        prrot = Rot(list(range(NPB)))
        dcnt = [0]

        def swa_steps(blocks, hook_every=0):
            SKEW = 2
            steps = []
            for blk in blocks:
                bnum, bden = nrot.next()
                for s in range(2):
                    steps.append((blk, s, bnum, bden))
            st = {}

            def front(i):
                (j, n, qsrc, qkeys, mprev, mkeys), s, bnum, bden = steps[i]
                r0, r1 = s * 64, (s + 1) * 64
                pb = prrot.next()
                blg = lrot.next()
                st[i] = pb
                for kb in range(2):
                    mm(g4(ps[blg[kb]][:]), skT[r0:r1, j, (n + kb) * 128:(n + kb + 1) * 128], qsrc[r0:r1],
                       True, True, r=[("skT", j)] + qkeys, w=[PSK[blg[kb]]], sig=True)
                    sc.op("act", lambda e: e.activation(out=ebuf[kb], in_=ps[blg[kb]][:], func=AF.Exp, scale=0.125),
                          r=[PSK[blg[kb]]], w=[("ebuf", kb)])
                    mk = mprev if kb == 0 else swamask[:, 128:256]
                    sc.op("dve" if kb == 0 else "pool",
                          lambda e: e.tensor_tensor(out=g4(probs[pb][:, kb, :]), in0=g4(ebuf[kb]),
                                                    in1=mk.unsqueeze(1).broadcast_to([P, 4, 128]), op=ALU.mult),
                          r=[("ebuf", kb), "swamask"] + mkeys, w=[("probs", pb, kb)])

            def back(i):
                (j, n, qsrc, qkeys, mprev, mkeys), s, bnum, bden = steps[i]
                r0, r1 = s * 64, (s + 1) * 64
                pb = st.pop(i)
                hk = 2 * j + s
                for kb in range(2):
                    mm(ps[bnum][r0:r1, :], svt[:, n + kb, hk * 64:(hk + 1) * 64], probs[pb][:, kb, :], kb == 0, kb == 1,
                       r=[("svt", n + kb), ("probs", pb, kb)], w=[PSK[bnum]], sig=(kb == 1))
                for kb in range(2):
                    mm(ps[bden][r0:r1, :], ones64[:], probs[pb][:, kb, :], kb == 0, kb == 1,
                       r=["ones64", ("probs", pb, kb)], w=[PSK[bden]], sig=(kb == 1))
                if s == 1:
                    di = dcnt[0] % 2
                    dcnt[0] += 1
                    dt_ = dtmp[di]
                    for g in range(4):
                        sc.op("act", lambda e: e.activation(out=dt_[:, g * 128:(g + 1) * 128], in_=ps[bden][:, g * 128:(g + 1) * 128], func=AF.Ln,
                                                            bias=sinkE[:, j * 4 + g:j * 4 + g + 1], scale=1.0),
                              r=[PSK[bden], "sinkE"], w=[("dtmp", di)])
                    sc.op("act", lambda e: e.activation(out=dt_, in_=dt_, func=AF.Exp, scale=-1.0), r=[("dtmp", di)], w=[("dtmp", di)])
                    sc.op("dve", lambda e: e.tensor_tensor(out=soT[:, j * 4:(j + 1) * 4, n * 128:(n + 1) * 128],
                                                            in0=g4(ps[bnum][:]), in1=g4(dt_), op=ALU.mult),
                          r=[PSK[bnum], ("dtmp", di)], w=[("so", j * 4 + g) for g in range(4)])

            N = len(steps)
            nh = 0
            for i in range(N + SKEW):
                if i < N:
                    front(i)
                if i - SKEW >= 0:
                    back(i - SKEW)
                if hook_every and i >= 3 and i % 2 == 1 and nh < 4:
                    nh += 1
                    bk = vrot.next()
                    ada_tile(bk, bk)
            while hook_every and nh < 4:
                nh += 1
                bk = vrot.next()
                ada_tile(bk, bk)

        def receive():
            first = {}

            def acc(eng, dst, src, scal, key, rkeys):
                if key not in first:
                    first[key] = 1
                    sc.op(eng, lambda e: e.tensor_scalar(out=dst, in0=src, scalar1=scal, scalar2=None, op0=ALU.mult),
                          r=["slot"] + rkeys, w=[key])
                else:
                    sc.op(eng, lambda e: e.scalar_tensor_tensor(out=dst, in0=src, scalar=scal, in1=dst, op0=ALU.mult, op1=ALU.add),
                          r=["slot", key] + rkeys, w=[key])

            for i4 in range(4):
                for hf in range(2):
                    sc.dma("sp", slot, gath[hf].ap()[i4 * P:(i4 + 1) * P, :], r=["gath"], w=["slot"], dsem=dq)
                    heads = range(6) if hf == 0 else range(6, 8)
                    for h in heads:
                        o = h * 256 - hf * 1536
                        acc("dve", Sst[:, h, :], slot[:, o:o + 256], coef[:, i4 * 8 + h:i4 * 8 + h + 1], ("Sst", h), ["coef"])
                    if hf == 1:
                        acc("dve", halo, slot[:, 512:1536], sel[:, 4 + i4:5 + i4], "halo", ["sel"])
            sc.op("act", lambda e: e.copy(out=skT[:, :, 0:128], in_=halo[:, 0:512].rearrange("p (j t) -> p j t", j=4)),
                  r=["halo"], w=[("skT", j) for j in range(4)])
            sc.op("act", lambda e: e.copy(out=svt[:, 0, :], in_=halo[:, 512:1024]), r=["halo"], w=[("svt", 0)])
            for h in range(8):
                sc.op("act", lambda e: e.copy(out=Sbf[:, h, :], in_=Sst[:, h, :]), r=[("Sst", h)], w=[("Sbf", h)])
            sc.op("dve", lambda e: e.tensor_scalar(out=maskp0, in0=swamask[:, 0:128], scalar1=sel[:, 8:9], scalar2=None, op0=ALU.mult),
                  r=["swamask", "sel"], w=["maskp0"])

        for j in range(4):
            if j == 3:
                receive()
            wt, wk = wnext()
            for g in range(4):
                for t in range(2):
                    bnk = vrot.next()
                    proj_fm(wt, wk, g, 4, t, bnk)
                    sc.op("act", lambda e: e.copy(out=sqT[:, g, t * TH:(t + 1) * TH], in_=ps[bnk][:]), r=[PSK[bnk]], w=[("sqT", g)])
            sc.op("pool", lambda e: e.tensor_copy(out=sq0[:, j * 4:(j + 1) * 4, :], in_=sqT[:, :, 0:128]),
                  r=[("sqT", g) for g in range(4)], w=[("sq0", j)])
            swa_steps([(j, n, sqT[:, :, n * 128:(n + 1) * 128], [("sqT", g) for g in range(4)], swamask[:, 0:128], []) for n in range(1, 8)],
                      hook_every=3)

        swa_steps([(j, 0, sq0[:, j * 4:(j + 1) * 4, :], [("sq0", j)], maskp0, ["maskp0"]) for j in range(4)])
        debug_dump("soT", carve(XB, 0, 32768, BF16), [P, KC * T], [("so", k) for k in range(KC)])
        sc.barrier()

        def gated_branch(src, skeys, accumulate):
            grot = Rot([(0, 1), (2, 3), (4, 5), (6, 7)])
            for mp in range(8):
                wt, wk = wnext()
                for m2 in range(2):
                    mmi = mp * 2 + m2
                    for t in range(2):
                        bg, by = grot.next()
                        proj_fm(wt, wk, m2, 4, t, bg)
                        for kc in range(KC):
                            o = kc * 512 + (2 + m2) * 128
                            mm(ps[by][:], wt[:, o:o + 128], src[:, kc, t * TH:(t + 1) * TH], kc == 0, kc == KC - 1,
                               r=[wk, skeys(kc)], w=[PSK[by]], sig=(kc == KC - 1))
                        s = t
                        sc.op("act", lambda e: e.activation(out=rt[s], in_=ps[bg][:], func=AF.Sigmoid), r=[PSK[bg]], w=[("rt", s)])
                        zs = zT[:, mmi, t * TH:(t + 1) * TH]
                        if not accumulate:
                            sc.op("dve", lambda e: e.tensor_tensor(out=zs, in0=rt[s], in1=ps[by][:], op=ALU.mult),
                                  r=[("rt", s), PSK[by]], w=[("z", mmi)])
                        else:
                            sc.op("dve", lambda e: e.tensor_tensor(out=rt[s], in0=rt[s], in1=ps[by][:], op=ALU.mult),
                                  r=[("rt", s), PSK[by]], w=[("rt", s)])
                            sc.op("pool", lambda e: e.tensor_tensor(out=zs, in0=rt[s], in1=zs, op=ALU.add),
                                  r=[("rt", s), ("z", mmi)], w=[("z", mmi)])

        gated_branch(soT, lambda kc: ("so", kc), False)
        sc.barrier()

        gam = [1.0 - 2.0 ** (-5.0 - h) for h in range(8)]
        for h in range(8):
            wq, wqk = wnext()
            wg, wgk = wq, wqk
            if True:
                sc.dma("sp", krh, krT_sp[:, h * T:(h + 1) * T], r=["krT_sp"], w=["krh"], dsem=dq)
                sc.dma("sp", vh, v_sp.rearrange("p (n c) -> p n c", n=8)[:, :, h * 256:(h + 1) * 256], r=["v_sp"], w=["vh"], dsem=dq2)
                for t in range(2):
                    bn, bs_ = prot.next()
                    proj_fm(wq, wqk, 0, 4, t, bn)
                    proj_fm(wq, wqk, 1, 4, t, bs_)
                    rotary_from(bn, bs_, t, qr[:, t * TH:(t + 1) * TH], ["qr"])
                    sc.op("pool", lambda e: e.tensor_tensor(out=qd[:, t * TH:(t + 1) * TH].rearrange("p (n i) -> p n i", n=4),
                                                             in0=qr[:, t * TH:(t + 1) * TH].rearrange("p (n i) -> p n i", n=4),
                                                             in1=qdec[:, h * 128:(h + 1) * 128].unsqueeze(1).broadcast_to([P, 4, 128]),
                                                             op=ALU.mult), r=["qr", "qdec"], w=["qd"])
                khh = kh[h % 2]
                for half in range(2):
                    bnk = vrot.next()
                    for n4 in range(4):
                        n = half * 4 + n4
                        mm(ps[bnk][:, n4 * 128:(n4 + 1) * 128], krh[:, n * 128:(n + 1) * 128], ident[:], True, True,
                           r=["krh", "ident"], w=[PSK[bnk]], sig=(n4 == 3))
                    sc.op("act", lambda e: e.activation(out=khh[:, half * 4:(half + 1) * 4, :],
                                                        in_=ps[bnk][:].rearrange("p (n d) -> p n d", n=4), func=AF.Identity,
                                                        scale=kdec[:, h:h + 1]), r=[PSK[bnk], "kdec"], w=[("kh", h % 2)])
                g128 = gam[h] ** 128

                def emit_sc(n):
                    b_sc = vrot.next()
                    mm(ps[b_sc][:, 0:128], krh[:, n * 128:(n + 1) * 128], qr[:, n * 128:(n + 1) * 128], True, True,
                       r=["krh", "qr"], w=[PSK[b_sc]], sig=True)
                    sb_ = scb[n % 2]
                    sc.op("dve", lambda e: e.tensor_tensor(out=sb_, in0=ps[b_sc][:, 0:128], in1=retmask[:, h * 128:(h + 1) * 128], op=ALU.mult),
                          r=[PSK[b_sc], "retmask"], w=[("scb", n % 2)])

                emit_sc(0)
                for n in range(8):
                    b_kv = None
                    if n < 7:
                        b_kv = vrot.next()
                        mm(ps[b_kv][:, 0:256], khh[:, n, :], vh[:, n, :], True, True,
                           r=[("kh", h % 2), "vh"], w=[PSK[b_kv]], sig=True)
                    sb_ = scb[n % 2]
                    b_o = vrot.next()
                    for eh in range(2):
                        mm(ps[b_o][:, eh * 128:(eh + 1) * 128], vh[:, n, eh * 128:(eh + 1) * 128], sb_, True, False,
                           r=["vh", ("scb", n % 2)], w=[PSK[b_o]], sig=False)
                        mm(ps[b_o][:, eh * 128:(eh + 1) * 128], Sbf[:, h, eh * 128:(eh + 1) * 128], qd[:, n * 128:(n + 1) * 128], False, True,
                           r=[("Sbf", h), "qd"], w=[PSK[b_o]], sig=True)
                    if n < 7:
                        emit_sc(n + 1)
                        sc.op("dve", lambda e: e.scalar_tensor_tensor(out=Sst[:, h, :], in0=Sst[:, h, :], scalar=float(g128), in1=ps[b_kv][:, 0:256],
                                                                       op0=ALU.mult, op1=ALU.add), r=[("Sst", h), PSK[b_kv]], w=[("Sst", h)])
                        sc.op("act", lambda e: e.copy(out=Sbf[:, h, :], in_=Sst[:, h, :]), r=[("Sst", h)], w=[("Sbf", h)])
                    sc.op("act", lambda e: e.copy(out=ro[:, :, n * 128:(n + 1) * 128], in_=ps[b_o][:, 0:256].rearrange("p (e q) -> p e q", e=2)),
                          r=[PSK[b_o]], w=["ro"])
                rms_stats(lambda eh: ro[:, eh, :], 2, ones256, lambda eh: "ro", (0, 1))
                for eh in range(2):
                    for t in range(2):
                        bnk = vrot.next()
                        proj_fm(wg, wgk, 2 + eh, 4, t, bnk)
                        s = t
                        sc.op("act", lambda e: e.activation(out=sgr[s], in_=ps[bnk][:], func=AF.Silu), r=[PSK[bnk]], w=[("sgr", s)])
                        sc.op("dve", lambda e: e.tensor_tensor(out=sgr[s], in0=sgr[s], in1=rstd[:, t * TH:(t + 1) * TH], op=ALU.mult),
                              r=[("sgr", s), "rstd"], w=[("sgr", s)])
                        sc.op("pool", lambda e: e.tensor_tensor(out=roG[:, h * 2 + eh, t * TH:(t + 1) * TH], in0=sgr[s],
                                                                 in1=ro[:, eh, t * TH:(t + 1) * TH], op=ALU.mult),
                              r=[("sgr", s), "ro"], w=[("roG", h * 2 + eh)])
        debug_dump("roG", carve(XB, 0, 32768, BF16), [P, KC * T], [("roG", k) for k in range(KC)])
        debug_dump("krh", krh, [P, T], ["krh"])
        debug_dump("qr", qr, [P, T], ["qr"])
        debug_dump("qd", qd, [P, T], ["qd"])
        debug_dump("vh", carve(XB, 53248, 4096, BF16), [P, 2048], ["vh"])
        debug_dump("ro", carve(XB, 32768, 8192, F32), [P, 2048], ["ro"])
        debug_dump("Sst", carve(XB, 45056, 8192, F32), [P, 2048], [("Sst", h) for h in range(8)])
        debug_dump("kh", carve(XB, 57344 + 2048, 2048, BF16), [P, 1024], [("kh", 1)])
        sc.barrier()
        gated_branch(roG, lambda kc: ("roG", kc), True)
        sc.barrier()
        sc.dma("sp", xT[:].rearrange("p a b -> p (a b)"), xspill, r=["xspill"], w=[Xk(k) for k in range(KC)], dsem=dq)
        orot = Rot([0, 1, 2, 3, 4, 5, 6, 7])
        for mg in range(4):
            wt, wk = wnext()
            for m in range(4):
                mmi = mg * 4 + m
                for t in range(2):
                    bnk = orot.next()
                    for kc in range(KC):
                        o = kc * 512 + m * 128
                        mm(ps[bnk][:], wt[:, o:o + 128], zT[:, kc, t * TH:(t + 1) * TH], kc == 0, kc == KC - 1,
                           r=[wk, ("z", kc)], w=[PSK[bnk]], sig=(kc == KC - 1))
                    xs = xT[:, mmi, t * TH:(t + 1) * TH]
                    sc.op("dve", lambda e: e.scalar_tensor_tensor(out=xs, in0=ps[bnk][:], scalar=colG(1, mmi), in1=xs,
                                                                   op0=ALU.mult, op1=ALU.add), r=[PSK[bnk], ("cols", 1, "G"), Xk(mmi)], w=[Xk(mmi)])
        sc.barrier()

    if "mixer" in stages:
        mixer()
    debug_dump("x2", xT[:].rearrange("p a b -> p (a b)"), [P, KC * T], [Xk(k) for k in range(KC)])
    assert wpos[0] == W_FFN2, (wpos[0], W_FFN2)
    if "ffn2" in stages:
        run_ffn(2)

    sc.barrier()
    dout = [sc.new_dsem("out%d" % i) for i in range(4)]
    for t in range(2):
        norm_stats(t)
        rs = rstd[:, t * TH:(t + 1) * TH]
        for kc in range(KC):
            ti = nm_cnt[0] % 4
            nm_cnt[0] += 1
            sc.op("dve", lambda e: e.scalar_tensor_tensor(out=tmh[ti], in0=xT[:, kc, t * TH:(t + 1) * TH], scalar=gains[:, 48 + kc:49 + kc], in1=rs,
                                                           op0=ALU.mult, op1=ALU.mult), r=[Xk(kc), ("rstd", t), "gains"], w=[("tmh", ti)])
            sc.dma("sp", yT_d[:, kc * T + t * TH:kc * T + (t + 1) * TH], tmh[ti], r=[("tmh", ti)], w=[("yT", kc, t)], dsem=dout[ti])
    sc.final_wait("sp", [("yT", kc, t) for kc in range(KC) for t in range(2)])
    E = sc.E["sp"]
    for d in dout:
        sc._wait(E, (d, d.cnt))
    for name, d in dbg_out.items():
        sc._wait(E, (d, d.cnt))
    assert wpos[0] == len(seq), (wpos[0], len(seq))
    stack.close()
    return nc


def _tile_lhs(w, blocks):
    nb = len(blocks)
    cols = np.concatenate(blocks)
    sub = w[:, cols].reshape(KC, P, nb * 128)
    return np.ascontiguousarray(sub.transpose(1, 0, 2)).reshape(P, KC * nb * 128)


def _prep_shared(inp):
    f = lambda a: np.asarray(a, dtype=np.float32)
    sh = {}
    w_ada = f(inp["w_ada"])[0]
    sh["wada"] = np.ascontiguousarray(w_ada.reshape(KC, P, 36, 512).transpose(2, 1, 0, 3)).reshape(36, P, WCOLS)
    for i, (gu, dn) in enumerate((("w_ffn1_gu", "w_ffn1_down"), ("w_ffn2_gu", "w_ffn2_down"))):
        wgu = f(inp[gu])[0]
        tiles = []
        for tt in range(22):
            blocks = []
            for j in (2 * tt, 2 * tt + 1):
                blocks.append(np.arange(j * 128, (j + 1) * 128))
                blocks.append(DFF + np.arange(j * 128, (j + 1) * 128))
            tiles.append(_tile_lhs(wgu, blocks))
        sh["wgu%d" % (i + 1)] = np.stack(tiles)
        wd = f(inp[dn])[0]
        dt = np.zeros((NCH * 4, P, MAXNJ * 512), np.float32)
        jb = 0
        for c, nj in enumerate(CHUNKS):
            rows = wd[jb * 128:(jb + nj) * 128].reshape(nj, P, 4, 512)
            for mg in range(4):
                dt[c * 4 + mg, :, :nj * 512] = rows[:, :, mg, :].transpose(1, 0, 2).reshape(P, nj * 512)
            jb += nj
        sh["wd%d" % (i + 1)] = dt
    w_in = f(inp["w_in"])[0]
    o_rq, o_rk, o_rv, o_rg, o_sq, o_sk, o_sv, o_gr, o_gs = 0, 1024, 2048, 4096, 6144, 8192, 8704, 9216, 11264
    sw = np.concatenate([np.arange(64, 128), np.arange(0, 64)])
    win = []

    def qk_tiles(off):
        out = []
        for tl in range(4):
            blocks = []
            for hh in range(2):
                h = tl * 2 + hh
                base = off + h * 128
                blocks.append(base + np.arange(128))
                blocks.append(base + sw)
            out.append(_tile_lhs(w_in, blocks))
        return out

    def rhs_tiles(off, ncols):
        out = []
        for tl in range(ncols // 512):
            sub = w_in[:, off + tl * 512: off + (tl + 1) * 512].reshape(KC, P, 512)
            out.append(np.ascontiguousarray(sub.transpose(1, 0, 2)).reshape(P, WCOLS))
        return out

    win += qk_tiles(o_rk)
    win += rhs_tiles(o_rv, 2048)
    win.append(_tile_lhs(w_in, [o_sk + j * 128 + np.arange(128) for j in range(4)]))
    win += rhs_tiles(o_sv, 512)
    for j in range(4):
        blocks = []
        for g in range(4):
            ha, hb = 8 * j + g, 8 * j + 4 + g
            blocks.append(np.concatenate([o_sq + ha * 64 + np.arange(64), o_sq + hb * 64 + np.arange(64)]))
        win.append(_tile_lhs(w_in, blocks))
    for h in range(8):
        win.append(_tile_lhs(w_in, [o_rq + h * 128 + np.arange(128), o_rq + h * 128 + sw,
                                    o_rg + (2 * h) * 128 + np.arange(128), o_rg + (2 * h + 1) * 128 + np.arange(128)]))
    sh["win"] = np.stack(win)
    assert sh["win"].shape[0] == 22
    w_swa = f(inp["w_swa_branch"])[0]
    perm = []
    for j in range(4):
        for g in range(4):
            for s in range(2):
                perm.append((8 * j + 4 * s + g) * 64 + np.arange(64))
    perm = np.concatenate(perm)
    w_swa_p = w_swa[perm]
    w_ret = f(inp["w_ret_branch"])[0]
    w_out = f(inp["w_out"])[0]
    wgb = []
    for goff, wmat in ((o_gs, w_swa_p), (o_gr, w_ret)):
        for mp in range(8):
            A = _tile_lhs(w_in, [goff + (2 * mp + b) * 128 + np.arange(128) for b in range(2)]).reshape(P, KC, 256)
            B = _tile_lhs(wmat, [(2 * mp + b) * 128 + np.arange(128) for b in range(2)]).reshape(P, KC, 256)
            wgb.append(np.concatenate([A, B], axis=2).reshape(P, WCOLS))
    sh["wgb"] = np.stack(wgb)
    sh["wbr"] = np.stack([_tile_lhs(w_out, [(mg * 4 + b) * 128 + np.arange(128) for b in range(4)]) for mg in range(4)])
    gam = np.array([1.0 - 2.0 ** (-5.0 - h) for h in range(8)], np.float64)
    i = np.arange(128)
    qdec = np.stack([gam[h] ** (i + 1.0) for h in range(8)]).reshape(1, 1024)
    sh["qdec"] = np.broadcast_to(qdec, (P, 1024)).astype(np.float32).copy()
    k_ = i[:, None]
    q_ = i[None, :]
    rm = np.concatenate([np.where(q_ >= k_, gam[h] ** np.maximum(q_ - k_, 0) * (128.0 ** -0.5), 0.0) for h in range(8)], axis=1)
    sh["retmask"] = rm.astype(np.float32)
    kd = np.zeros((P, 72), np.float64)
    for h in range(8):
        kd[:, h] = (128.0 ** -0.5) * gam[h] ** (127.0 - i)
        for n in range(8):
            kd[:, 8 + h * 8 + n] = (128.0 ** -0.5) * gam[h] ** (1023.0 - 128.0 * n - i)
    sh["kdec"] = kd.astype(np.float32)
    sm = np.zeros((P, 256), np.float32)
    sm[:, 0:128] = (k_ > q_)
    sm[:, 128:256] = (k_ <= q_)
    sh["swamask"] = sm
    sh["ident"] = np.eye(P, dtype=np.float32)
    gl = np.zeros((P, 64), np.float32)
    for c, nm in enumerate(("norm_ffn1", "norm_mix", "norm_ffn2")):
        gl[:, c * 16:(c + 1) * 16] = f(inp[nm])[0].reshape(KC, P).T
    gl[:, 48:64] = f(inp["norm_final"]).reshape(KC, P).T
    sh["gains"] = gl
    sk = f(inp["sinks"])[0]
    sl = np.zeros((P, 16), np.float32)
    for j in range(4):
        for g in range(4):
            sl[0:64, j * 4 + g] = sk[8 * j + g]
            sl[64:128, j * 4 + g] = sk[8 * j + 4 + g]
    sh["sinksl"] = sl
    return sh, gam


def _prep_core(inp, r, gam):
    f = lambda a: np.asarray(a, dtype=np.float32)
    b, q = r // 4, r % 4
    m = {}
    xs = f(inp["x"])[b, q * T:(q + 1) * T]
    m["xT"] = np.ascontiguousarray(xs.reshape(T, KC, P).transpose(2, 1, 0)).reshape(P, KC * T)
    m["cT"] = np.ascontiguousarray(f(inp["c"])[b].reshape(KC, P).T)
    m["badaT"] = np.ascontiguousarray(f(inp["b_ada"])[0].reshape(144, P).T)
    half = 64
    inv_freq = (1.0 / (np.float32(10000.0) ** (np.arange(half, dtype=np.float32) / np.float32(half)))).astype(np.float32)
    pos = np.arange(q * T, (q + 1) * T, dtype=np.float32)
    ang = (pos[None, :] * inv_freq[:, None]).astype(np.float32)
    cos = np.cos(ang).astype(np.float32)
    sin = np.sin(ang).astype(np.float32)
    rope = np.zeros((P, 2 * T), np.float32)
    rope[0:64, 0:T] = cos
    rope[64:128, 0:T] = cos
    rope[0:64, T:] = -sin
    rope[64:128, T:] = sin
    m["rope"] = rope
    sel = np.zeros((P, 16), np.float32)
    sel[:, q] = 1.0
    if q > 0:
        sel[:, 4 + q - 1] = 1.0
        sel[:, 8] = 1.0
    m["sel"] = sel
    cf = np.zeros((P, 32), np.float64)
    for i4 in range(q):
        for h in range(8):
            cf[:, i4 * 8 + h] = gam[h] ** (128.0 * 8 * (q - 1 - i4))
    m["coef"] = cf.astype(np.float32)
    return m


_CACHE = {}


def kernel(**inputs):
    sh, gam = _prep_shared(inputs)
    if "nc" not in _CACHE:
        _CACHE["nc"] = build_program()
    nc = _CACHE["nc"]
    in_maps = []
    for r in range(NCORES):
        m = dict(sh)
        m.update(_prep_core(inputs, r, gam))
        in_maps.append(m)
    res = run_bass_kernel_spmd(nc, in_maps, core_ids=list(range(NCORES)))
    B, S = 2, 4096
    out = np.zeros((B, S, D), np.float32)
    for r in range(NCORES):
        b, q = r // 4, r % 4
        yT = np.asarray(res.results[r]["yT"]).reshape(P, KC, T)
        out[b, q * T:(q + 1) * T, :] = yT.transpose(2, 1, 0).reshape(T, D)
    return out
```
